# Optimizing a Trainium2 kernel written in Bass

```python
import math
import jax, jax.numpy as jnp
from jax import lax
import numpy as np

D_MODEL = 1024
BATCH = 4
SEQ = 8192
DEPTH = 2

N_MIXERS = 2
N_RET = (DEPTH + 1) // 2
N_LRU = DEPTH // 2

RET_HEADS = D_MODEL // 256
RET_DK = 256
RET_DV = 2 * RET_DK
RET_QK_W = RET_HEADS * RET_DK
RET_V_W = RET_HEADS * RET_DV
RET_CHUNK = 128
ROPE_BASE = 10000.0

LRU_WIDTH = 1536
LRU_BLOCK = 256
LRU_BLOCKS = LRU_WIDTH // LRU_BLOCK
LRU_CONV = 4
LRU_C = 8.0

D_FF = ((8 * D_MODEL // 3 + 255) // 256) * 256

NORM_EPS = 1e-6

kernel_name = "hybrid_retention_rglru_sandwich"


def rms_norm(x, g):
    xf = x.astype(jnp.float32)
    y = xf * lax.rsqrt(jnp.mean(xf * xf, axis=-1, keepdims=True) + NORM_EPS)
    return (y * g.astype(jnp.float32)).astype(x.dtype)


def rope(x, pos):
    half = x.shape[-1] // 2
    inv = 1.0 / (ROPE_BASE ** (jnp.arange(half, dtype=jnp.float32) / half))
    ang = pos.astype(jnp.float32)[:, None] * inv[None, :]
    cos = jnp.cos(ang)[None, :, None, :]
    sin = jnp.sin(ang)[None, :, None, :]
    x1, x2 = x[..., :half], x[..., half:]
    return jnp.concatenate([x1 * cos - x2 * sin, x1 * sin + x2 * cos], axis=-1)


def retention_mixer(x, w_in, w_out):
    B, S, _ = x.shape
    H, DK, DV, C = RET_HEADS, RET_DK, RET_DV, RET_CHUNK
    N = S // C
    proj = x @ w_in
    q, k, v, g = jnp.split(proj, [RET_QK_W, 2 * RET_QK_W, 2 * RET_QK_W + RET_V_W], axis=-1)
    pos = jnp.arange(S)
    q = rope(q.reshape(B, S, H, DK).astype(jnp.float32), pos)
    k = rope(k.reshape(B, S, H, DK).astype(jnp.float32), pos) * (DK ** -0.5)
    v = v.reshape(B, S, H, DV).astype(jnp.float32)

    log_gamma = jnp.log1p(-jnp.exp2(-5.0 - jnp.arange(H, dtype=jnp.float32)))
    idx = jnp.arange(C, dtype=jnp.float32)
    rel = idx[:, None] - idx[None, :]
    causal = rel >= 0
    decay = jnp.where(causal[None],
                      jnp.exp(jnp.where(causal, rel, 0.0)[None] * log_gamma[:, None, None]),
                      0.0)
    q_decay = jnp.exp((idx[:, None] + 1.0) * log_gamma[None, :])
    k_decay = jnp.exp((C - 1.0 - idx[:, None]) * log_gamma[None, :])
    chunk_decay = jnp.exp(C * log_gamma)

    qc = q.reshape(B, N, C, H, DK)
    kc = k.reshape(B, N, C, H, DK)
    vc = v.reshape(B, N, C, H, DV)

    scores = jnp.einsum('bnqhd,bnkhd->bnhqk', qc, kc) * decay[None, None]
    intra = jnp.einsum('bnhqk,bnkhe->bnqhe', scores, vc)

    def step(state, inp):
        qi, ki, vi = inp
        out = jnp.einsum('bchd,bhde->bche', qi, state) * q_decay[None, :, :, None]
        state = state * chunk_decay[None, :, None, None] + jnp.einsum(
            'bchd,bche->bhde', ki * k_decay[None, :, :, None], vi)
        return state, out

    state0 = jnp.zeros((B, H, DK, DV), jnp.float32)
    _, inter = lax.scan(step, state0, (qc.transpose(1, 0, 2, 3, 4),
                                       kc.transpose(1, 0, 2, 3, 4),
                                       vc.transpose(1, 0, 2, 3, 4)))
    o = (intra + inter.transpose(1, 0, 2, 3, 4)).reshape(B, S, H, DV)

    mu = jnp.mean(o, axis=-1, keepdims=True)
    oc = o - mu
    o = oc * lax.rsqrt(jnp.mean(oc * oc, axis=-1, keepdims=True) + NORM_EPS)
    o = o.reshape(B, S, RET_V_W).astype(x.dtype) * jax.nn.silu(g)
    return o @ w_out


def rglru_mixer(x, w_in, conv_w, conv_b, gate_w, gate_b, a_param, w_out):
    B, S, _ = x.shape
    proj = x @ w_in
    y_branch, u = jnp.split(proj, 2, axis=-1)
    y_branch = jax.nn.gelu(y_branch, approximate=True)

    u = lax.conv_general_dilated(
        u, conv_w[:, None, :], window_strides=(1,), padding=[(LRU_CONV - 1, 0)],
        dimension_numbers=('NWC', 'WIO', 'NWC'), feature_group_count=LRU_WIDTH) + conv_b

    ub = u.reshape(B, S, LRU_BLOCKS, LRU_BLOCK)
    gates = jnp.einsum('bsnk,gnkj->gbsnj', ub, gate_w) + gate_b[:, None, None]
    gates = jax.nn.sigmoid(gates.astype(jnp.float32)).reshape(2, B, S, LRU_WIDTH)
    r, i = gates[0], gates[1]

    log_a = -LRU_C * r * jax.nn.softplus(-a_param.astype(jnp.float32))
    a = jnp.exp(log_a)
    mult = jnp.sqrt(-jnp.expm1(2.0 * log_a))
    b = mult * (i * u.astype(jnp.float32))

    def combine(lhs, rhs):
        a1, b1 = lhs
        a2, b2 = rhs
        return a1 * a2, a2 * b1 + b2

    _, h = lax.associative_scan(combine, (a, b), axis=1)
    return (h.astype(x.dtype) * y_branch) @ w_out


def swiglu_ffn(x, w_in, w_out):
    gate, up = jnp.split(x @ w_in, 2, axis=-1)
    return (jax.nn.silu(gate) * up) @ w_out


def setup_inputs(seed: int = 0) -> dict:
    key = jax.random.key(seed)
    ks = jax.random.split(key, 16)
    f32 = jnp.float32

    def normal(k, shape, fan_in):
        return jax.random.normal(k, shape, f32) * (fan_in ** -0.5)

    x = jax.random.normal(ks[0], (BATCH, SEQ, D_MODEL), f32)
    ret_w_in = normal(ks[1], (N_RET, D_MODEL, 2 * RET_QK_W + 2 * RET_V_W), D_MODEL)
    ret_w_out = normal(ks[2], (N_RET, RET_V_W, D_MODEL), RET_V_W)
    lru_w_in = normal(ks[3], (N_LRU, D_MODEL, 2 * LRU_WIDTH), D_MODEL)
    lru_conv_w = normal(ks[4], (N_LRU, LRU_CONV, LRU_WIDTH), LRU_CONV)
    lru_conv_b = 0.01 * jax.random.normal(ks[5], (N_LRU, LRU_WIDTH), f32)
    lru_gate_w = normal(ks[6], (N_LRU, 2, LRU_BLOCKS, LRU_BLOCK, LRU_BLOCK), LRU_BLOCK)
    lru_gate_b = 0.01 * jax.random.normal(ks[7], (N_LRU, 2, LRU_BLOCKS, LRU_BLOCK), f32)
    a0 = jax.random.uniform(ks[8], (N_LRU, LRU_WIDTH), f32, minval=0.9, maxval=0.999)
    lru_a_param = jnp.log(a0) - jnp.log1p(-a0)
    lru_w_out = normal(ks[9], (N_LRU, LRU_WIDTH, D_MODEL), LRU_WIDTH)
    norm_g = 1.0 + 0.02 * jax.random.normal(ks[10], (DEPTH, 4, D_MODEL), f32)
    ffn_w_in = normal(ks[11], (DEPTH, D_MODEL, 2 * D_FF), D_MODEL)
    ffn_w_out = normal(ks[12], (DEPTH, D_FF, D_MODEL), D_FF)
    return {"x": x, "ret_w_in": ret_w_in, "ret_w_out": ret_w_out,
            "lru_w_in": lru_w_in, "lru_conv_w": lru_conv_w, "lru_conv_b": lru_conv_b,
            "lru_gate_w": lru_gate_w, "lru_gate_b": lru_gate_b, "lru_a_param": lru_a_param,
            "lru_w_out": lru_w_out, "norm_g": norm_g,
            "ffn_w_in": ffn_w_in, "ffn_w_out": ffn_w_out}


def reference(x, ret_w_in, ret_w_out, lru_w_in, lru_conv_w, lru_conv_b, lru_gate_w,
              lru_gate_b, lru_a_param, lru_w_out, norm_g, ffn_w_in, ffn_w_out):
    for layer in range(DEPTH):
        j = layer // N_MIXERS
        h = rms_norm(x, norm_g[layer, 0])
        if layer % N_MIXERS == 0:
            m = retention_mixer(h, ret_w_in[j], ret_w_out[j])
        else:
            m = rglru_mixer(h, lru_w_in[j], lru_conv_w[j], lru_conv_b[j], lru_gate_w[j],
                            lru_gate_b[j], lru_a_param[j], lru_w_out[j])
        x = x + rms_norm(m, norm_g[layer, 1])
        f = swiglu_ffn(rms_norm(x, norm_g[layer, 2]), ffn_w_in[layer], ffn_w_out[layer])
        x = x + rms_norm(f, norm_g[layer, 3])
    return x
```

```python
import contextlib
import math
import numpy as np
import concourse.bass as bass
import concourse.mybir as mybir
from concourse.bass_utils import run_bass_kernel_spmd

F32 = mybir.dt.float32
BF16 = mybir.dt.bfloat16
I32 = mybir.dt.int32
AF = mybir.ActivationFunctionType
ALU = mybir.AluOpType

ENGS = ("pe", "act", "dve", "pool", "sp")

NCORES = 8
D = 1024
C = 128
SEQ = 8192
HALF = 4096
NLOC = 64
HALO = 31
NEXT = 33
H = 4
DK = 256
DV = 512
QKW = 1024
VW = 2048
DFF = 2816
LW = 1536
EPS = 1e-6
LG = [math.log1p(-2.0 ** (-5 - h)) for h in range(H)]


class Buf:
    __slots__ = ("name", "last_w", "readers", "sem", "sem_cnt")

    def __init__(self, name=""):
        self.name = name
        self.last_w = None
        self.readers = []
        self.sem = None
        self.sem_cnt = 0

    def reset(self):
        self.last_w = None
        self.readers = []
        self.sem = None
        self.sem_cnt = 0


class Op:
    __slots__ = ("eng", "fn", "deps", "is_dma", "needs_inc", "count", "dsem", "dval", "oid")

    def __init__(self, eng, fn, is_dma):
        self.eng = eng
        self.fn = fn
        self.deps = set()
        self.is_dma = is_dma
        self.needs_inc = False
        self.count = None
        self.dsem = None
        self.dval = None


class Prog:
    uid = 0

    def __init__(self, nc):
        self.nc = nc
        self.ops = []
        self.n_dma_sems = 0
        self.seen = {}

    def _add(self, op, reads, writes):
        op.oid = len(self.ops)
        for b in reads:
            if b.last_w is not None:
                op.deps.add(b.last_w)
        for b in writes:
            cands = list(b.readers)
            if b.last_w is not None:
                cands.append(b.last_w)
            for r in cands:
                p = self.ops[r]
                if op.is_dma or p.is_dma or p.eng != op.eng:
                    op.deps.add(r)
        op.deps.discard(op.oid)
        for b in reads:
            self.seen[id(b)] = b
        for b in writes:
            self.seen[id(b)] = b
        for b in reads:
            b.readers.append(op.oid)
        for b in writes:
            b.last_w = op.oid
            b.readers = []
        self.ops.append(op)
        return op

    def op(self, eng, fn, reads=(), writes=()):
        return self._add(Op(eng, fn, False), reads, writes)

    def dma(self, eng, fn, reads=(), writes=(), key=None):
        op = Op(eng, fn, True)
        kb = key if key is not None else (writes[0] if writes else reads[0])
        if kb.sem is None:
            kb.sem = self.n_dma_sems
            self.n_dma_sems += 1
            self.seen[id(kb)] = kb
        kb.sem_cnt += 16
        op.dsem = kb.sem
        op.dval = kb.sem_cnt
        return self._add(op, reads, writes)

    def emit(self):
        nc = self.nc
        ops = self.ops
        for o in ops:
            for d in o.deps:
                ops[d].needs_inc = True
        cnt = {e: 0 for e in ENGS}
        for o in ops:
            if not o.is_dma and o.needs_inc:
                cnt[o.eng] += 1
                o.count = cnt[o.eng]
        by_eng = {e: [o for o in ops if o.eng == e] for e in ENGS}
        with contextlib.ExitStack() as st:
            Prog.uid += 1
            u = Prog.uid
            esem = {e: st.enter_context(nc.semaphore("se%d_%s" % (u, e))) for e in ENGS}
            dsem = [st.enter_context(nc.semaphore("sd%d_%d" % (u, i))) for i in range(self.n_dma_sems)]
            with nc.Block() as b0:
                @b0.gpsimd
                def _(g):
                    for sm in list(esem.values()) + dsem:
                        g.sem_clear(sm)
            block = st.enter_context(nc.Block())

            def body(ename, eng):
                known = {}
                for o in by_eng[ename]:
                    need = {}
                    for d in o.deps:
                        p = ops[d]
                        if p.is_dma:
                            k = ("d", p.dsem)
                            v = p.dval
                        else:
                            k = ("e", p.eng)
                            v = p.count
                        if need.get(k, 0) < v:
                            need[k] = v
                    for k, v in need.items():
                        if known.get(k, 0) >= v:
                            continue
                        known[k] = v
                        s = dsem[k[1]] if k[0] == "d" else esem[k[1]]
                        eng.wait_ge(s, v)
                    ins = o.fn(eng)
                    if o.is_dma:
                        ins.then_inc(dsem[o.dsem], 16)
                    elif o.needs_inc:
                        ins.then_inc(esem[ename], 1)
                last = {}
                for o in by_eng[ename]:
                    if o.is_dma:
                        last[o.dsem] = max(last.get(o.dsem, 0), o.dval)
                for s, v in last.items():
                    if known.get(("d", s), 0) < v:
                        eng.wait_ge(dsem[s], v)

            @block.tensor
            def _(e):
                body("pe", e)

            @block.scalar
            def _(e):
                body("act", e)

            @block.vector
            def _(e):
                body("dve", e)

            @block.gpsimd
            def _(e):
                body("pool", e)

            @block.sync
            def _(e):
                body("sp", e)
        for b in self.seen.values():
            b.reset()


class Tl:
    __slots__ = ("t", "b")

    def __init__(self, t, name):
        self.t = t
        self.b = Buf(name)


class Ctx:
    gn = 0

    def __init__(self, nc, st):
        self.nc = nc
        self.st = st
        self.n = 0

    def sb(self, name, shape, dt):
        Ctx.gn += 1
        nm = "%s_%d" % (name, Ctx.gn)
        return Tl(self.st.enter_context(self.nc.sbuf_tensor(nm, shape, dt)), nm)

    def ps(self, name, shape, dt):
        Ctx.gn += 1
        nm = "%s_%d" % (name, Ctx.gn)
        return Tl(self.st.enter_context(self.nc.psum_tensor(nm, shape, dt)), nm)


def run_pipeline(n, stages, lo=0):
    ns = len(stages)
    for it in range(lo, lo + n + ns - 1):
        for k in reversed(range(ns)):
            c = it - k
            if lo <= c < lo + n:
                stages[k](c)


def alloc_weight(nc, st, kchunks, ncols, name):
    Ctx.gn += 1
    return st.enter_context(nc.sbuf_tensor("%s_%d" % (name, Ctx.gn), [128, kchunks, ncols], BF16))


def issue_weight_load(P, wt, w_ap, kchunks, ncols, name, col0=0):
    bufs = [Buf("%s_k%d" % (name, k)) for k in range(kchunks)]
    keyb = Buf(name + "_sem")
    for k in range(kchunks):
        P.dma("pool", (lambda e, k=k: e.dma_start(out=wt[:, k, :],
                                                  in_=w_ap[k * 128:(k + 1) * 128, col0:col0 + ncols])),
              writes=[bufs[k]], key=keyb)
    return bufs


def load_weight(P, cx, w_ap, kchunks, ncols, name, col0=0):
    wt = alloc_weight(cx.nc, cx.st, kchunks, ncols, name)
    return wt, issue_weight_load(P, wt, w_ap, kchunks, ncols, name, col0)


def make_ident(P, cx):
    nc = cx.nc
    identf = cx.sb("identf", [128, 128], F32)
    ident = cx.sb("ident", [128, 128], BF16)

    P.op("pool", lambda e: e.memset(identf.t[:], 0.0), writes=[identf.b])
    P.op("pool", lambda e: e.affine_select(out=identf.t[:], in_=identf.t[:], pattern=[[-1, 128]],
                                           compare_op=ALU.not_equal, fill=1.0, base=0, channel_multiplier=1),
         reads=[identf.b], writes=[identf.b])
    P.op("pool", lambda e: e.tensor_copy(out=ident.t[:], in_=identf.t[:]), reads=[identf.b], writes=[ident.b])
    return ident


def emit_rstd(P, ss, rstd, cnst, width):
    P.op("dve", lambda e: e.tensor_scalar(out=rstd.t[:], in0=ss.t[:], scalar1=1.0 / width, scalar2=EPS,
                                          op0=ALU.mult, op1=ALU.add), reads=[ss.b], writes=[rstd.b])
    n = rstd.t.shape[1]
    P.op("pool", lambda e: e.tensor_tensor(out=rstd.t[:], in0=rstd.t[:], in1=cnst.t[:, 0:n], op=ALU.pow),
         reads=[rstd.b, cnst.b], writes=[rstd.b])


TWO_PI = 2.0 * math.pi
CW1 = 6.28125
CW2 = 0.0019350051879882812
CW3 = TWO_PI - CW1 - CW2


def phase_tables(nc, pos, invf, tabs, prefetch=None):
    TW = 512
    with contextlib.ExitStack() as st:
        cx = Ctx(nc, st)
        P = Prog(nc)
        posb = cx.sb("posb", [128, SEQ], F32)
        inv = cx.sb("inv", [128, 1], F32)
        P.dma("sp", lambda e: e.dma_start(out=posb.t[:], in_=pos.partition_broadcast(128)), writes=[posb.b])
        P.dma("sp", lambda e: e.dma_start(out=inv.t[:], in_=invf[:, :]), writes=[inv.b])
        if prefetch is not None:
            prefetch(P)
        NS = 4
        ang = [cx.sb("ang", [128, TW], F32) for _ in range(NS)]
        ki = [cx.sb("ki", [128, TW], I32) for _ in range(NS)]
        kf = [cx.sb("kf", [128, TW], F32) for _ in range(NS)]
        rr = [cx.sb("rr", [128, TW], F32) for _ in range(NS)]
        rs = [cx.sb("rs", [128, TW], F32) for _ in range(NS)]
        rc = [cx.sb("rc", [128, TW], F32) for _ in range(NS)]
        cs = [cx.sb("cs", [128, 2, TW], F32) for _ in range(NS)]
        ct = [cx.sb("ct", [128, 2, 128], F32) for _ in range(4)]
        pT = [cx.ps("pT", [128, 2, 128], F32) for _ in range(4)]
        identf = cx.sb("identf", [128, 128], F32)
        P.op("pool", lambda e: e.memset(identf.t[:], 0.0), writes=[identf.b])
        P.op("pool", lambda e: e.affine_select(out=identf.t[:], in_=identf.t[:], pattern=[[-1, 128]],
                                               compare_op=ALU.not_equal, fill=1.0, base=0, channel_multiplier=1),
             reads=[identf.b], writes=[identf.b])
        tabB = Buf("tabs")
        for i in range(SEQ // TW):
            s = i % NS
            sl = slice(i * TW, (i + 1) * TW)
            P.op("dve", lambda e, s=s, sl=sl: e.tensor_scalar(out=ang[s].t[:], in0=posb.t[:, sl], scalar1=inv.t[:, 0:1],
                                                              scalar2=None, op0=ALU.mult),
                 reads=[posb.b, inv.b], writes=[ang[s].b])
            P.op("dve", lambda e, s=s: e.tensor_scalar(out=ki[s].t[:], in0=ang[s].t[:], scalar1=1.0 / TWO_PI,
                                                       scalar2=None, op0=ALU.mult),
                 reads=[ang[s].b], writes=[ki[s].b])
            P.op("pool", lambda e, s=s: e.tensor_copy(out=kf[s].t[:], in_=ki[s].t[:]), reads=[ki[s].b], writes=[kf[s].b])
            P.op("dve", lambda e, s=s: e.scalar_tensor_tensor(out=rr[s].t[:], in0=kf[s].t[:], scalar=-CW1, in1=ang[s].t[:], op0=ALU.mult, op1=ALU.add),
                 reads=[ang[s].b, kf[s].b], writes=[rr[s].b])
            P.op("dve", lambda e, s=s: e.scalar_tensor_tensor(out=rs[s].t[:], in0=kf[s].t[:], scalar=-CW2, in1=rr[s].t[:], op0=ALU.mult, op1=ALU.add),
                 reads=[rr[s].b, kf[s].b], writes=[rs[s].b])
            P.op("dve", lambda e, s=s: e.scalar_tensor_tensor(out=rr[s].t[:], in0=kf[s].t[:], scalar=-CW3, in1=rs[s].t[:], op0=ALU.mult, op1=ALU.add),
                 reads=[rs[s].b, kf[s].b], writes=[rr[s].b])
            P.op("dve", lambda e, s=s: e.tensor_scalar(out=ang[s].t[:], in0=rr[s].t[:], scalar1=math.pi, scalar2=-TWO_PI,
                                                       op0=ALU.is_gt, op1=ALU.mult), reads=[rr[s].b], writes=[ang[s].b])
            P.op("dve", lambda e, s=s: e.tensor_tensor(out=rs[s].t[:], in0=rr[s].t[:], in1=ang[s].t[:], op=ALU.add),
                 reads=[rr[s].b, ang[s].b], writes=[rs[s].b])
            P.op("dve", lambda e, s=s: e.tensor_scalar(out=ang[s].t[:], in0=rr[s].t[:], scalar1=math.pi / 2, scalar2=-TWO_PI,
                                                       op0=ALU.is_gt, op1=ALU.mult), reads=[rr[s].b, rs[s].b], writes=[ang[s].b])
            P.op("dve", lambda e, s=s: e.scalar_tensor_tensor(out=rc[s].t[:], in0=rr[s].t[:], scalar=math.pi / 2, in1=ang[s].t[:],
                                                              op0=ALU.add, op1=ALU.add),
                 reads=[rr[s].b, ang[s].b], writes=[rc[s].b])
            P.op("act", lambda e, s=s: e.activation(out=cs[s].t[:, 0, :], in_=rc[s].t[:], func=AF.Sin), reads=[rc[s].b], writes=[cs[s].b])
            P.op("act", lambda e, s=s: e.activation(out=cs[s].t[:, 1, :], in_=rs[s].t[:], func=AF.Sin), reads=[rs[s].b], writes=[cs[s].b])
            for tbk in range(TW // 128):
                pt_ = pT[(i * (TW // 128) + tbk) % 4]
                ct_ = ct[(i * (TW // 128) + tbk) % 4]

                def trc(e, s=s, tbk=tbk, pt_=pt_):
                    e.transpose(out=pt_.t[:, 0, :], in_=cs[s].t[:, 0, tbk * 128:(tbk + 1) * 128], identity=identf.t[:])
                    return e.transpose(out=pt_.t[:, 1, :], in_=cs[s].t[:, 1, tbk * 128:(tbk + 1) * 128], identity=identf.t[:])
                P.op("pe", trc, reads=[cs[s].b, identf.b], writes=[pt_.b])
                P.op("act", lambda e, pt_=pt_, ct_=ct_: e.copy(out=ct_.t[:], in_=pt_.t[:]), reads=[pt_.b], writes=[ct_.b])
                t0 = i * TW + tbk * 128
                P.dma("sp", lambda e, t0=t0, ct_=ct_: e.dma_start(out=tabs[t0:t0 + 128, :, :], in_=ct_.t[:]), reads=[ct_.b], writes=[tabB], key=ct_.b)
        P.emit()


def phase_ret(nc, xf, tabs, w_in, g0, ogd, c_begin=0, c_end=NLOC, dbg=None, pre_w=None):
    with contextlib.ExitStack() as st:
        cx = Ctx(nc, st)
        P = Prog(nc)
        if pre_w is not None:
            wt, wb = pre_w, []
        else:
            wt, wb = load_weight(P, cx, w_in, 8, 2 * QKW + 2 * VW, "retwin")
        ident = make_ident(P, cx)
        g0b = cx.sb("g0b", [128, D], F32)
        P.dma("sp", lambda e: e.dma_start(out=g0b.t[:], in_=g0.partition_broadcast(128)), writes=[g0b.b])
        cm05 = cx.sb("cm05", [128, 8], F32)
        P.op("pool", lambda e: e.memset(cm05.t[:], -0.5), writes=[cm05.b])
        dif_i = cx.sb("dif_i", [128, 128], I32)
        dif = cx.sb("dif", [128, 128], F32)
        dpos = cx.sb("dpos", [128, 128], F32)
        dge = cx.sb("dge", [128, 128], F32)
        maskT = cx.sb("maskT", [128, H, 128], F32)
        qdec = cx.sb("qdec", [128, H, 128], F32)
        kdec = cx.sb("kdec", [128, H], F32)
        qi_i = cx.sb("qi_i", [128, 128], I32)
        qi = cx.sb("qi", [128, 128], F32)
        kk_i = cx.sb("kk_i", [128, 1], I32)
        kk = cx.sb("kk", [128, 1], F32)
        P.op("pool", lambda e: e.iota(dif_i.t[:], pattern=[[1, 128]], base=0, channel_multiplier=-1), writes=[dif_i.b])
        P.op("pool", lambda e: e.tensor_copy(out=dif.t[:], in_=dif_i.t[:]), reads=[dif_i.b], writes=[dif.b])
        P.op("pool", lambda e: e.iota(qi_i.t[:], pattern=[[1, 128]], base=1, channel_multiplier=0), writes=[qi_i.b])
        P.op("pool", lambda e: e.tensor_copy(out=qi.t[:], in_=qi_i.t[:]), reads=[qi_i.b], writes=[qi.b])
        P.op("pool", lambda e: e.iota(kk_i.t[:], pattern=[[0, 1]], base=127, channel_multiplier=-1), writes=[kk_i.b])
        P.op("pool", lambda e: e.tensor_copy(out=kk.t[:], in_=kk_i.t[:]), reads=[kk_i.b], writes=[kk.b])
        P.op("dve", lambda e: e.tensor_scalar(out=dpos.t[:], in0=dif.t[:], scalar1=0.0, scalar2=None, op0=ALU.max),
             reads=[dif.b], writes=[dpos.b])
        P.op("dve", lambda e: e.tensor_scalar(out=dge.t[:], in0=dif.t[:], scalar1=0.0, scalar2=1.0 / 16.0, op0=ALU.is_ge, op1=ALU.mult),
             reads=[dif.b], writes=[dge.b])
        for h in range(H):
            P.op("act", lambda e, h=h: e.activation(out=maskT.t[:, h, :], in_=dpos.t[:], func=AF.Exp, scale=LG[h]),
                 reads=[dpos.b], writes=[maskT.b])
            P.op("act", lambda e, h=h: e.activation(out=qdec.t[:, h, :], in_=qi.t[:], func=AF.Exp, scale=LG[h]),
                 reads=[qi.b], writes=[qdec.b])
            P.op("act", lambda e, h=h: e.activation(out=kdec.t[:, h:h + 1], in_=kk.t[:], func=AF.Exp, scale=LG[h]),
                 reads=[kk.b], writes=[kdec.b])
        P.op("dve", lambda e: e.tensor_tensor(out=maskT.t[:], in0=maskT.t[:], in1=dge.t[:].unsqueeze(1).broadcast_to([128, H, 128]), op=ALU.mult),
             reads=[maskT.b, dge.b], writes=[maskT.b])
        P.op("dve", lambda e: e.tensor_scalar(out=kdec.t[:], in0=kdec.t[:], scalar1=1.0 / 16.0, scalar2=None, op0=ALU.mult),
             reads=[kdec.b], writes=[kdec.b])
        S = cx.sb("S", [128, 2 * H, DV], F32)
        Sb = cx.sb("Sb", [128, 2 * H, DV], BF16)
        Sbufs = [Buf("S%d" % i) for i in range(2 * H)]
        P.op("pool", lambda e: e.memset(S.t[:], 0.0), writes=Sbufs)
        P.op("pool", lambda e: e.memset(Sb.t[:], 0.0), writes=[Sb.b])
        NS = 2
        NTB = 4
        xt = [cx.sb("xt", [128, D], F32) for _ in range(NS)]
        tb = [cx.sb("tb", [128, 2, 128], F32) for _ in range(NTB)]
        junk = None
        ss = [cx.sb("ss", [128, 1], F32) for _ in range(2)]
        rstd = [cx.sb("rstd", [128, 1], F32) for _ in range(2)]
        hb = cx.sb("hb", [128, D], BF16)
        hT = [cx.sb("hT", [128, 8, 128], BF16) for _ in range(2)]
        t1 = cx.sb("t1", [128, 2, 128], F32)
        t2 = cx.sb("t2", [128, 2, 128], F32)
        t3 = cx.sb("t3", [128, 2, 128], F32)
        t4 = cx.sb("t4", [128, 2, 128], F32)
        ktm = [cx.sb("ktm", [128, H, 2, 128], BF16) for _ in range(1)]
        qtm = cx.sb("qtm", [128, H, 2, 128], BF16)
        qT = [cx.sb("qT", [128, H, 2, 128], BF16) for _ in range(2)]
        kT = [cx.sb("kT", [128, H, 2, 128], BF16) for _ in range(2)]
        qd = [cx.sb("qd", [128, H, 2, 128], BF16) for _ in range(2)]
        kd = [cx.sb("kd", [128, H, 2 * 128], BF16) for _ in range(2)]
        vb = [cx.sb("vb", [128, VW], BF16) for _ in range(2)]
        sg = cx.sb("sg", [128, VW], F32)
        sT = cx.sb("sT", [128, H, 128], BF16)
        stats = cx.sb("stats", [128, H, 6], F32)
        mv = cx.sb("mv", [128, H, 2], F32)
        var = cx.sb("var", [128, H], F32)
        rs = cx.sb("rs", [128, H], F32)
        nb = cx.sb("nb", [128, H], F32)
        on = [cx.sb("on", [128, DV], F32) for _ in range(2)]
        og = [cx.sb("og", [128, VW], BF16) for _ in range(NS)]
        p_tr = cx.ps("p_tr", [128, 8, 128], BF16)
        p_qa = cx.ps("p_qa", [128, H, 128], F32)
        p_qb = cx.ps("p_qb", [128, H, 128], F32)
        p_ka = cx.ps("p_ka", [128, H, 128], F32)
        p_kb = cx.ps("p_kb", [128, H, 128], F32)
        p_w = [cx.ps("p_w", [128, 512], F32) for _ in range(2)]
        p_sc = cx.ps("p_sc", [128, H, 128], F32)
        p_o = [p_qa, p_qb, p_ka, p_kb]
        ogB = Buf("ogd")
        cosb = lambda tbt: tbt.t[:, 0, :].unsqueeze(1).broadcast_to([128, H, 128])
        sinb = lambda tbt: tbt.t[:, 1, :].unsqueeze(1).broadcast_to([128, H, 128])

        def stL(c):
            tok = slice(c * C, (c + 1) * C)
            x_ = xt[c % NS]
            tb_ = tb[c % NTB]
            P.dma("act", lambda e: e.dma_start(out=x_.t[:], in_=xf[tok, :]), writes=[x_.b])
            P.dma("act", lambda e: e.dma_start(out=tb_.t[:], in_=tabs[tok, :, :]), writes=[tb_.b])

        def stA(c):
            x_ = xt[c % NS]
            s2 = c % 2
            P.op("act", lambda e: e.activation(out=hb.t[:], in_=x_.t[:], func=AF.Square, accum_out=ss[s2].t[:]),
                 reads=[x_.b], writes=[hb.b, ss[s2].b])
            emit_rstd(P, ss[s2], rstd[s2], cm05, D)
            P.op("dve", lambda e: e.scalar_tensor_tensor(out=hb.t[:], in0=x_.t[:], scalar=rstd[s2].t[:, 0:1], in1=g0b.t[:],
                                                         op0=ALU.mult, op1=ALU.mult),
                 reads=[x_.b, rstd[s2].b, g0b.b], writes=[hb.b])

        def stB_g(c, full, hT_):
            if full:
                for n in range(4):
                    pw = p_w[n % 2]

                    def mm_g(e, n=n, pw=pw):
                        for k in range(8):
                            ins = e.matmul(out=pw.t[:], lhsT=hT_.t[:, k, :],
                                           rhs=wt[:, k, 2 * QKW + VW + n * 512:2 * QKW + VW + (n + 1) * 512],
                                           start=(k == 0), stop=(k == 7))
                        return ins
                    P.op("pe", mm_g, reads=[hT_.b] + wb, writes=[pw.b])
                    P.op("act", lambda e, n=n, pw=pw: e.activation(out=sg.t[:, n * 512:(n + 1) * 512], in_=pw.t[:], func=AF.Silu),
                         reads=[pw.b], writes=[sg.b])


        def stB_t(c, full, ktm_, qtm_, kT_, qT_, qd_):
            if full:
                def tr_q(e):
                    for h in range(H):
                        for half in range(2):
                            ins = e.transpose(out=p_tr.t[:, h * 2 + half, :], in_=qtm_.t[:, h, half, :], identity=ident.t[:])
                    return ins
                P.op("pe", tr_q, reads=[qtm_.b, ident.b], writes=[p_tr.b])
                P.op("act", lambda e: e.copy(out=qT_.t[:].rearrange("p h a t -> p (h a) t"), in_=p_tr.t[:]), reads=[p_tr.b], writes=[qT_.b])
                P.op("pool", lambda e: e.tensor_tensor(out=qd_.t[:], in0=qT_.t[:],
                                                       in1=qdec.t[:].unsqueeze(2).broadcast_to([128, H, 2, 128]), op=ALU.mult),
                     reads=[qT_.b, qdec.b], writes=[qd_.b])

                def tr_k(e):
                    for h in range(H):
                        for half in range(2):
                            ins = e.transpose(out=p_tr.t[:, h * 2 + half, :], in_=ktm_.t[:, h, half, :], identity=ident.t[:])
                    return ins
                P.op("pe", tr_k, reads=[ktm_.b, ident.b], writes=[p_tr.b])
                P.op("dve", lambda e: e.tensor_copy(out=kT_.t[:].rearrange("p h a t -> p (h a) t"), in_=p_tr.t[:]), reads=[p_tr.b], writes=[kT_.b])


        def stA2(c):
            s2 = c % 2

            def tr_h(e):
                for k in range(8):
                    ins = e.transpose(out=p_tr.t[:, k, :], in_=hb.t[:, k * 128:(k + 1) * 128], identity=ident.t[:])
                return ins
            P.op("pe", tr_h, reads=[hb.b, ident.b], writes=[p_tr.b])
            P.op("act", lambda e: e.copy(out=hT[s2].t[:], in_=p_tr.t[:]), reads=[p_tr.b], writes=[hT[s2].b])

        def stB(c, part):
            full = c >= HALO
            s2 = c % 2
            tb_ = tb[c % NTB]
            hT_ = hT[s2]
            kT_, qT_, qd_, kd_, vb_ = kT[s2], qT[s2], qd[s2], kd[s2], vb[s2]

            cos2 = tb_.t[:, 0, :].unsqueeze(1).broadcast_to([128, 2, 128])
            sin2 = tb_.t[:, 1, :].unsqueeze(1).broadcast_to([128, 2, 128])

            def proj_tm(n, pp):
                def f(e):
                    for k in range(8):
                        ins = e.matmul(out=pp.t[:].rearrange("p a b -> p (a b)"), lhsT=hT_.t[:, k, :], rhs=wt[:, k, n * 512:(n + 1) * 512],
                                       start=(k == 0), stop=(k == 7))
                    return ins
                return f

            def rope_tm(pp, dst, hh0):
                pv = pp.t[:].rearrange("p a b -> p (a b)").rearrange("p (h a j) -> p h a j", h=2, a=2)
                A = pv[:, :, 0, :]
                B = pv[:, :, 1, :]
                P.op("dve", lambda e: e.tensor_tensor(out=t1.t[:, 0:2, :], in0=A, in1=cos2, op=ALU.mult), reads=[pp.b, tb_.b], writes=[t1.b])
                P.op("dve", lambda e: e.tensor_tensor(out=t2.t[:, 0:2, :], in0=B, in1=sin2, op=ALU.mult), reads=[pp.b, tb_.b], writes=[t2.b])
                P.op("pool", lambda e: e.tensor_tensor(out=dst.t[:, hh0:hh0 + 2, 0, :], in0=t1.t[:, 0:2, :], in1=t2.t[:, 0:2, :], op=ALU.subtract),
                     reads=[t1.b, t2.b], writes=[dst.b])
                P.op("dve", lambda e: e.tensor_tensor(out=t3.t[:, 0:2, :], in0=A, in1=sin2, op=ALU.mult), reads=[pp.b, tb_.b], writes=[t3.b])
                P.op("dve", lambda e: e.tensor_tensor(out=t4.t[:, 0:2, :], in0=B, in1=cos2, op=ALU.mult), reads=[pp.b, tb_.b], writes=[t4.b])
                P.op("pool", lambda e: e.tensor_tensor(out=dst.t[:, hh0:hh0 + 2, 1, :], in0=t3.t[:, 0:2, :], in1=t4.t[:, 0:2, :], op=ALU.add),
                     reads=[t3.b, t4.b], writes=[dst.b])

            ktm_, qtm_ = ktm[0], qtm
            if part == "g":
                stB_g(c, full, hT_)
                return
            if part == "t":
                stB_t(c, full, ktm_, qtm_, kT_, qT_, qd_)
                return
            P.op("pe", proj_tm(2, p_ka), reads=[hT_.b] + wb, writes=[p_ka.b])
            P.op("pe", proj_tm(3, p_kb), reads=[hT_.b] + wb, writes=[p_kb.b])
            if full:
                P.op("pe", proj_tm(0, p_qa), reads=[hT_.b] + wb, writes=[p_qa.b])
                P.op("pe", proj_tm(1, p_qb), reads=[hT_.b] + wb, writes=[p_qb.b])
            rope_tm(p_ka, ktm_, 0)
            rope_tm(p_kb, ktm_, 2)
            P.op("dve", lambda e: e.tensor_tensor(out=kd_.t[:], in0=ktm_.t[:].rearrange("p h a j -> p h (a j)"),
                                                  in1=kdec.t[:].unsqueeze(2).broadcast_to([128, H, 256]), op=ALU.mult),
                 reads=[ktm_.b, kdec.b], writes=[kd_.b])
            if full:
                rope_tm(p_qa, qtm_, 0)
                rope_tm(p_qb, qtm_, 2)
            for n in range(4):
                pw = p_w[n % 2]

                def mm_v(e, n=n, pw=pw):
                    for k in range(8):
                        ins = e.matmul(out=pw.t[:], lhsT=hT_.t[:, k, :], rhs=wt[:, k, 2 * QKW + n * 512:2 * QKW + (n + 1) * 512],
                                       start=(k == 0), stop=(k == 7))
                    return ins
                P.op("pe", mm_v, reads=[hT_.b] + wb, writes=[pw.b])
                P.op("act", lambda e, n=n, pw=pw: e.copy(out=vb_.t[:, n * 512:(n + 1) * 512], in_=pw.t[:]), reads=[pw.b], writes=[vb_.b])
        def stC(c):
            full = c >= HALO
            s2 = c % 2
            s = c % NS
            kT_, qT_, qd_, kd_, vb_ = kT[s2], qT[s2], qd[s2], kd[s2], vb[s2]
            if full:
                def mm_sc(e):
                    for h in range(H):
                        for half in range(2):
                            ins = e.matmul(out=p_sc.t[:, h, :], lhsT=kT_.t[:, h, half, :], rhs=qT_.t[:, h, half, :],
                                           start=(half == 0), stop=(half == 1))
                    return ins
                P.op("pe", mm_sc, reads=[kT_.b, qT_.b], writes=[p_sc.b])
                P.op("dve", lambda e: e.tensor_tensor(out=sT.t[:], in0=p_sc.t[:], in1=maskT.t[:], op=ALU.mult),
                     reads=[p_sc.b, maskT.b], writes=[sT.b])
                for h in range(H):
                    def mm_o(e, h=h):
                        e.matmul(out=p_o[h].t[:].rearrange("p a b -> p (a b)"), lhsT=sT.t[:, h, :], rhs=vb_.t[:, h * DV:(h + 1) * DV],
                                 start=True, stop=False)
                        e.matmul(out=p_o[h].t[:].rearrange("p a b -> p (a b)"), lhsT=qd_.t[:, h, 0, :], rhs=Sb.t[:, 2 * h, :],
                                 start=False, stop=False)
                        return e.matmul(out=p_o[h].t[:].rearrange("p a b -> p (a b)"), lhsT=qd_.t[:, h, 1, :], rhs=Sb.t[:, 2 * h + 1, :],
                                        start=False, stop=True)
                    P.op("pe", mm_o, reads=[sT.b, vb_.b, qd_.b, Sb.b], writes=[p_o[h].b])
                    P.op("dve", lambda e, h=h: e.bn_stats(out=stats.t[:, h, :], in_=p_o[h].t[:].rearrange("p a b -> p (a b)")),
                         reads=[p_o[h].b], writes=[stats.b])
            if full:
                P.op("dve", lambda e: [e.bn_aggr(out=mv.t[:, h, :], in_=stats.t[:, h, :]) for h in range(H)][-1],
                     reads=[stats.b], writes=[mv.b])
                P.op("dve", lambda e: e.tensor_scalar(out=var.t[:], in0=mv.t[:, :, 1], scalar1=EPS, scalar2=None, op0=ALU.add),
                     reads=[mv.b], writes=[var.b])
                P.op("pool", lambda e: e.tensor_tensor(out=rs.t[:], in0=var.t[:], in1=cm05.t[:, 0:H], op=ALU.pow),
                     reads=[var.b, cm05.b], writes=[rs.b])
                P.op("dve", lambda e: e.scalar_tensor_tensor(out=nb.t[:], in0=mv.t[:, :, 0], scalar=-1.0, in1=rs.t[:],
                                                             op0=ALU.mult, op1=ALU.mult),
                     reads=[mv.b, rs.b], writes=[nb.b])
                for h in range(H):
                    o_n = on[h % 2]
                    P.op("act", lambda e, h=h, o_n=o_n: e.activation(out=o_n.t[:], in_=p_o[h].t[:].rearrange("p a b -> p (a b)"),
                                                                     func=AF.Identity, scale=rs.t[:, h:h + 1], bias=nb.t[:, h:h + 1]),
                         reads=[p_o[h].b, rs.b, nb.b], writes=[o_n.b])
                    P.op("pool", lambda e, h=h, o_n=o_n: e.tensor_tensor(out=og[s].t[:, h * DV:(h + 1) * DV], in0=o_n.t[:],
                                                                         in1=sg.t[:, h * DV:(h + 1) * DV], op=ALU.mult),
                         reads=[o_n.b, sg.b], writes=[og[s].b])
                P.dma("sp", lambda e: e.dma_start(out=ogd[c - HALO, :, :], in_=og[s].t[:]), reads=[og[s].b], writes=[ogB], key=og[s].b)

        def stC2(c):
            s2 = c % 2
            kd_, vb_ = kd[s2], vb[s2]
            for h in range(H):
                for half in range(2):
                    i = 2 * h + half
                    pw = p_w[i % 2]

                    def mm_s(e, h=h, half=half, pw=pw):
                        return e.matmul(out=pw.t[:], lhsT=kd_.t[:, h, half * 128:(half + 1) * 128], rhs=vb_.t[:, h * DV:(h + 1) * DV],
                                        start=True, stop=True)
                    P.op("pe", mm_s, reads=[kd_.b, vb_.b], writes=[pw.b])
                    P.op("dve", lambda e, i=i, h=h, pw=pw: e.scalar_tensor_tensor(out=S.t[:, i, :], in0=S.t[:, i, :],
                                                                                  scalar=math.exp(C * LG[h]), in1=pw.t[:],
                                                                                  op0=ALU.mult, op1=ALU.add),
                         reads=[Sbufs[i], pw.b], writes=[Sbufs[i]])
            P.op("act", lambda e: e.copy(out=Sb.t[:], in_=S.t[:]), reads=Sbufs, writes=[Sb.b])

        n_ = c_end - c_begin
        lo_ = c_begin
        ok_ = lambda c: lo_ <= c < lo_ + n_
        for it in range(lo_, lo_ + n_ + 3):
            if ok_(it - 1):
                stA(it - 1)
            if ok_(it - 2):
                stB(it - 2, "kqv")
            if ok_(it - 1):
                stA2(it - 1)
            if ok_(it - 3):
                stC(it - 3)
            if ok_(it - 2):
                stB(it - 2, "t")
            if ok_(it - 3):
                stC2(it - 3)
            if ok_(it - 2):
                stB(it - 2, "g")
            if ok_(it):
                stL(it)
        qT, kT = qT[(c_end - 1) % 2], kT[(c_end - 1) % 2]
        if dbg is not None:
            dB = Buf("dbg")
            P.dma("sp", lambda e: e.dma_start(out=dbg["qT"][:, :], in_=qT.t[:].rearrange("p h a t -> p (h a t)")), reads=[qT.b], writes=[dB], key=qT.b)
            P.dma("sp", lambda e: e.dma_start(out=dbg["kT"][:, :], in_=kT.t[:].rearrange("p h a t -> p (h a t)")), reads=[kT.b], writes=[dB], key=kT.b)
            P.dma("sp", lambda e: e.dma_start(out=dbg["S"][:, :], in_=S.t[:].rearrange("p i e -> p (i e)")), reads=Sbufs, writes=[dB], key=S.b)
        P.emit()


def emit_post_norm_residual(P, pm, xres, gb, outt, junk, ss, rstd, cm05, tmp):
    def sq(e):
        e.activation(out=tmp.t[:, 0:512], in_=pm[0].t[:], func=AF.Square, accum_out=ss.t[:, 0:1])
        return e.activation(out=tmp.t[:, 512:1024], in_=pm[1].t[:], func=AF.Square, accum_out=ss.t[:, 1:2])
    P.op("act", sq, reads=[pm[0].b, pm[1].b], writes=[tmp.b, ss.b])
    P.op("dve", lambda e: e.tensor_tensor(out=ss.t[:, 2:3], in0=ss.t[:, 0:1], in1=ss.t[:, 1:2], op=ALU.add),
         reads=[ss.b], writes=[ss.b])
    P.op("dve", lambda e: e.tensor_scalar(out=rstd.t[:], in0=ss.t[:, 2:3], scalar1=1.0 / D, scalar2=EPS,
                                          op0=ALU.mult, op1=ALU.add), reads=[ss.b], writes=[rstd.b])
    P.op("pool", lambda e: e.tensor_tensor(out=rstd.t[:], in0=rstd.t[:], in1=cm05.t[:, 0:1], op=ALU.pow),
         reads=[rstd.b, cm05.b], writes=[rstd.b])
    for n in range(2):
        P.op("dve", lambda e, n=n: e.scalar_tensor_tensor(out=tmp.t[:, n * 512:(n + 1) * 512], in0=pm[n].t[:], scalar=rstd.t[:, 0:1],
                                                          in1=gb.t[:, n * 512:(n + 1) * 512], op0=ALU.mult, op1=ALU.mult),
             reads=[pm[n].b, rstd.b, gb.b], writes=[tmp.b])
    P.op("pool", lambda e: e.tensor_tensor(out=outt.t[:], in0=tmp.t[:], in1=xres.t[:], op=ALU.add),
         reads=[tmp.b, xres.b], writes=[outt.b])


def phase_ret_out(nc, xf, ogd, w_out, g1, xmid, prefetch=None):
    with contextlib.ExitStack() as st:
        cx = Ctx(nc, st)
        P = Prog(nc)
        wt, wb = load_weight(P, cx, w_out, 16, D, "retwout")
        if prefetch is not None:
            prefetch(P)
        ident = make_ident(P, cx)
        g1b = cx.sb("g1b", [128, D], F32)
        P.dma("sp", lambda e: e.dma_start(out=g1b.t[:], in_=g1.partition_broadcast(128)), writes=[g1b.b])
        cm05 = cx.sb("cm05", [128, 8], F32)
        P.op("pool", lambda e: e.memset(cm05.t[:], -0.5), writes=[cm05.b])
        NX = 5
        xt = [cx.sb("xt", [128, D], F32) for _ in range(NX)]
        ogt = [cx.sb("ogt", [128, VW], BF16) for _ in range(4)]
        ogT = [cx.sb("ogT", [128, 16, 128], BF16) for _ in range(3)]
        junk = None
        ss = [cx.sb("ss", [128, 4], F32) for _ in range(2)]
        rstd = [cx.sb("rstd", [128, 1], F32) for _ in range(2)]
        tmp = [cx.sb("tmp", [128, D], F32) for _ in range(2)]
        xo = [cx.sb("xo", [128, D], F32) for _ in range(2)]
        p_tr = [cx.ps("p_tr", [128, 8, 128], BF16) for _ in range(2)]
        p_m = [[cx.ps("p_m", [128, 512], F32) for _ in range(2)] for _ in range(2)]
        xmB = Buf("xmid")

        def stL(c):
            tok = slice((HALO + c) * C, (HALO + c + 1) * C)
            x_ = xt[c % NX]
            o_ = ogt[c % 4]
            P.dma("act", lambda e: e.dma_start(out=x_.t[:], in_=xf[tok, :]), writes=[x_.b])
            P.dma("act", lambda e: e.dma_start(out=o_.t[:], in_=ogd[c, :, :]), writes=[o_.b])

        def stA(c):
            o_ = ogt[c % 4]
            oT = ogT[c % 3]
            for half in range(2):
                def tr(e, half=half):
                    for k in range(8):
                        kk = half * 8 + k
                        ins = e.transpose(out=p_tr[half].t[:, k, :], in_=o_.t[:, kk * 128:(kk + 1) * 128], identity=ident.t[:])
                    return ins
                P.op("pe", tr, reads=[o_.b, ident.b], writes=[p_tr[half].b])
                if half == 0:
                    P.op("act", lambda e: e.copy(out=oT.t[:, 0:8, :], in_=p_tr[0].t[:]), reads=[p_tr[0].b], writes=[oT.b])
                else:
                    P.op("dve", lambda e: e.tensor_copy(out=oT.t[:, 8:16, :], in_=p_tr[1].t[:]), reads=[p_tr[1].b], writes=[oT.b])

        def stB(c):
            s = c % 2
            oT = ogT[c % 3]
            pm = p_m[s]
            for n in range(2):
                def mm(e, n=n):
                    for k in range(16):
                        ins = e.matmul(out=pm[n].t[:], lhsT=oT.t[:, k, :], rhs=wt[:, k, n * 512:(n + 1) * 512],
                                       start=(k == 0), stop=(k == 15))
                    return ins
                P.op("pe", mm, reads=[oT.b] + wb, writes=[pm[n].b])
            emit_post_norm_residual(P, pm, xt[c % NX], g1b, xo[s], junk, ss[s], rstd[s], cm05, tmp[s])
            P.dma("sp", lambda e: e.dma_start(out=xmid[c, :, :], in_=xo[s].t[:]), reads=[xo[s].b], writes=[xmB], key=xo[s].b)

        run_pipeline(NEXT, [stL, stA, (lambda c: None), stB])
        P.emit()


def phase_ffn(nc, xin, w_in, w_out, g2, g3, xout_fn, c0, c1, pre_wi=None):
    with contextlib.ExitStack() as st:
        cx = Ctx(nc, st)
        P = Prog(nc)
        if pre_wi is not None:
            wi, wib = pre_wi, []
        else:
            wi, wib = load_weight(P, cx, w_in, 8, 2 * DFF, "ffnwin")
        wo, wob = load_weight(P, cx, w_out, 22, D, "ffnwout")
        ident = make_ident(P, cx)
        g2b = cx.sb("g2b", [128, D], F32)
        g3b = cx.sb("g3b", [128, D], F32)
        P.dma("sp", lambda e: e.dma_start(out=g2b.t[:], in_=g2.partition_broadcast(128)), writes=[g2b.b])
        P.dma("sp", lambda e: e.dma_start(out=g3b.t[:], in_=g3.partition_broadcast(128)), writes=[g3b.b])
        cm05 = cx.sb("cm05", [128, 8], F32)
        P.op("pool", lambda e: e.memset(cm05.t[:], -0.5), writes=[cm05.b])
        NX = 5
        xt = [cx.sb("xt", [128, D], F32) for _ in range(NX)]
        junk = None
        ss = cx.sb("ss", [128, 4], F32)
        ss1 = [cx.sb("ss1", [128, 1], F32) for _ in range(2)]
        rstd = cx.sb("rstd", [128, 1], F32)
        rstd2 = [cx.sb("rstd2", [128, 1], F32) for _ in range(2)]
        hb = [cx.sb("hb", [128, D], BF16) for _ in range(2)]
        hT = [cx.sb("hT", [128, 8, 128], BF16) for _ in range(2)]
        sgl = [cx.sb("sgl", [128, 512], F32) for _ in range(2)]
        ab = [cx.sb("ab", [128, DFF], BF16) for _ in range(2)]
        aT = cx.sb("aT", [128, 22, 128], BF16)
        tmp = cx.sb("tmp", [128, D], F32)
        xo = [cx.sb("xo", [128, D], F32) for _ in range(2)]
        p_tr = [cx.ps("p_tr", [128, 8, 128], BF16) for _ in range(2)]
        p_g = [cx.ps("p_g", [128, 512], F32) for _ in range(2)]
        p_u = [cx.ps("p_u", [128, 512], F32) for _ in range(2)]
        p_f = [cx.ps("p_f", [128, 512], F32) for _ in range(2)]
        xoB = Buf("xout")
        tiles = [(j * 512, 512) for j in range(5)] + [(2560, 256)]

        def stL(c):
            x_ = xt[c % NX]
            P.dma("act", lambda e: e.dma_start(out=x_.t[:], in_=xin[c, :, :]), writes=[x_.b])

        def stA(c):
            x_ = xt[c % NX]
            s = c % 2
            P.op("act", lambda e: e.activation(out=hb[s].t[:], in_=x_.t[:], func=AF.Square, accum_out=ss1[s].t[:]),
                 reads=[x_.b], writes=[hb[s].b, ss1[s].b])
            emit_rstd(P, ss1[s], rstd2[s], cm05, D)
            P.op("dve", lambda e: e.scalar_tensor_tensor(out=hb[s].t[:], in0=x_.t[:], scalar=rstd2[s].t[:, 0:1], in1=g2b.t[:],
                                                         op0=ALU.mult, op1=ALU.mult),
                 reads=[x_.b, rstd2[s].b, g2b.b], writes=[hb[s].b])

        def stA2(c):
            s = c % 2

            def tr_h(e):
                for k in range(8):
                    ins = e.transpose(out=p_tr[0].t[:, k, :], in_=hb[s].t[:, k * 128:(k + 1) * 128], identity=ident.t[:])
                return ins
            P.op("pe", tr_h, reads=[hb[s].b, ident.b], writes=[p_tr[0].b])
            P.op("act", lambda e: e.copy(out=hT[s].t[:], in_=p_tr[0].t[:]), reads=[p_tr[0].b], writes=[hT[s].b])

        def stB(c):
            s = c % 2
            for j, (c0h, wd) in enumerate(tiles):
                pg = p_g[j % 2]
                pu = p_u[j % 2]
                sl_ = sgl[j % 2]

                def mm_gu(e, c0h=c0h, wd=wd, pg=pg, pu=pu):
                    for k in range(8):
                        e.matmul(out=pg.t[:, 0:wd], lhsT=hT[s].t[:, k, :], rhs=wi[:, k, c0h:c0h + wd], start=(k == 0), stop=(k == 7))
                    for k in range(8):
                        ins = e.matmul(out=pu.t[:, 0:wd], lhsT=hT[s].t[:, k, :], rhs=wi[:, k, DFF + c0h:DFF + c0h + wd],
                                       start=(k == 0), stop=(k == 7))
                    return ins
                P.op("pe", mm_gu, reads=[hT[s].b] + wib, writes=[pg.b, pu.b])
                P.op("act", lambda e, wd=wd, pg=pg, sl_=sl_: e.activation(out=sl_.t[:, 0:wd], in_=pg.t[:, 0:wd], func=AF.Silu),
                     reads=[pg.b], writes=[sl_.b])
                P.op("dve", lambda e, c0h=c0h, wd=wd, pu=pu, sl_=sl_: e.tensor_tensor(out=ab[s].t[:, c0h:c0h + wd], in0=pu.t[:, 0:wd],
                                                                                      in1=sl_.t[:, 0:wd], op=ALU.mult),
                     reads=[pu.b, sl_.b], writes=[ab[s].b])

        def stC(c):
            s = c % 2
            for gi, (k0, nk) in enumerate(((0, 8), (8, 8), (16, 6))):
                pt = p_tr[(gi + 1) % 2]

                def tr_a(e, k0=k0, nk=nk, pt=pt):
                    for k in range(nk):
                        ins = e.transpose(out=pt.t[:, k, :], in_=ab[s].t[:, (k0 + k) * 128:(k0 + k + 1) * 128], identity=ident.t[:])
                    return ins
                P.op("pe", tr_a, reads=[ab[s].b, ident.b], writes=[pt.b])
                if gi % 2 == 0:
                    P.op("act", lambda e, k0=k0, nk=nk, pt=pt: e.copy(out=aT.t[:, k0:k0 + nk, :], in_=pt.t[:, 0:nk, :]),
                         reads=[pt.b], writes=[aT.b])
                else:
                    P.op("dve", lambda e, k0=k0, nk=nk, pt=pt: e.tensor_copy(out=aT.t[:, k0:k0 + nk, :], in_=pt.t[:, 0:nk, :]),
                         reads=[pt.b], writes=[aT.b])

        def stC2(c):
            s = c % 2
            for n in range(2):
                def mm_f(e, n=n):
                    for k in range(22):
                        ins = e.matmul(out=p_f[n].t[:], lhsT=aT.t[:, k, :], rhs=wo[:, k, n * 512:(n + 1) * 512],
                                       start=(k == 0), stop=(k == 21))
                    return ins
                P.op("pe", mm_f, reads=[aT.b] + wob, writes=[p_f[n].b])
            emit_post_norm_residual(P, p_f, xt[c % NX], g3b, xo[s], junk, ss, rstd, cm05, tmp)
            P.dma("sp", lambda e: e.dma_start(out=xout_fn(c), in_=xo[s].t[:]), reads=[xo[s].b], writes=[xoB], key=xo[s].b)

        n_ = c1 - c0
        for it in range(c0, c0 + n_ + 3):
            if c0 <= it - 3 < c0 + n_:
                stC(it - 3)
            if c0 <= it - 1 < c0 + n_:
                stA(it - 1)
            if c0 <= it - 2 < c0 + n_:
                stB(it - 2)
            if c0 <= it - 1 < c0 + n_:
                stA2(it - 1)
            if c0 <= it - 3 < c0 + n_:
                stC2(it - 3)
            if c0 <= it < c0 + n_:
                stL(it)
        P.emit()


def phase_lru(nc, x1, w_in, conv_w, conv_b, gate_w, gate_b, a_param, g0, pqd, hend, dbg=None, ntiles=None):
    T = 256
    NT = HALF // T
    if ntiles is not None:
        NT = ntiles
    NCH = LW // 128
    with contextlib.ExitStack() as st:
        cx = Ctx(nc, st)
        P = Prog(nc)
        st.enter_context(nc.allow_non_contiguous_dma(reason="tiny per-channel parameter vectors"))
        wt, wb = load_weight(P, cx, w_in, 8, 2 * LW, "lruwin")
        Ctx.gn += 1
        gw = st.enter_context(nc.sbuf_tensor("gw_%d" % Ctx.gn, [128, 24, 256], BF16))
        gwB = Buf("gw")
        P.dma("pool", lambda e: e.dma_start(out=gw[:, :, :], in_=gate_w.rearrange("g n (ki p) j -> p (g n ki) j", p=128)), writes=[gwB])
        ident = make_ident(P, cx)
        g0b = cx.sb("g0b", [128, D], F32)
        P.dma("sp", lambda e: e.dma_start(out=g0b.t[:], in_=g0.partition_broadcast(128)), writes=[g0b.b])
        cm05 = cx.sb("cm05", [128, 8], F32)
        P.op("pool", lambda e: e.memset(cm05.t[:], -0.5), writes=[cm05.b])
        cw = cx.sb("cw", [128, 4, NCH], F32)
        cb = cx.sb("cb", [128, NCH], F32)
        gb = cx.sb("gb", [128, 2, NCH], F32)
        ap_ = cx.sb("ap", [128, NCH], F32)
        cv = cx.sb("cv", [128, NCH], F32)
        cv2 = cx.sb("cv2", [128, NCH], F32)
        for i in range(4):
            P.dma("sp", lambda e, i=i: e.dma_start(out=cw.t[:, i, :], in_=conv_w[i:i + 1, :].rearrange("o (c p) -> p (o c)", p=128)), writes=[cw.b])
        P.dma("sp", lambda e: e.dma_start(out=cb.t[:], in_=conv_b.rearrange("o (c p) -> p (o c)", p=128)), writes=[cb.b])
        P.dma("sp", lambda e: e.dma_start(out=ap_.t[:], in_=a_param.rearrange("o (c p) -> p (o c)", p=128)), writes=[ap_.b])
        for g in range(2):
            P.dma("sp", lambda e, g=g: e.dma_start(out=gb.t[:, g, :], in_=gate_b[g:g + 1, :].rearrange("o (c p) -> p (o c)", p=128)), writes=[gb.b])
        P.op("act", lambda e: e.activation(out=cv.t[:], in_=ap_.t[:], func=AF.Exp, scale=-1.0), reads=[ap_.b], writes=[cv.b])
        P.op("act", lambda e: e.activation(out=cv.t[:], in_=cv.t[:], func=AF.Ln, bias=1.0), reads=[cv.b], writes=[cv.b])
        P.op("dve", lambda e: e.tensor_scalar(out=cv2.t[:], in0=cv.t[:], scalar1=-16.0, scalar2=None, op0=ALU.mult), reads=[cv.b], writes=[cv2.b])
        P.op("dve", lambda e: e.tensor_scalar(out=cv.t[:], in0=cv.t[:], scalar1=-8.0, scalar2=None, op0=ALU.mult), reads=[cv.b, cv2.b], writes=[cv.b])
        hst = cx.sb("hst", [128, NCH], F32)
        Ast = cx.sb("Ast", [128, NCH], F32)
        hstB = [Buf("hst%d" % c) for c in range(NCH)]
        AstB = [Buf("Ast%d" % c) for c in range(NCH)]
        P.op("pool", lambda e: e.memset(hst.t[:], 0.0), writes=hstB)
        P.op("pool", lambda e: e.memset(Ast.t[:], 1.0), writes=AstB)
        NX = 4
        NH = 3
        xt = [cx.sb("xt", [128, D], F32) for _ in range(NX)]
        junk = None
        ss = [cx.sb("ss", [128, 1], F32) for _ in range(2)]
        rstd = [cx.sb("rstd", [128, 1], F32) for _ in range(2)]
        hb2 = [cx.sb("hb", [128, D], BF16) for _ in range(2)]
        hT = [cx.sb("hT", [128, 8, T], BF16) for _ in range(NH)]
        yb = cx.sb("yb", [128, NCH, T], F32)
        ub = cx.sb("ub", [128, NCH, T + 3], F32)
        uc = cx.sb("uc", [128, NCH, T], F32)
        ucb = cx.sb("ucb", [128, NCH, T], BF16)
        rt = cx.sb("rt", [128, NCH, T], F32)
        it2 = [cx.sb("it", [128, NCH, T], F32) for _ in range(2)]
        at2 = [cx.sb("at", [128, NCH, T], F32) for _ in range(2)]
        hs = [cx.sb("hs", [128, T], F32) for _ in range(2)]
        cA = [cx.sb("cA", [128, T], F32) for _ in range(2)]
        NPQ = 2
        Pc = [cx.sb("Pc", [128, T], F32) for _ in range(NPQ)]
        Qc = [cx.sb("Qc", [128, T], F32) for _ in range(NPQ)]
        ybB = [Buf("yb%d" % c) for c in range(NCH)]
        ucB = [Buf("uc%d" % c) for c in range(NCH)]
        rB = [Buf("r%d" % c) for c in range(NCH)]
        iB = [[Buf("i%d_%d" % (p_, c)) for c in range(NCH)] for p_ in range(2)]
        aB = [[Buf("a%d_%d" % (p_, c)) for c in range(NCH)] for p_ in range(2)]
        p_tr2 = [cx.ps("p_tr", [128, 8, 128], BF16) for _ in range(2)]
        NPP = 4

        class PV:
            def __init__(self, tl):
                self.tl, self.b = tl, tl.b

            def sl(self, a, b_):
                return self.tl.t[:, a:b_]
        p_p = [PV(cx.ps("p_p", [128, 512], F32)) for k in range(NPP)]
        p_g = [cx.ps("p_g", [128, 512], F32) for _ in range(2)]
        pqB = Buf("pqd")
        P.op("pool", lambda e: e.memset(ub.t[:], 0.0), writes=[ub.b])
        nrm = [0]

        def load_x(cidx):
            x_ = xt[cidx % NX]
            P.dma("act", lambda e: e.dma_start(out=x_.t[:], in_=x1[cidx, :, :]), writes=[x_.b])

        def norm_a(cidx, slot):
            x_ = xt[cidx % NX]
            hb = hb2[slot]
            p_tr = p_tr2[slot]
            s2 = nrm[0] % 2
            nrm[0] += 1
            P.op("act", lambda e: e.activation(out=hb.t[:], in_=x_.t[:], func=AF.Square, accum_out=ss[s2].t[:]),
                 reads=[x_.b], writes=[hb.b, ss[s2].b])
            emit_rstd(P, ss[s2], rstd[s2], cm05, D)
            P.op("dve", lambda e: e.scalar_tensor_tensor(out=hb.t[:], in0=x_.t[:], scalar=rstd[s2].t[:, 0:1], in1=g0b.t[:],
                                                         op0=ALU.mult, op1=ALU.mult),
                 reads=[x_.b, rstd[s2].b, g0b.b], writes=[hb.b])

            def tr_h(e):
                for k in range(8):
                    ins = e.transpose(out=p_tr.t[:, k, :], in_=hb.t[:, k * 128:(k + 1) * 128], identity=ident.t[:])
                return ins
            P.op("pe", tr_h, reads=[hb.b, ident.b], writes=[p_tr.b])

        def norm_b(slot, hT_, dst_col):
            p_tr = p_tr2[slot]
            P.op("act", lambda e: e.copy(out=hT_.t[:, :, dst_col:dst_col + 128], in_=p_tr.t[:]), reads=[p_tr.b], writes=[hT_.b])

        def norm_T(cidx, hT_, dst_col):
            norm_a(cidx, 0)
            norm_b(0, hT_, dst_col)

        def proj(fc, n, pp, hT_):
            def f(e):
                for k in range(8):
                    ins = e.matmul(out=pp.sl(0, n), lhsT=wt[:, k, fc * 128:(fc + 1) * 128], rhs=hT_.t[:, k, 0:n],
                                   start=(k == 0), stop=(k == 7))
                return ins
            return f

        load_x(0)
        norm_T(0, hT[NH - 1], 0)
        for c in range(NCH):
            pp = p_p[c % NPP]
            P.op("pe", proj(NCH + c, 128, pp, hT[NH - 1]), reads=[hT[NH - 1].b] + wb, writes=[pp.b])
            P.op("act", lambda e, c=c, pp=pp: e.copy(out=ub.t[:, c, 0:3], in_=pp.sl(125, 128)), reads=[pp.b], writes=[ub.b])

        def stL(ti):
            for ci in range(T // 128):
                load_x(1 + ti * (T // 128) + ci)

        def stA_norm_a(ti):
            for ci in range(T // 128):
                norm_a(1 + ti * (T // 128) + ci, ci)

        def stA_norm_b(ti):
            hT_ = hT[ti % NH]
            for ci in range(T // 128):
                norm_b(ci, hT_, ci * 128)

        def stA_gen(ti):
            hT_ = hT[ti % NH]
            for c in range(NCH):
                pp = p_p[c % NPP]
                P.op("pe", proj(NCH + c, T, pp, hT_), reads=[hT_.b] + wb, writes=[pp.b])
                P.op("act", lambda e, c=c, pp=pp: e.copy(out=ub.t[:, c, 3:3 + T], in_=pp.sl(0, T)),
                     reads=[pp.b], writes=[ub.b])
                yield

        GRP = [list(range(0, NCH // 2)), list(range(NCH // 2, NCH))]

        def conv_act(ti):
            for c in range(NCH):
                P.op("act", lambda e, c=c: e.activation(out=uc.t[:, c, :], in_=ub.t[:, c, 3:3 + T], func=AF.Identity,
                                                        scale=cw.t[:, 3, c:c + 1], bias=cb.t[:, c:c + 1]),
                     reads=[ub.b, cw.b, cb.b], writes=[ucB[c]])

        def y_gelu(ti):
            hT_ = hT[ti % NH]
            for c in range(NCH):
                pp = p_p[c % NPP]
                P.op("pe", proj(c, T, pp, hT_), reads=[hT_.b] + wb, writes=[pp.b])
                P.op("act", lambda e, c=c, pp=pp: e.activation(out=yb.t[:, c, :], in_=pp.sl(0, T), func=AF.Gelu_apprx_tanh),
                     reads=[pp.b], writes=[ybB[c]])

        def conv_gates(ti):
            it_, iB_ = it2[ti % 2], iB[ti % 2]
            for gi, grp in enumerate(GRP):
                for i in range(3):
                    for c in grp:
                        P.op("dve", lambda e, c=c, i=i: e.scalar_tensor_tensor(out=uc.t[:, c, :], in0=ub.t[:, c, i:i + T],
                                                                                scalar=cw.t[:, i, c:c + 1], in1=uc.t[:, c, :],
                                                                                op0=ALU.mult, op1=ALU.add),
                             reads=[ub.b, cw.b, ucB[c]], writes=[ucB[c]])
                c0g, c1g = grp[0], grp[-1] + 1
                P.op("dve", lambda e, c0g=c0g, c1g=c1g: e.tensor_copy(out=ucb.t[:, c0g:c1g, :], in_=uc.t[:, c0g:c1g, :]),
                     reads=[ucB[c] for c in grp], writes=[ucbB[gi]])
                if gi == 1:
                    P.op("pool", lambda e: e.tensor_copy(out=ub.t[:, :, 0:3], in_=ub.t[:, :, T:T + 3]), reads=[ub.b], writes=[ub.b])
                for g in range(2):
                    for c in grp:
                        n, jo = c // 2, c % 2
                        pg = p_g[(g * NCH + c) % 2]

                        def mm_g(e, g=g, n=n, jo=jo, pg=pg):
                            for ki in range(2):
                                ins = e.matmul(out=pg.t[:, 0:T], lhsT=gw[:, (g * 6 + n) * 2 + ki, jo * 128:(jo + 1) * 128],
                                               rhs=ucb.t[:, 2 * n + ki, :], start=(ki == 0), stop=(ki == 1))
                            return ins
                        P.op("pe", mm_g, reads=[ucbB[gi], gwB], writes=[pg.b])
                        dst, dB = (rt, rB) if g == 0 else (it_, iB_)
                        P.op("act", lambda e, g=g, c=c, pg=pg, dst=dst: e.activation(out=dst.t[:, c, :], in_=pg.t[:, 0:T], func=AF.Sigmoid,
                                                                                     bias=gb.t[:, g, c:c + 1]),
                             reads=[pg.b, gb.b], writes=[dB[c]])

        def exp_sqrt_gen(ti):
            at_, aB_ = at2[ti % 2], aB[ti % 2]
            for c in range(NCH):
                P.op("act", lambda e, c=c: e.activation(out=at_.t[:, c, :], in_=rt.t[:, c, :], func=AF.Exp, scale=cv.t[:, c:c + 1]),
                     reads=[rB[c], cv.b], writes=[aB_[c]])
                yield
                P.op("act", lambda e, c=c: e.activation(out=rt.t[:, c, :], in_=rt.t[:, c, :], func=AF.Exp, scale=cv2.t[:, c:c + 1]),
                     reads=[rB[c], cv2.b], writes=[rB[c]])
                yield
            for c in range(NCH):
                P.op("act", lambda e, c=c: e.activation(out=rt.t[:, c, :], in_=rt.t[:, c, :], func=AF.Sqrt, scale=-1.0, bias=1.0),
                     reads=[rB[c]], writes=[rB[c]])
                yield

        def interleave(ga, gb, ratio):
            da = db = False
            while not (da and db):
                for _ in range(ratio):
                    if not da:
                        try:
                            next(ga)
                        except StopIteration:
                            da = True
                if not db:
                    try:
                        next(gb)
                    except StopIteration:
                        db = True

        def empty_gen():
            return
            yield

        def b_mul(ti):
            it_, iB_ = it2[ti % 2], iB[ti % 2]
            for c in range(NCH):
                P.op("dve", lambda e, c=c: e.tensor_tensor(out=it_.t[:, c, :], in0=it_.t[:, c, :], in1=uc.t[:, c, :], op=ALU.mult),
                     reads=[iB_[c], ucB[c]], writes=[iB_[c]])
            for c in range(NCH):
                P.op("dve", lambda e, c=c: e.tensor_tensor(out=it_.t[:, c, :], in0=it_.t[:, c, :], in1=rt.t[:, c, :], op=ALU.mult),
                     reads=[iB_[c], rB[c]], writes=[iB_[c]])

        def scans(ti):
            it_, iB_ = it2[ti % 2], iB[ti % 2]
            at_, aB_ = at2[ti % 2], aB[ti % 2]
            for c in range(NCH):
                s2 = c % 2
                sq = c % NPQ
                P.op("dve", lambda e, c=c, s2=s2: e.tensor_tensor_scan(out=hs[s2].t[:], data0=at_.t[:, c, :], data1=it_.t[:, c, :],
                                                                       initial=hst.t[:, c:c + 1], op0=ALU.mult, op1=ALU.add),
                     reads=[aB_[c], iB_[c], hstB[c]], writes=[hs[s2].b])
                P.op("dve", lambda e, c=c, s2=s2: e.tensor_tensor_scan(out=cA[s2].t[:], data0=at_.t[:, c, :], data1=at_.t[:, c, :],
                                                                       initial=Ast.t[:, c:c + 1], op0=ALU.mult, op1=ALU.min),
                     reads=[aB_[c], AstB[c]], writes=[cA[s2].b])
                P.op("pool", lambda e, c=c, s2=s2: e.tensor_copy(out=hst.t[:, c:c + 1], in_=hs[s2].t[:, T - 1:T]), reads=[hs[s2].b], writes=[hstB[c]])
                P.op("pool", lambda e, c=c, s2=s2: e.tensor_copy(out=Ast.t[:, c:c + 1], in_=cA[s2].t[:, T - 1:T]), reads=[cA[s2].b], writes=[AstB[c]])
                P.op("dve", lambda e, c=c, s2=s2, sq=sq: e.tensor_tensor(out=Pc[sq].t[:], in0=hs[s2].t[:], in1=yb.t[:, c, :], op=ALU.mult),
                     reads=[hs[s2].b, ybB[c]], writes=[Pc[sq].b])
                P.op("pool", lambda e, c=c, s2=s2, sq=sq: e.tensor_tensor(out=Qc[sq].t[:], in0=cA[s2].t[:], in1=yb.t[:, c, :], op=ALU.mult),
                     reads=[cA[s2].b, ybB[c]], writes=[Qc[sq].b])
                P.dma("sp", lambda e, c=c, sq=sq: e.dma_start(out=pqd[0, ti, :, c * T:(c + 1) * T], in_=Pc[sq].t[:]), reads=[Pc[sq].b], writes=[pqB], key=Pc[sq].b)
                P.dma("sp", lambda e, c=c, sq=sq: e.dma_start(out=pqd[1, ti, :, c * T:(c + 1) * T], in_=Qc[sq].t[:]), reads=[Qc[sq].b], writes=[pqB], key=Qc[sq].b)

        ucbB = [Buf("ucb0"), Buf("ucb1")]
        stL(0)
        stA_norm_a(0)
        stA_norm_b(0)
        for it_i in range(1, NT + 3):
            i = it_i - 2
            if 0 <= i + 2 < NT:
                stL(i + 2)
            if 0 <= i < NT:
                conv_act(i)
            if 0 <= i + 1 < NT and i + 1 >= 1:
                stA_norm_a(i + 1)
            if 0 <= i - 1 < NT:
                y_gelu(i - 1)
            if 0 <= i < NT:
                conv_gates(i)
            if 0 <= i + 1 < NT and i + 1 >= 1:
                stA_norm_b(i + 1)
            interleave(exp_sqrt_gen(i) if 0 <= i < NT else empty_gen(),
                       stA_gen(i + 1) if 0 <= i + 1 < NT else empty_gen(), 3)
            if 0 <= i - 1 < NT:
                scans(i - 1)
            if 0 <= i < NT:
                b_mul(i)
        hB = Buf("hend")
        P.dma("sp", lambda e: e.dma_start(out=hend[:, :], in_=hst.t[:]), reads=hstB, writes=[hB], key=hst.b)
        P.emit()


def phase_exchange(nc, hend, hall):
    Prog.uid += 1
    with nc.semaphore("cc_sem%d" % Prog.uid) as cc_sem:
      with nc.Block() as b0:
        @b0.gpsimd
        def _(g):
            g.sem_clear(cc_sem)
      with nc.Block() as block:
        @block.gpsimd
        def _(g):
            g.collective_compute("AllGather", ALU.bypass, replica_groups=[[0, 1], [2, 3], [4, 5], [6, 7]],
                                 ins=[hend.opt()], outs=[hall.opt()]).then_inc(cc_sem)
            g.wait_ge(cc_sem, 1)


def phase_lru_out(nc, x1, pqd, hall, isodd, w_out, g1, xmid, nchunks=None):
    NCH = LW // 128
    T = 256
    with contextlib.ExitStack() as st:
        cx = Ctx(nc, st)
        P = Prog(nc)
        wt, wb = load_weight(P, cx, w_out, NCH, D, "lruwout")
        g1b = cx.sb("g1b", [128, D], F32)
        P.dma("sp", lambda e: e.dma_start(out=g1b.t[:], in_=g1.partition_broadcast(128)), writes=[g1b.b])
        cm05 = cx.sb("cm05", [128, 8], F32)
        P.op("pool", lambda e: e.memset(cm05.t[:], -0.5), writes=[cm05.b])
        h0 = cx.sb("h0", [128, NCH], F32)
        odd = cx.sb("odd", [128, 1], F32)
        P.dma("sp", lambda e: e.dma_start(out=h0.t[:], in_=hall[0:128, :]), writes=[h0.b])
        P.dma("sp", lambda e: e.dma_start(out=odd.t[:], in_=isodd[:, :]), writes=[odd.b])
        P.op("dve", lambda e: e.tensor_scalar(out=h0.t[:], in0=h0.t[:], scalar1=odd.t[:, 0:1], scalar2=None, op0=ALU.mult),
             reads=[h0.b, odd.b], writes=[h0.b])
        NS = 3
        NX = 8
        xt = [cx.sb("xt", [128, D], F32) for _ in range(NX)]
        Pt = [cx.sb("Pt", [128, NCH, T], F32) for _ in range(NS)]
        Qt = [cx.sb("Qt", [128, NCH, T], F32) for _ in range(NS)]
        hy = [cx.sb("hy", [128, NCH, T], BF16) for _ in range(NS)]
        junk = None
        ss = [cx.sb("ss", [128, 4], F32) for _ in range(2)]
        rstd = [cx.sb("rstd", [128, 1], F32) for _ in range(2)]
        tmp = [cx.sb("tmp", [128, D], F32) for _ in range(2)]
        xo = [cx.sb("xo", [128, D], F32) for _ in range(2)]
        p_m = [[cx.ps("p_m", [128, 512], F32) for _ in range(2)] for _ in range(2)]
        xmB = Buf("xmid")
        n_grp = (HALF // T) if nchunks is None else nchunks // 2

        def stL(g):
            s = g % NS
            for ci in range(2):
                c = 2 * g + ci
                x_ = xt[c % NX]
                P.dma("act", lambda e, x_=x_, c=c: e.dma_start(out=x_.t[:], in_=x1[c + 1, :, :]), writes=[x_.b])
            P.dma("act", lambda e: e.dma_start(out=Pt[s].t[:], in_=pqd[0, g].rearrange("p (c t) -> p c t", c=NCH)), writes=[Pt[s].b])
            P.dma("act", lambda e: e.dma_start(out=Qt[s].t[:], in_=pqd[1, g].rearrange("p (c t) -> p c t", c=NCH)), writes=[Qt[s].b])

        def stA(g):
            s = g % NS
            P.op("dve", lambda e: e.tensor_tensor(out=Qt[s].t[:], in0=Qt[s].t[:], in1=h0.t[:].unsqueeze(2).broadcast_to([128, NCH, T]), op=ALU.mult),
                 reads=[Qt[s].b, h0.b], writes=[Qt[s].b])
            P.op("pool", lambda e: e.tensor_tensor(out=hy[s].t[:], in0=Qt[s].t[:], in1=Pt[s].t[:], op=ALU.add),
                 reads=[Qt[s].b, Pt[s].b], writes=[hy[s].b])

        def stB(g):
            s = g % NS
            for ci in range(2):
                c = 2 * g + ci
                pm = p_m[ci]
                so = c % 2
                for n in range(2):
                    def mm(e, n=n, pm=pm, ci=ci):
                        for k in range(NCH):
                            ins = e.matmul(out=pm[n].t[:], lhsT=hy[s].t[:, k, ci * 128:(ci + 1) * 128], rhs=wt[:, k, n * 512:(n + 1) * 512],
                                           start=(k == 0), stop=(k == NCH - 1))
                        return ins
                    P.op("pe", mm, reads=[hy[s].b] + wb, writes=[pm[n].b])
                emit_post_norm_residual(P, pm, xt[c % NX], g1b, xo[so], junk, ss[so], rstd[so], cm05, tmp[so])
                P.dma("sp", lambda e, c=c, so=so: e.dma_start(out=xmid[c + 1, :, :], in_=xo[so].t[:]), reads=[xo[so].b], writes=[xmB], key=xo[so].b)

        run_pipeline(n_grp, [stL, stA, stB])
        P.emit()


def build_program(mode="fused"):
    nc = bass.Bass("TRN2", target_bir_lowering=False)
    dt = lambda name, shape, d=F32, **kw: nc.dram_tensor(name, shape, d, **kw).ap()
    inp = dict(kind="ExternalInput")
    outk = dict(kind="ExternalOutput")
    mid_out = outk if mode == "A" else (inp if mode == "B" else {})
    isodd = dt("isodd", [128, 1], **inp)
    lru_w_out = dt("lru_w_out", [LW, D], **inp)
    ng = dt("norm_g", [8, D], **inp)
    fwi = dt("ffn_w_in", [2, D, 2 * DFF], **inp)
    fwo = dt("ffn_w_out", [2, DFF, D], **inp)
    x1 = dt("x1", [NEXT, 128, D], **mid_out)
    pqd = dt("pqd", [2, HALF // 256, 128, (LW // 128) * 256], **mid_out)
    hall = dt("hall", [256, LW // 128], **({} if mode != "B" else inp))
    xmid = dt("xmid", [NEXT, 128, D])
    if mode != "B":
        xf = dt("xf", [SEQ, D], **inp)
        pos = dt("pos", [1, SEQ], **inp)
        invf = dt("invf", [128, 1], **inp)
        ret_w_in = dt("ret_w_in", [D, 2 * QKW + 2 * VW], **inp)
        ret_w_out = dt("ret_w_out", [VW, D], **inp)
        lru_w_in = dt("lru_w_in", [D, 2 * LW], **inp)
        lru_conv_w = dt("lru_conv_w", [4, LW], **inp)
        lru_conv_b = dt("lru_conv_b", [1, LW], **inp)
        lru_gate_w = dt("lru_gate_w", [2, 6, 256, 256], **inp)
        lru_gate_b = dt("lru_gate_b", [2, LW], **inp)
        lru_a = dt("lru_a_param", [1, LW], **inp)
        tabs = dt("tabs", [SEQ, 2, 128])
        ogd = dt("ogd", [NEXT, 128, VW], BF16)
        hend = dt("hend", [128, LW // 128])
        with contextlib.ExitStack() as o0:
            rw = alloc_weight(nc, o0, 8, 2 * QKW + 2 * VW, "retwin")
            phase_tables(nc, pos, invf, tabs,
                         prefetch=lambda P: issue_weight_load(P, rw, ret_w_in, 8, 2 * QKW + 2 * VW, "retwin"))
            phase_ret(nc, xf, tabs, ret_w_in, ng[0:1, :], ogd, pre_w=rw)
        with contextlib.ExitStack() as o1:
            fw0 = alloc_weight(nc, o1, 8, 2 * DFF, "ffnwin")
            phase_ret_out(nc, xf, ogd, ret_w_out, ng[1:2, :], xmid,
                          prefetch=lambda P: issue_weight_load(P, fw0, fwi[0], 8, 2 * DFF, "ffnwin"))
            phase_ffn(nc, xmid, fwi[0], fwo[0], ng[2:3, :], ng[3:4, :], lambda c: x1[c, :, :], 0, NEXT, pre_wi=fw0)
        phase_lru(nc, x1, lru_w_in, lru_conv_w, lru_conv_b, lru_gate_w, lru_gate_b, lru_a, ng[4:5, :], pqd, hend)
        phase_exchange(nc, hend, hall)
        if mode == "A":
            hallo = dt("hallo", [256, LW // 128], **outk)
            Prog.uid += 1
            with nc.semaphore("cps%d" % Prog.uid) as s:
                with nc.Block() as b0:
                    @b0.gpsimd
                    def _(g):
                        g.sem_clear(s)
                with nc.Block() as blk:
                    @blk.sync
                    def _(e):
                        e.dma_start(out=hallo[:, :], in_=hall[:, :]).then_inc(s, 16)
                        e.wait_ge(s, 16)
            return nc
    out = dt("out", [HALF // C, C, D], **outk)
    phase_lru_out(nc, x1, pqd, hall, isodd, lru_w_out, ng[5:6, :], xmid)
    phase_ffn(nc, xmid, fwi[1], fwo[1], ng[6:7, :], ng[7:8, :], lambda c: out[c - 1, :, :], 1, NEXT)
    return nc


_INV = None


def make_in_maps(x, ret_w_in, ret_w_out, lru_w_in, lru_conv_w, lru_conv_b, lru_gate_w, lru_gate_b,
                 lru_a_param, lru_w_out, norm_g, ffn_w_in, ffn_w_out):
    f = lambda a: np.ascontiguousarray(np.asarray(a, dtype=np.float32))
    inv = (1.0 / (np.float32(10000.0) ** (np.arange(128, dtype=np.float32) / np.float32(128)))).astype(np.float32)
    x = f(x)
    shared = {
        "invf": inv[:, None].copy(),
        "ret_w_in": f(ret_w_in[0]), "ret_w_out": f(ret_w_out[0]), "lru_w_in": f(lru_w_in[0]),
        "lru_conv_w": f(lru_conv_w[0]), "lru_conv_b": f(lru_conv_b), "lru_gate_w": f(lru_gate_w[0]),
        "lru_gate_b": f(np.asarray(lru_gate_b)[0].reshape(2, LW)), "lru_a_param": f(lru_a_param),
        "lru_w_out": f(lru_w_out[0]), "norm_g": f(np.asarray(norm_g).reshape(8, D)),
        "ffn_w_in": f(ffn_w_in), "ffn_w_out": f(ffn_w_out),
    }
    maps = []
    for c in range(NCORES):
        b, half = c // 2, c % 2
        if half:
            xfull = x[b]
            p = np.arange(SEQ)
        else:
            xfull = np.concatenate([np.zeros((HALF, D), np.float32), x[b, :HALF]], 0)
            p = (np.arange(SEQ) + HALF) % SEQ
        m = dict(shared)
        m["xf"] = np.ascontiguousarray(xfull)
        m["pos"] = p[None, :].astype(np.float32)
        m["isodd"] = np.full((128, 1), float(half), np.float32)
        maps.append(m)
    return maps


FUSED = True


def kernel(**inputs):
    maps = make_in_maps(**inputs)
    cores = list(range(NCORES))
    if FUSED:
        res = run_bass_kernel_spmd(build_program("fused"), maps, core_ids=cores)
        outs = [r["out"] for r in res.results]
    else:
        keysB = ("isodd", "lru_w_out", "norm_g", "ffn_w_in", "ffn_w_out")
        resA = run_bass_kernel_spmd(build_program("A"), maps, core_ids=cores)
        mapsB = []
        for c in range(NCORES):
            m = {k: maps[c][k] for k in keysB}
            m["x1"] = np.asarray(resA.results[c]["x1"])
            m["pqd"] = np.asarray(resA.results[c]["pqd"])
            m["hall"] = np.asarray(resA.results[c]["hallo"])
            mapsB.append(m)
        del resA
        resB = run_bass_kernel_spmd(build_program("B"), mapsB, core_ids=cores)
        outs = [r["out"] for r in resB.results]
    B = NCORES // 2
    out = np.empty((B, SEQ, D), np.float32)
    for c in range(NCORES):
        b, half = c // 2, c % 2
        out[b, half * HALF:(half + 1) * HALF] = np.asarray(outs[c]).reshape(HALF, D)
    return out
```

```python
import contextlib
import math
import numpy as np
import concourse.bass as bass
import concourse.mybir as mybir
from concourse.bass_utils import run_bass_kernel_spmd

F32 = mybir.dt.float32
BF16 = mybir.dt.bfloat16
I32 = mybir.dt.int32
AF = mybir.ActivationFunctionType
ALU = mybir.AluOpType

ENGS = ("pe", "act", "dve", "pool", "sp")

NCORES = 8
D = 1024
C = 128
SEQ = 8192
HALF = 4096
NLOC = 64
HALO = 31
NEXT = 33
H = 4
DK = 256
DV = 512
QKW = 1024
VW = 2048
DFF = 2816
LW = 1536
EPS = 1e-6
LG = [math.log1p(-2.0 ** (-5 - h)) for h in range(H)]


class Buf:
    __slots__ = ("name", "last_w", "readers", "sem", "sem_cnt")

    def __init__(self, name=""):
        self.name = name
        self.last_w = None
        self.readers = []
        self.sem = None
        self.sem_cnt = 0

    def reset(self):
        self.last_w = None
        self.readers = []
        self.sem = None
        self.sem_cnt = 0


class Op:
    __slots__ = ("eng", "fn", "deps", "is_dma", "needs_inc", "count", "dsem", "dval", "oid")

    def __init__(self, eng, fn, is_dma):
        self.eng = eng
        self.fn = fn
        self.deps = set()
        self.is_dma = is_dma
        self.needs_inc = False
        self.count = None
        self.dsem = None
        self.dval = None


class Prog:
    uid = 0

    def __init__(self, nc):
        self.nc = nc
        self.ops = []
        self.n_dma_sems = 0
        self.seen = {}
        self.sw_sems = set()

    def _add(self, op, reads, writes):
        op.oid = len(self.ops)
        for b in reads:
            if b.last_w is not None:
                op.deps.add(b.last_w)
        for b in writes:
            cands = list(b.readers)
            if b.last_w is not None:
                cands.append(b.last_w)
            for r in cands:
                p = self.ops[r]
                if op.is_dma or p.is_dma or p.eng != op.eng:
                    op.deps.add(r)
        op.deps.discard(op.oid)
        for b in reads:
            self.seen[id(b)] = b
        for b in writes:
            self.seen[id(b)] = b
        for b in reads:
            b.readers.append(op.oid)
        for b in writes:
            b.last_w = op.oid
            b.readers = []
        self.ops.append(op)
        return op

    def op(self, eng, fn, reads=(), writes=()):
        return self._add(Op(eng, fn, False), reads, writes)

    def dma(self, eng, fn, reads=(), writes=(), key=None):
        op = Op(eng, fn, True)
        kb = key if key is not None else (writes[0] if writes else reads[0])
        if kb.sem is None:
            kb.sem = self.n_dma_sems
            self.n_dma_sems += 1
            self.seen[id(kb)] = kb
        if eng == "pool":
            self.sw_sems.add(kb.sem)
        kb.sem_cnt += 16
        op.dsem = kb.sem
        op.dval = kb.sem_cnt
        return self._add(op, reads, writes)

    def emit(self):
        nc = self.nc
        ops = self.ops
        for o in ops:
            for d in o.deps:
                ops[d].needs_inc = True
        cnt = {e: 0 for e in ENGS}
        for o in ops:
            if not o.is_dma and o.needs_inc:
                cnt[o.eng] += 1
                o.count = cnt[o.eng]
        by_eng = {e: [o for o in ops if o.eng == e] for e in ENGS}
        with contextlib.ExitStack() as st:
            Prog.uid += 1
            u = Prog.uid
            esem = {e: st.enter_context(nc.semaphore("se%d_%s" % (u, e))) for e in ENGS}
            NSW = 4
            assert len(self.sw_sems) <= NSW
            swpool = [st.enter_context(nc.semaphore("sw%d_%d" % (u, i))) for i in range(NSW)]
            dsem = [None] * self.n_dma_sems
            for j, i in enumerate(sorted(self.sw_sems)):
                dsem[i] = swpool[j]
            for i in range(self.n_dma_sems):
                if dsem[i] is None:
                    dsem[i] = st.enter_context(nc.semaphore("sd%d_%d" % (u, i)))
            with nc.Block() as b0:
                @b0.gpsimd
                def _(g):
                    for sm in list(esem.values()) + swpool + [d for d in dsem if d not in swpool]:
                        g.sem_clear(sm)
            block = st.enter_context(nc.Block())

            def body(ename, eng):
                known = {}
                for o in by_eng[ename]:
                    need = {}
                    for d in o.deps:
                        p = ops[d]
                        if p.is_dma:
                            k = ("d", p.dsem)
                            v = p.dval
                        else:
                            k = ("e", p.eng)
                            v = p.count
                        if need.get(k, 0) < v:
                            need[k] = v
                    for k, v in need.items():
                        if known.get(k, 0) >= v:
                            continue
                        known[k] = v
                        s = dsem[k[1]] if k[0] == "d" else esem[k[1]]
                        eng.wait_ge(s, v)
                    ins = o.fn(eng)
                    if o.is_dma:
                        ins.then_inc(dsem[o.dsem], 16)
                    elif o.needs_inc:
                        ins.then_inc(esem[ename], 1)
                last = {}
                for o in by_eng[ename]:
                    if o.is_dma:
                        last[o.dsem] = max(last.get(o.dsem, 0), o.dval)
                for s, v in last.items():
                    if known.get(("d", s), 0) < v:
                        eng.wait_ge(dsem[s], v)

            @block.tensor
            def _(e):
                body("pe", e)

            @block.scalar
            def _(e):
                body("act", e)

            @block.vector
            def _(e):
                body("dve", e)

            @block.gpsimd
            def _(e):
                body("pool", e)

            @block.sync
            def _(e):
                body("sp", e)
        for b in self.seen.values():
            b.reset()


class Tl:
    __slots__ = ("t", "b")

    def __init__(self, t, name):
        self.t = t
        self.b = Buf(name)


class Ctx:
    gn = 0

    def __init__(self, nc, st):
        self.nc = nc
        self.st = st
        self.n = 0

    def sb(self, name, shape, dt):
        Ctx.gn += 1
        nm = "%s_%d" % (name, Ctx.gn)
        return Tl(self.st.enter_context(self.nc.sbuf_tensor(nm, shape, dt)), nm)

    def ps(self, name, shape, dt):
        Ctx.gn += 1
        nm = "%s_%d" % (name, Ctx.gn)
        return Tl(self.st.enter_context(self.nc.psum_tensor(nm, shape, dt)), nm)


def run_pipeline(n, stages, lo=0):
    ns = len(stages)
    for it in range(lo, lo + n + ns - 1):
        for k in reversed(range(ns)):
            c = it - k
            if lo <= c < lo + n:
                stages[k](c)


def alloc_weight(nc, st, kchunks, ncols, name):
    Ctx.gn += 1
    return st.enter_context(nc.sbuf_tensor("%s_%d" % (name, Ctx.gn), [128, kchunks, ncols], BF16))


def issue_weight_load(P, wt, w_ap, kchunks, ncols, name, col0=0):
    bufs = [Buf("%s_k%d" % (name, k)) for k in range(kchunks)]
    keyb = Buf(name + "_sem")
    for k in range(kchunks):
        P.dma("pool", (lambda e, k=k: e.dma_start(out=wt[:, k, :],
                                                  in_=w_ap[k * 128:(k + 1) * 128, col0:col0 + ncols])),
              writes=[bufs[k]], key=keyb)
    return bufs


def load_weight(P, cx, w_ap, kchunks, ncols, name, col0=0):
    wt = alloc_weight(cx.nc, cx.st, kchunks, ncols, name)
    return wt, issue_weight_load(P, wt, w_ap, kchunks, ncols, name, col0)


def make_ident(P, cx):
    nc = cx.nc
    identf = cx.sb("identf", [128, 128], F32)
    ident = cx.sb("ident", [128, 128], BF16)

    P.op("pool", lambda e: e.memset(identf.t[:], 0.0), writes=[identf.b])
    P.op("pool", lambda e: e.affine_select(out=identf.t[:], in_=identf.t[:], pattern=[[-1, 128]],
                                           compare_op=ALU.not_equal, fill=1.0, base=0, channel_multiplier=1),
         reads=[identf.b], writes=[identf.b])
    P.op("pool", lambda e: e.tensor_copy(out=ident.t[:], in_=identf.t[:]), reads=[identf.b], writes=[ident.b])
    return ident


def emit_rstd(P, ss, rstd, cnst, width):
    P.op("dve", lambda e: e.tensor_scalar(out=rstd.t[:], in0=ss.t[:], scalar1=1.0 / width, scalar2=EPS,
                                          op0=ALU.mult, op1=ALU.add), reads=[ss.b], writes=[rstd.b])
    n = rstd.t.shape[1]
    P.op("pool", lambda e: e.tensor_tensor(out=rstd.t[:], in0=rstd.t[:], in1=cnst.t[:, 0:n], op=ALU.pow),
         reads=[rstd.b, cnst.b], writes=[rstd.b])


TWO_PI = 2.0 * math.pi
CW1 = 6.28125
CW2 = 0.0019350051879882812
CW3 = TWO_PI - CW1 - CW2


def phase_tables(nc, pos, invf, tabs, prefetch=None):
    TW = 512
    with contextlib.ExitStack() as st:
        cx = Ctx(nc, st)
        P = Prog(nc)
        posb = cx.sb("posb", [128, SEQ], F32)
        inv = cx.sb("inv", [128, 1], F32)
        P.dma("sp", lambda e: e.dma_start(out=posb.t[:], in_=pos.partition_broadcast(128)), writes=[posb.b])
        P.dma("sp", lambda e: e.dma_start(out=inv.t[:], in_=invf[:, :]), writes=[inv.b])
        if prefetch is not None:
            prefetch(P)
        NS = 4
        ang = [cx.sb("ang", [128, TW], F32) for _ in range(NS)]
        ki = [cx.sb("ki", [128, TW], I32) for _ in range(NS)]
        kf = [cx.sb("kf", [128, TW], F32) for _ in range(NS)]
        rr = [cx.sb("rr", [128, TW], F32) for _ in range(NS)]
        rs = [cx.sb("rs", [128, TW], F32) for _ in range(NS)]
        rc = [cx.sb("rc", [128, TW], F32) for _ in range(NS)]
        cs = [cx.sb("cs", [128, 2, TW], F32) for _ in range(NS)]
        ct = [cx.sb("ct", [128, 2, 128], F32) for _ in range(4)]
        pT = [cx.ps("pT", [128, 2, 128], F32) for _ in range(4)]
        identf = cx.sb("identf", [128, 128], F32)
        P.op("pool", lambda e: e.memset(identf.t[:], 0.0), writes=[identf.b])
        P.op("pool", lambda e: e.affine_select(out=identf.t[:], in_=identf.t[:], pattern=[[-1, 128]],
                                               compare_op=ALU.not_equal, fill=1.0, base=0, channel_multiplier=1),
             reads=[identf.b], writes=[identf.b])
        tabB = Buf("tabs")
        for i in range(SEQ // TW):
            s = i % NS
            sl = slice(i * TW, (i + 1) * TW)
            P.op("dve", lambda e, s=s, sl=sl: e.tensor_scalar(out=ang[s].t[:], in0=posb.t[:, sl], scalar1=inv.t[:, 0:1],
                                                              scalar2=None, op0=ALU.mult),
                 reads=[posb.b, inv.b], writes=[ang[s].b])
            P.op("dve", lambda e, s=s: e.tensor_scalar(out=ki[s].t[:], in0=ang[s].t[:], scalar1=1.0 / TWO_PI,
                                                       scalar2=None, op0=ALU.mult),
                 reads=[ang[s].b], writes=[ki[s].b])
            P.op("pool", lambda e, s=s: e.tensor_copy(out=kf[s].t[:], in_=ki[s].t[:]), reads=[ki[s].b], writes=[kf[s].b])
            P.op("dve", lambda e, s=s: e.scalar_tensor_tensor(out=rr[s].t[:], in0=kf[s].t[:], scalar=-CW1, in1=ang[s].t[:], op0=ALU.mult, op1=ALU.add),
                 reads=[ang[s].b, kf[s].b], writes=[rr[s].b])
            P.op("dve", lambda e, s=s: e.scalar_tensor_tensor(out=rs[s].t[:], in0=kf[s].t[:], scalar=-CW2, in1=rr[s].t[:], op0=ALU.mult, op1=ALU.add),
                 reads=[rr[s].b, kf[s].b], writes=[rs[s].b])
            P.op("dve", lambda e, s=s: e.scalar_tensor_tensor(out=rr[s].t[:], in0=kf[s].t[:], scalar=-CW3, in1=rs[s].t[:], op0=ALU.mult, op1=ALU.add),
                 reads=[rs[s].b, kf[s].b], writes=[rr[s].b])
            P.op("dve", lambda e, s=s: e.tensor_scalar(out=ang[s].t[:], in0=rr[s].t[:], scalar1=math.pi, scalar2=-TWO_PI,
                                                       op0=ALU.is_gt, op1=ALU.mult), reads=[rr[s].b], writes=[ang[s].b])
            P.op("dve", lambda e, s=s: e.tensor_tensor(out=rs[s].t[:], in0=rr[s].t[:], in1=ang[s].t[:], op=ALU.add),
                 reads=[rr[s].b, ang[s].b], writes=[rs[s].b])
            P.op("dve", lambda e, s=s: e.tensor_scalar(out=ang[s].t[:], in0=rr[s].t[:], scalar1=math.pi / 2, scalar2=-TWO_PI,
                                                       op0=ALU.is_gt, op1=ALU.mult), reads=[rr[s].b, rs[s].b], writes=[ang[s].b])
            P.op("dve", lambda e, s=s: e.scalar_tensor_tensor(out=rc[s].t[:], in0=rr[s].t[:], scalar=math.pi / 2, in1=ang[s].t[:],
                                                              op0=ALU.add, op1=ALU.add),
                 reads=[rr[s].b, ang[s].b], writes=[rc[s].b])
            P.op("act", lambda e, s=s: e.activation(out=cs[s].t[:, 0, :], in_=rc[s].t[:], func=AF.Sin), reads=[rc[s].b], writes=[cs[s].b])
            P.op("act", lambda e, s=s: e.activation(out=cs[s].t[:, 1, :], in_=rs[s].t[:], func=AF.Sin), reads=[rs[s].b], writes=[cs[s].b])
            for tbk in range(TW // 128):
                pt_ = pT[(i * (TW // 128) + tbk) % 4]
                ct_ = ct[(i * (TW // 128) + tbk) % 4]

                def trc(e, s=s, tbk=tbk, pt_=pt_):
                    e.transpose(out=pt_.t[:, 0, :], in_=cs[s].t[:, 0, tbk * 128:(tbk + 1) * 128], identity=identf.t[:])
                    return e.transpose(out=pt_.t[:, 1, :], in_=cs[s].t[:, 1, tbk * 128:(tbk + 1) * 128], identity=identf.t[:])
                P.op("pe", trc, reads=[cs[s].b, identf.b], writes=[pt_.b])
                P.op("act", lambda e, pt_=pt_, ct_=ct_: e.copy(out=ct_.t[:], in_=pt_.t[:]), reads=[pt_.b], writes=[ct_.b])
                t0 = i * TW + tbk * 128
                P.dma("sp", lambda e, t0=t0, ct_=ct_: e.dma_start(out=tabs[t0:t0 + 128, :, :], in_=ct_.t[:]), reads=[ct_.b], writes=[tabB], key=ct_.b)
        P.emit()


def phase_ret(nc, xf, tabs, w_in, g0, ogd, c_begin=0, c_end=NLOC, dbg=None, pre_w=None):
    with contextlib.ExitStack() as st:
        cx = Ctx(nc, st)
        P = Prog(nc)
        if pre_w is not None:
            wt, wb = pre_w, []
        else:
            wt, wb = load_weight(P, cx, w_in, 8, 2 * QKW + 2 * VW, "retwin")
        ident = make_ident(P, cx)
        g0b = cx.sb("g0b", [128, D], F32)
        P.dma("sp", lambda e: e.dma_start(out=g0b.t[:], in_=g0.partition_broadcast(128)), writes=[g0b.b])
        cm05 = cx.sb("cm05", [128, 8], F32)
        P.op("pool", lambda e: e.memset(cm05.t[:], -0.5), writes=[cm05.b])
        dif_i = cx.sb("dif_i", [128, 128], I32)
        dif = cx.sb("dif", [128, 128], F32)
        dpos = cx.sb("dpos", [128, 128], F32)
        dge = cx.sb("dge", [128, 128], F32)
        maskT = cx.sb("maskT", [128, H, 128], F32)
        qdec = cx.sb("qdec", [128, H, 128], F32)
        kdec = cx.sb("kdec", [128, H], F32)
        qi_i = cx.sb("qi_i", [128, 128], I32)
        qi = cx.sb("qi", [128, 128], F32)
        kk_i = cx.sb("kk_i", [128, 1], I32)
        kk = cx.sb("kk", [128, 1], F32)
        P.op("pool", lambda e: e.iota(dif_i.t[:], pattern=[[1, 128]], base=0, channel_multiplier=-1), writes=[dif_i.b])
        P.op("pool", lambda e: e.tensor_copy(out=dif.t[:], in_=dif_i.t[:]), reads=[dif_i.b], writes=[dif.b])
        P.op("pool", lambda e: e.iota(qi_i.t[:], pattern=[[1, 128]], base=1, channel_multiplier=0), writes=[qi_i.b])
        P.op("pool", lambda e: e.tensor_copy(out=qi.t[:], in_=qi_i.t[:]), reads=[qi_i.b], writes=[qi.b])
        P.op("pool", lambda e: e.iota(kk_i.t[:], pattern=[[0, 1]], base=127, channel_multiplier=-1), writes=[kk_i.b])
        P.op("pool", lambda e: e.tensor_copy(out=kk.t[:], in_=kk_i.t[:]), reads=[kk_i.b], writes=[kk.b])
        P.op("dve", lambda e: e.tensor_scalar(out=dpos.t[:], in0=dif.t[:], scalar1=0.0, scalar2=None, op0=ALU.max),
             reads=[dif.b], writes=[dpos.b])
        P.op("dve", lambda e: e.tensor_scalar(out=dge.t[:], in0=dif.t[:], scalar1=0.0, scalar2=1.0 / 16.0, op0=ALU.is_ge, op1=ALU.mult),
             reads=[dif.b], writes=[dge.b])
        for h in range(H):
            P.op("act", lambda e, h=h: e.activation(out=maskT.t[:, h, :], in_=dpos.t[:], func=AF.Exp, scale=LG[h]),
                 reads=[dpos.b], writes=[maskT.b])
            P.op("act", lambda e, h=h: e.activation(out=qdec.t[:, h, :], in_=qi.t[:], func=AF.Exp, scale=LG[h]),
                 reads=[qi.b], writes=[qdec.b])
            P.op("act", lambda e, h=h: e.activation(out=kdec.t[:, h:h + 1], in_=kk.t[:], func=AF.Exp, scale=LG[h]),
                 reads=[kk.b], writes=[kdec.b])
        P.op("dve", lambda e: e.tensor_tensor(out=maskT.t[:], in0=maskT.t[:], in1=dge.t[:].unsqueeze(1).broadcast_to([128, H, 128]), op=ALU.mult),
             reads=[maskT.b, dge.b], writes=[maskT.b])
        P.op("dve", lambda e: e.tensor_scalar(out=kdec.t[:], in0=kdec.t[:], scalar1=1.0 / 16.0, scalar2=None, op0=ALU.mult),
             reads=[kdec.b], writes=[kdec.b])
        S = cx.sb("S", [128, 2 * H, DV], F32)
        Sb = cx.sb("Sb", [128, 2 * H, DV], BF16)
        Sbufs = [Buf("S%d" % i) for i in range(2 * H)]
        P.op("pool", lambda e: e.memset(S.t[:], 0.0), writes=Sbufs)
        P.op("pool", lambda e: e.memset(Sb.t[:], 0.0), writes=[Sb.b])
        NS = 2
        NTB = 4
        xt = [cx.sb("xt", [128, D], F32) for _ in range(NS)]
        tb = [cx.sb("tb", [128, 2, 128], F32) for _ in range(NTB)]
        junk = None
        ss = [cx.sb("ss", [128, 1], F32) for _ in range(2)]
        rstd = [cx.sb("rstd", [128, 1], F32) for _ in range(2)]
        hb = cx.sb("hb", [128, D], BF16)
        hT = [cx.sb("hT", [128, 8, 128], BF16) for _ in range(2)]
        t1 = cx.sb("t1", [128, 2, 128], F32)
        t2 = cx.sb("t2", [128, 2, 128], F32)
        t3 = cx.sb("t3", [128, 2, 128], F32)
        t4 = cx.sb("t4", [128, 2, 128], F32)
        ktm = [cx.sb("ktm", [128, H, 2, 128], BF16) for _ in range(1)]
        qtm = cx.sb("qtm", [128, H, 2, 128], BF16)
        qT = [cx.sb("qT", [128, H, 2, 128], BF16) for _ in range(2)]
        kT = [cx.sb("kT", [128, H, 2, 128], BF16) for _ in range(2)]
        qd = [cx.sb("qd", [128, H, 2, 128], BF16) for _ in range(2)]
        kd = [cx.sb("kd", [128, H, 2 * 128], BF16) for _ in range(2)]
        vb = [cx.sb("vb", [128, VW], BF16) for _ in range(2)]
        sg = cx.sb("sg", [128, VW], F32)
        sT = cx.sb("sT", [128, H, 128], BF16)
        stats = cx.sb("stats", [128, H, 6], F32)
        mv = cx.sb("mv", [128, H, 2], F32)
        var = cx.sb("var", [128, H], F32)
        rs = cx.sb("rs", [128, H], F32)
        nb = cx.sb("nb", [128, H], F32)
        on = [cx.sb("on", [128, DV], F32) for _ in range(2)]
        og = [cx.sb("og", [128, VW], BF16) for _ in range(NS)]
        p_tr = cx.ps("p_tr", [128, 8, 128], BF16)
        p_qa = cx.ps("p_qa", [128, H, 128], F32)
        p_qb = cx.ps("p_qb", [128, H, 128], F32)
        p_ka = cx.ps("p_ka", [128, H, 128], F32)
        p_kb = cx.ps("p_kb", [128, H, 128], F32)
        p_w = [cx.ps("p_w", [128, 512], F32) for _ in range(2)]
        p_sc = cx.ps("p_sc", [128, H, 128], F32)
        p_o = [p_qa, p_qb, p_ka, p_kb]
        ogB = Buf("ogd")
        cosb = lambda tbt: tbt.t[:, 0, :].unsqueeze(1).broadcast_to([128, H, 128])
        sinb = lambda tbt: tbt.t[:, 1, :].unsqueeze(1).broadcast_to([128, H, 128])

        def stL(c):
            tok = slice(c * C, (c + 1) * C)
            x_ = xt[c % NS]
            tb_ = tb[c % NTB]
            P.dma("act", lambda e: e.dma_start(out=x_.t[:], in_=xf[tok, :]), writes=[x_.b])
            P.dma("act", lambda e: e.dma_start(out=tb_.t[:], in_=tabs[tok, :, :]), writes=[tb_.b])

        def stA(c):
            x_ = xt[c % NS]
            s2 = c % 2
            P.op("act", lambda e: e.activation(out=hb.t[:], in_=x_.t[:], func=AF.Square, accum_out=ss[s2].t[:]),
                 reads=[x_.b], writes=[hb.b, ss[s2].b])
            emit_rstd(P, ss[s2], rstd[s2], cm05, D)
            P.op("dve", lambda e: e.scalar_tensor_tensor(out=hb.t[:], in0=x_.t[:], scalar=rstd[s2].t[:, 0:1], in1=g0b.t[:],
                                                         op0=ALU.mult, op1=ALU.mult),
                 reads=[x_.b, rstd[s2].b, g0b.b], writes=[hb.b])

        def stB_g(c, full, hT_):
            if full:
                for n in range(4):
                    pw = p_w[n % 2]

                    def mm_g(e, n=n, pw=pw):
                        for k in range(8):
                            ins = e.matmul(out=pw.t[:], lhsT=hT_.t[:, k, :],
                                           rhs=wt[:, k, 2 * QKW + VW + n * 512:2 * QKW + VW + (n + 1) * 512],
                                           start=(k == 0), stop=(k == 7))
                        return ins
                    P.op("pe", mm_g, reads=[hT_.b] + wb, writes=[pw.b])
                    P.op("act", lambda e, n=n, pw=pw: e.activation(out=sg.t[:, n * 512:(n + 1) * 512], in_=pw.t[:], func=AF.Silu),
                         reads=[pw.b], writes=[sg.b])


        def stB_t(c, full, ktm_, qtm_, kT_, qT_, qd_):
            if full:
                def tr_q(e):
                    for h in range(H):
                        for half in range(2):
                            ins = e.transpose(out=p_tr.t[:, h * 2 + half, :], in_=qtm_.t[:, h, half, :], identity=ident.t[:])
                    return ins
                P.op("pe", tr_q, reads=[qtm_.b, ident.b], writes=[p_tr.b])
                P.op("act", lambda e: e.copy(out=qT_.t[:].rearrange("p h a t -> p (h a) t"), in_=p_tr.t[:]), reads=[p_tr.b], writes=[qT_.b])
                P.op("pool", lambda e: e.tensor_tensor(out=qd_.t[:], in0=qT_.t[:],
                                                       in1=qdec.t[:].unsqueeze(2).broadcast_to([128, H, 2, 128]), op=ALU.mult),
                     reads=[qT_.b, qdec.b], writes=[qd_.b])

                def tr_k(e):
                    for h in range(H):
                        for half in range(2):
                            ins = e.transpose(out=p_tr.t[:, h * 2 + half, :], in_=ktm_.t[:, h, half, :], identity=ident.t[:])
                    return ins
                P.op("pe", tr_k, reads=[ktm_.b, ident.b], writes=[p_tr.b])
                P.op("dve", lambda e: e.tensor_copy(out=kT_.t[:].rearrange("p h a t -> p (h a) t"), in_=p_tr.t[:]), reads=[p_tr.b], writes=[kT_.b])


        def stA2(c):
            s2 = c % 2

            def tr_h(e):
                for k in range(8):
                    ins = e.transpose(out=p_tr.t[:, k, :], in_=hb.t[:, k * 128:(k + 1) * 128], identity=ident.t[:])
                return ins
            P.op("pe", tr_h, reads=[hb.b, ident.b], writes=[p_tr.b])
            P.op("act", lambda e: e.copy(out=hT[s2].t[:], in_=p_tr.t[:]), reads=[p_tr.b], writes=[hT[s2].b])

        def stB(c, part):
            full = c >= HALO
            s2 = c % 2
            tb_ = tb[c % NTB]
            hT_ = hT[s2]
            kT_, qT_, qd_, kd_, vb_ = kT[s2], qT[s2], qd[s2], kd[s2], vb[s2]

            cos2 = tb_.t[:, 0, :].unsqueeze(1).broadcast_to([128, 2, 128])
            sin2 = tb_.t[:, 1, :].unsqueeze(1).broadcast_to([128, 2, 128])

            def proj_tm(n, pp):
                def f(e):
                    for k in range(8):
                        ins = e.matmul(out=pp.t[:].rearrange("p a b -> p (a b)"), lhsT=hT_.t[:, k, :], rhs=wt[:, k, n * 512:(n + 1) * 512],
                                       start=(k == 0), stop=(k == 7))
                    return ins
                return f

            def rope_tm(pp, dst, hh0):
                pv = pp.t[:].rearrange("p a b -> p (a b)").rearrange("p (h a j) -> p h a j", h=2, a=2)
                A = pv[:, :, 0, :]
                B = pv[:, :, 1, :]
                P.op("dve", lambda e: e.tensor_tensor(out=t1.t[:, 0:2, :], in0=A, in1=cos2, op=ALU.mult), reads=[pp.b, tb_.b], writes=[t1.b])
                P.op("dve", lambda e: e.tensor_tensor(out=t2.t[:, 0:2, :], in0=B, in1=sin2, op=ALU.mult), reads=[pp.b, tb_.b], writes=[t2.b])
                P.op("pool", lambda e: e.tensor_tensor(out=dst.t[:, hh0:hh0 + 2, 0, :], in0=t1.t[:, 0:2, :], in1=t2.t[:, 0:2, :], op=ALU.subtract),
                     reads=[t1.b, t2.b], writes=[dst.b])
                P.op("dve", lambda e: e.tensor_tensor(out=t3.t[:, 0:2, :], in0=A, in1=sin2, op=ALU.mult), reads=[pp.b, tb_.b], writes=[t3.b])
                P.op("dve", lambda e: e.tensor_tensor(out=t4.t[:, 0:2, :], in0=B, in1=cos2, op=ALU.mult), reads=[pp.b, tb_.b], writes=[t4.b])
                P.op("pool", lambda e: e.tensor_tensor(out=dst.t[:, hh0:hh0 + 2, 1, :], in0=t3.t[:, 0:2, :], in1=t4.t[:, 0:2, :], op=ALU.add),
                     reads=[t3.b, t4.b], writes=[dst.b])

            ktm_, qtm_ = ktm[0], qtm
            if part == "g":
                stB_g(c, full, hT_)
                return
            if part == "t":
                stB_t(c, full, ktm_, qtm_, kT_, qT_, qd_)
                return
            P.op("pe", proj_tm(2, p_ka), reads=[hT_.b] + wb, writes=[p_ka.b])
            P.op("pe", proj_tm(3, p_kb), reads=[hT_.b] + wb, writes=[p_kb.b])
            if full:
                P.op("pe", proj_tm(0, p_qa), reads=[hT_.b] + wb, writes=[p_qa.b])
                P.op("pe", proj_tm(1, p_qb), reads=[hT_.b] + wb, writes=[p_qb.b])
            rope_tm(p_ka, ktm_, 0)
            rope_tm(p_kb, ktm_, 2)
            P.op("dve", lambda e: e.tensor_tensor(out=kd_.t[:], in0=ktm_.t[:].rearrange("p h a j -> p h (a j)"),
                                                  in1=kdec.t[:].unsqueeze(2).broadcast_to([128, H, 256]), op=ALU.mult),
                 reads=[ktm_.b, kdec.b], writes=[kd_.b])
            if full:
                rope_tm(p_qa, qtm_, 0)
                rope_tm(p_qb, qtm_, 2)
            for n in range(4):
                pw = p_w[n % 2]

                def mm_v(e, n=n, pw=pw):
                    for k in range(8):
                        ins = e.matmul(out=pw.t[:], lhsT=hT_.t[:, k, :], rhs=wt[:, k, 2 * QKW + n * 512:2 * QKW + (n + 1) * 512],
                                       start=(k == 0), stop=(k == 7))
                    return ins
                P.op("pe", mm_v, reads=[hT_.b] + wb, writes=[pw.b])
                P.op("act", lambda e, n=n, pw=pw: e.copy(out=vb_.t[:, n * 512:(n + 1) * 512], in_=pw.t[:]), reads=[pw.b], writes=[vb_.b])
        def stC(c):
            full = c >= HALO
            s2 = c % 2
            s = c % NS
            kT_, qT_, qd_, kd_, vb_ = kT[s2], qT[s2], qd[s2], kd[s2], vb[s2]
            if full:
                def mm_sc(e):
                    for h in range(H):
                        for half in range(2):
                            ins = e.matmul(out=p_sc.t[:, h, :], lhsT=kT_.t[:, h, half, :], rhs=qT_.t[:, h, half, :],
                                           start=(half == 0), stop=(half == 1))
                    return ins
                P.op("pe", mm_sc, reads=[kT_.b, qT_.b], writes=[p_sc.b])
                P.op("dve", lambda e: e.tensor_tensor(out=sT.t[:], in0=p_sc.t[:], in1=maskT.t[:], op=ALU.mult),
                     reads=[p_sc.b, maskT.b], writes=[sT.b])
                for h in range(H):
                    def mm_o(e, h=h):
                        e.matmul(out=p_o[h].t[:].rearrange("p a b -> p (a b)"), lhsT=sT.t[:, h, :], rhs=vb_.t[:, h * DV:(h + 1) * DV],
                                 start=True, stop=False)
                        e.matmul(out=p_o[h].t[:].rearrange("p a b -> p (a b)"), lhsT=qd_.t[:, h, 0, :], rhs=Sb.t[:, 2 * h, :],
                                 start=False, stop=False)
                        return e.matmul(out=p_o[h].t[:].rearrange("p a b -> p (a b)"), lhsT=qd_.t[:, h, 1, :], rhs=Sb.t[:, 2 * h + 1, :],
                                        start=False, stop=True)
                    P.op("pe", mm_o, reads=[sT.b, vb_.b, qd_.b, Sb.b], writes=[p_o[h].b])
                    P.op("dve", lambda e, h=h: e.bn_stats(out=stats.t[:, h, :], in_=p_o[h].t[:].rearrange("p a b -> p (a b)")),
                         reads=[p_o[h].b], writes=[stats.b])
            if full:
                P.op("dve", lambda e: [e.bn_aggr(out=mv.t[:, h, :], in_=stats.t[:, h, :]) for h in range(H)][-1],
                     reads=[stats.b], writes=[mv.b])
                P.op("dve", lambda e: e.tensor_scalar(out=var.t[:], in0=mv.t[:, :, 1], scalar1=EPS, scalar2=None, op0=ALU.add),
                     reads=[mv.b], writes=[var.b])
                P.op("pool", lambda e: e.tensor_tensor(out=rs.t[:], in0=var.t[:], in1=cm05.t[:, 0:H], op=ALU.pow),
                     reads=[var.b, cm05.b], writes=[rs.b])
                P.op("dve", lambda e: e.scalar_tensor_tensor(out=nb.t[:], in0=mv.t[:, :, 0], scalar=-1.0, in1=rs.t[:],
                                                             op0=ALU.mult, op1=ALU.mult),
                     reads=[mv.b, rs.b], writes=[nb.b])
                for h in range(H):
                    o_n = on[h % 2]
                    P.op("act", lambda e, h=h, o_n=o_n: e.activation(out=o_n.t[:], in_=p_o[h].t[:].rearrange("p a b -> p (a b)"),
                                                                     func=AF.Identity, scale=rs.t[:, h:h + 1], bias=nb.t[:, h:h + 1]),
                         reads=[p_o[h].b, rs.b, nb.b], writes=[o_n.b])
                    P.op("pool", lambda e, h=h, o_n=o_n: e.tensor_tensor(out=og[s].t[:, h * DV:(h + 1) * DV], in0=o_n.t[:],
                                                                         in1=sg.t[:, h * DV:(h + 1) * DV], op=ALU.mult),
                         reads=[o_n.b, sg.b], writes=[og[s].b])
                P.dma("sp", lambda e: e.dma_start(out=ogd[c - HALO, :, :], in_=og[s].t[:]), reads=[og[s].b], writes=[ogB], key=og[s].b)

        def stC2(c):
            s2 = c % 2
            kd_, vb_ = kd[s2], vb[s2]
            for h in range(H):
                for half in range(2):
                    i = 2 * h + half
                    pw = p_w[i % 2]

                    def mm_s(e, h=h, half=half, pw=pw):
                        return e.matmul(out=pw.t[:], lhsT=kd_.t[:, h, half * 128:(half + 1) * 128], rhs=vb_.t[:, h * DV:(h + 1) * DV],
                                        start=True, stop=True)
                    P.op("pe", mm_s, reads=[kd_.b, vb_.b], writes=[pw.b])
                    P.op("dve", lambda e, i=i, h=h, pw=pw: e.scalar_tensor_tensor(out=S.t[:, i, :], in0=S.t[:, i, :],
                                                                                  scalar=math.exp(C * LG[h]), in1=pw.t[:],
                                                                                  op0=ALU.mult, op1=ALU.add),
                         reads=[Sbufs[i], pw.b], writes=[Sbufs[i]])
            P.op("act", lambda e: e.copy(out=Sb.t[:], in_=S.t[:]), reads=Sbufs, writes=[Sb.b])

        n_ = c_end - c_begin
        lo_ = c_begin
        ok_ = lambda c: lo_ <= c < lo_ + n_
        for it in range(lo_, lo_ + n_ + 3):
            if ok_(it - 1):
                stA(it - 1)
            if ok_(it - 2):
                stB(it - 2, "kqv")
            if ok_(it - 1):
                stA2(it - 1)
            if ok_(it - 3):
                stC(it - 3)
            if ok_(it - 2):
                stB(it - 2, "t")
            if ok_(it - 3):
                stC2(it - 3)
            if ok_(it - 2):
                stB(it - 2, "g")
            if ok_(it):
                stL(it)
        qT, kT = qT[(c_end - 1) % 2], kT[(c_end - 1) % 2]
        if dbg is not None:
            dB = Buf("dbg")
            P.dma("sp", lambda e: e.dma_start(out=dbg["qT"][:, :], in_=qT.t[:].rearrange("p h a t -> p (h a t)")), reads=[qT.b], writes=[dB], key=qT.b)
            P.dma("sp", lambda e: e.dma_start(out=dbg["kT"][:, :], in_=kT.t[:].rearrange("p h a t -> p (h a t)")), reads=[kT.b], writes=[dB], key=kT.b)
            P.dma("sp", lambda e: e.dma_start(out=dbg["S"][:, :], in_=S.t[:].rearrange("p i e -> p (i e)")), reads=Sbufs, writes=[dB], key=S.b)
        P.emit()


def emit_post_norm_residual(P, pm, xres, gb, outt, junk, ss, rstd, cm05, tmp):
    def sq(e):
        e.activation(out=tmp.t[:, 0:512], in_=pm[0].t[:], func=AF.Square, accum_out=ss.t[:, 0:1])
        return e.activation(out=tmp.t[:, 512:1024], in_=pm[1].t[:], func=AF.Square, accum_out=ss.t[:, 1:2])
    P.op("act", sq, reads=[pm[0].b, pm[1].b], writes=[tmp.b, ss.b])
    P.op("dve", lambda e: e.tensor_tensor(out=ss.t[:, 2:3], in0=ss.t[:, 0:1], in1=ss.t[:, 1:2], op=ALU.add),
         reads=[ss.b], writes=[ss.b])
    P.op("dve", lambda e: e.tensor_scalar(out=rstd.t[:], in0=ss.t[:, 2:3], scalar1=1.0 / D, scalar2=EPS,
                                          op0=ALU.mult, op1=ALU.add), reads=[ss.b], writes=[rstd.b])
    P.op("pool", lambda e: e.tensor_tensor(out=rstd.t[:], in0=rstd.t[:], in1=cm05.t[:, 0:1], op=ALU.pow),
         reads=[rstd.b, cm05.b], writes=[rstd.b])
    for n in range(2):
        P.op("dve", lambda e, n=n: e.scalar_tensor_tensor(out=tmp.t[:, n * 512:(n + 1) * 512], in0=pm[n].t[:], scalar=rstd.t[:, 0:1],
                                                          in1=gb.t[:, n * 512:(n + 1) * 512], op0=ALU.mult, op1=ALU.mult),
             reads=[pm[n].b, rstd.b, gb.b], writes=[tmp.b])
    P.op("pool", lambda e: e.tensor_tensor(out=outt.t[:], in0=tmp.t[:], in1=xres.t[:], op=ALU.add),
         reads=[tmp.b, xres.b], writes=[outt.b])


def phase_ret_out(nc, xf, ogd, w_out, g1, xmid, prefetch=None):
    with contextlib.ExitStack() as st:
        cx = Ctx(nc, st)
        P = Prog(nc)
        wt, wb = load_weight(P, cx, w_out, 16, D, "retwout")
        if prefetch is not None:
            prefetch(P)
        ident = make_ident(P, cx)
        g1b = cx.sb("g1b", [128, D], F32)
        P.dma("sp", lambda e: e.dma_start(out=g1b.t[:], in_=g1.partition_broadcast(128)), writes=[g1b.b])
        cm05 = cx.sb("cm05", [128, 8], F32)
        P.op("pool", lambda e: e.memset(cm05.t[:], -0.5), writes=[cm05.b])
        NX = 5
        xt = [cx.sb("xt", [128, D], F32) for _ in range(NX)]
        ogt = [cx.sb("ogt", [128, VW], BF16) for _ in range(4)]
        ogT = [cx.sb("ogT", [128, 16, 128], BF16) for _ in range(3)]
        junk = None
        ss = [cx.sb("ss", [128, 4], F32) for _ in range(2)]
        rstd = [cx.sb("rstd", [128, 1], F32) for _ in range(2)]
        tmp = [cx.sb("tmp", [128, D], F32) for _ in range(2)]
        xo = [cx.sb("xo", [128, D], F32) for _ in range(2)]
        p_tr = [cx.ps("p_tr", [128, 8, 128], BF16) for _ in range(2)]
        p_m = [[cx.ps("p_m", [128, 512], F32) for _ in range(2)] for _ in range(2)]
        xmB = Buf("xmid")

        def stL(c):
            tok = slice((HALO + c) * C, (HALO + c + 1) * C)
            x_ = xt[c % NX]
            o_ = ogt[c % 4]
            P.dma("act", lambda e: e.dma_start(out=x_.t[:], in_=xf[tok, :]), writes=[x_.b])
            P.dma("act", lambda e: e.dma_start(out=o_.t[:], in_=ogd[c, :, :]), writes=[o_.b])

        def stA(c):
            o_ = ogt[c % 4]
            oT = ogT[c % 3]
            for half in range(2):
                def tr(e, half=half):
                    for k in range(8):
                        kk = half * 8 + k
                        ins = e.transpose(out=p_tr[half].t[:, k, :], in_=o_.t[:, kk * 128:(kk + 1) * 128], identity=ident.t[:])
                    return ins
                P.op("pe", tr, reads=[o_.b, ident.b], writes=[p_tr[half].b])
                if half == 0:
                    P.op("act", lambda e: e.copy(out=oT.t[:, 0:8, :], in_=p_tr[0].t[:]), reads=[p_tr[0].b], writes=[oT.b])
                else:
                    P.op("dve", lambda e: e.tensor_copy(out=oT.t[:, 8:16, :], in_=p_tr[1].t[:]), reads=[p_tr[1].b], writes=[oT.b])

        def stB(c):
            s = c % 2
            oT = ogT[c % 3]
            pm = p_m[s]
            for n in range(2):
                def mm(e, n=n):
                    for k in range(16):
                        ins = e.matmul(out=pm[n].t[:], lhsT=oT.t[:, k, :], rhs=wt[:, k, n * 512:(n + 1) * 512],
                                       start=(k == 0), stop=(k == 15))
                    return ins
                P.op("pe", mm, reads=[oT.b] + wb, writes=[pm[n].b])
            emit_post_norm_residual(P, pm, xt[c % NX], g1b, xo[s], junk, ss[s], rstd[s], cm05, tmp[s])
            P.dma("sp", lambda e: e.dma_start(out=xmid[c, :, :], in_=xo[s].t[:]), reads=[xo[s].b], writes=[xmB], key=xo[s].b)

        run_pipeline(NEXT, [stL, stA, (lambda c: None), stB])
        P.emit()


def phase_ffn(nc, xin, w_in, w_out, g2, g3, xout_fn, c0, c1, pre_wi=None):
    with contextlib.ExitStack() as st:
        cx = Ctx(nc, st)
        P = Prog(nc)
        if pre_wi is not None:
            wi, wib = pre_wi, []
        else:
            wi, wib = load_weight(P, cx, w_in, 8, 2 * DFF, "ffnwin")
        wo, wob = load_weight(P, cx, w_out, 22, D, "ffnwout")
        ident = make_ident(P, cx)
        g2b = cx.sb("g2b", [128, D], F32)
        g3b = cx.sb("g3b", [128, D], F32)
        P.dma("sp", lambda e: e.dma_start(out=g2b.t[:], in_=g2.partition_broadcast(128)), writes=[g2b.b])
        P.dma("sp", lambda e: e.dma_start(out=g3b.t[:], in_=g3.partition_broadcast(128)), writes=[g3b.b])
        cm05 = cx.sb("cm05", [128, 8], F32)
        P.op("pool", lambda e: e.memset(cm05.t[:], -0.5), writes=[cm05.b])
        NX = 5
        xt = [cx.sb("xt", [128, D], F32) for _ in range(NX)]
        junk = None
        ss = cx.sb("ss", [128, 4], F32)
        ss1 = [cx.sb("ss1", [128, 1], F32) for _ in range(2)]
        rstd = cx.sb("rstd", [128, 1], F32)
        rstd2 = [cx.sb("rstd2", [128, 1], F32) for _ in range(2)]
        hb = [cx.sb("hb", [128, D], BF16) for _ in range(2)]
        hT = [cx.sb("hT", [128, 8, 128], BF16) for _ in range(2)]
        sgl = [cx.sb("sgl", [128, 512], F32) for _ in range(2)]
        ab = [cx.sb("ab", [128, DFF], BF16) for _ in range(2)]
        aT = cx.sb("aT", [128, 22, 128], BF16)
        tmp = cx.sb("tmp", [128, D], F32)
        xo = [cx.sb("xo", [128, D], F32) for _ in range(2)]
        p_tr = [cx.ps("p_tr", [128, 8, 128], BF16) for _ in range(2)]
        p_g = [cx.ps("p_g", [128, 512], F32) for _ in range(2)]
        p_u = [cx.ps("p_u", [128, 512], F32) for _ in range(2)]
        p_f = [cx.ps("p_f", [128, 512], F32) for _ in range(2)]
        xoB = Buf("xout")
        tiles = [(j * 512, 512) for j in range(5)] + [(2560, 256)]

        def stL(c):
            x_ = xt[c % NX]
            P.dma("act", lambda e: e.dma_start(out=x_.t[:], in_=xin[c, :, :]), writes=[x_.b])

        def stA(c):
            x_ = xt[c % NX]
            s = c % 2
            P.op("act", lambda e: e.activation(out=hb[s].t[:], in_=x_.t[:], func=AF.Square, accum_out=ss1[s].t[:]),
                 reads=[x_.b], writes=[hb[s].b, ss1[s].b])
            emit_rstd(P, ss1[s], rstd2[s], cm05, D)
            P.op("dve", lambda e: e.scalar_tensor_tensor(out=hb[s].t[:], in0=x_.t[:], scalar=rstd2[s].t[:, 0:1], in1=g2b.t[:],
                                                         op0=ALU.mult, op1=ALU.mult),
                 reads=[x_.b, rstd2[s].b, g2b.b], writes=[hb[s].b])

        def stA2(c):
            s = c % 2

            def tr_h(e):
                for k in range(8):
                    ins = e.transpose(out=p_tr[0].t[:, k, :], in_=hb[s].t[:, k * 128:(k + 1) * 128], identity=ident.t[:])
                return ins
            P.op("pe", tr_h, reads=[hb[s].b, ident.b], writes=[p_tr[0].b])
            P.op("act", lambda e: e.copy(out=hT[s].t[:], in_=p_tr[0].t[:]), reads=[p_tr[0].b], writes=[hT[s].b])

        def stB(c):
            s = c % 2
            for j, (c0h, wd) in enumerate(tiles):
                pg = p_g[j % 2]
                pu = p_u[j % 2]
                sl_ = sgl[j % 2]

                def mm_gu(e, c0h=c0h, wd=wd, pg=pg, pu=pu):
                    for k in range(8):
                        e.matmul(out=pg.t[:, 0:wd], lhsT=hT[s].t[:, k, :], rhs=wi[:, k, c0h:c0h + wd], start=(k == 0), stop=(k == 7))
                    for k in range(8):
                        ins = e.matmul(out=pu.t[:, 0:wd], lhsT=hT[s].t[:, k, :], rhs=wi[:, k, DFF + c0h:DFF + c0h + wd],
                                       start=(k == 0), stop=(k == 7))
                    return ins
                P.op("pe", mm_gu, reads=[hT[s].b] + wib, writes=[pg.b, pu.b])
                P.op("act", lambda e, wd=wd, pg=pg, sl_=sl_: e.activation(out=sl_.t[:, 0:wd], in_=pg.t[:, 0:wd], func=AF.Silu),
                     reads=[pg.b], writes=[sl_.b])
                P.op("dve", lambda e, c0h=c0h, wd=wd, pu=pu, sl_=sl_: e.tensor_tensor(out=ab[s].t[:, c0h:c0h + wd], in0=pu.t[:, 0:wd],
                                                                                      in1=sl_.t[:, 0:wd], op=ALU.mult),
                     reads=[pu.b, sl_.b], writes=[ab[s].b])

        def stC(c):
            s = c % 2
            for gi, (k0, nk) in enumerate(((0, 8), (8, 8), (16, 6))):
                pt = p_tr[(gi + 1) % 2]

                def tr_a(e, k0=k0, nk=nk, pt=pt):
                    for k in range(nk):
                        ins = e.transpose(out=pt.t[:, k, :], in_=ab[s].t[:, (k0 + k) * 128:(k0 + k + 1) * 128], identity=ident.t[:])
                    return ins
                P.op("pe", tr_a, reads=[ab[s].b, ident.b], writes=[pt.b])
                if gi % 2 == 0:
                    P.op("act", lambda e, k0=k0, nk=nk, pt=pt: e.copy(out=aT.t[:, k0:k0 + nk, :], in_=pt.t[:, 0:nk, :]),
                         reads=[pt.b], writes=[aT.b])
                else:
                    P.op("dve", lambda e, k0=k0, nk=nk, pt=pt: e.tensor_copy(out=aT.t[:, k0:k0 + nk, :], in_=pt.t[:, 0:nk, :]),
                         reads=[pt.b], writes=[aT.b])

        def stC2(c):
            s = c % 2
            for n in range(2):
                def mm_f(e, n=n):
                    for k in range(22):
                        ins = e.matmul(out=p_f[n].t[:], lhsT=aT.t[:, k, :], rhs=wo[:, k, n * 512:(n + 1) * 512],
                                       start=(k == 0), stop=(k == 21))
                    return ins
                P.op("pe", mm_f, reads=[aT.b] + wob, writes=[p_f[n].b])
            emit_post_norm_residual(P, p_f, xt[c % NX], g3b, xo[s], junk, ss, rstd, cm05, tmp)
            P.dma("sp", lambda e: e.dma_start(out=xout_fn(c), in_=xo[s].t[:]), reads=[xo[s].b], writes=[xoB], key=xo[s].b)

        n_ = c1 - c0
        for it in range(c0, c0 + n_ + 3):
            if c0 <= it - 3 < c0 + n_:
                stC(it - 3)
            if c0 <= it - 1 < c0 + n_:
                stA(it - 1)
            if c0 <= it - 2 < c0 + n_:
                stB(it - 2)
            if c0 <= it - 1 < c0 + n_:
                stA2(it - 1)
            if c0 <= it - 3 < c0 + n_:
                stC2(it - 3)
            if c0 <= it < c0 + n_:
                stL(it)
        P.emit()


def phase_lru(nc, x1, w_in, conv_w, conv_b, gate_w, gate_b, a_param, g0, pqd, hend, dbg=None, ntiles=None):
    T = 256
    NT = HALF // T
    if ntiles is not None:
        NT = ntiles
    NCH = LW // 128
    with contextlib.ExitStack() as st:
        cx = Ctx(nc, st)
        P = Prog(nc)
        st.enter_context(nc.allow_non_contiguous_dma(reason="tiny per-channel parameter vectors"))
        wt, wb = load_weight(P, cx, w_in, 8, 2 * LW, "lruwin")
        Ctx.gn += 1
        gw = st.enter_context(nc.sbuf_tensor("gw_%d" % Ctx.gn, [128, 24, 256], BF16))
        gwB = Buf("gw")
        P.dma("pool", lambda e: e.dma_start(out=gw[:, :, :], in_=gate_w.rearrange("g n (ki p) j -> p (g n ki) j", p=128)), writes=[gwB])
        ident = make_ident(P, cx)
        g0b = cx.sb("g0b", [128, D], F32)
        P.dma("sp", lambda e: e.dma_start(out=g0b.t[:], in_=g0.partition_broadcast(128)), writes=[g0b.b])
        cm05 = cx.sb("cm05", [128, 8], F32)
        P.op("pool", lambda e: e.memset(cm05.t[:], -0.5), writes=[cm05.b])
        cw = cx.sb("cw", [128, 4, NCH], F32)
        cb = cx.sb("cb", [128, NCH], F32)
        gb = cx.sb("gb", [128, 2, NCH], F32)
        ap_ = cx.sb("ap", [128, NCH], F32)
        cv = cx.sb("cv", [128, NCH], F32)
        cv2 = cx.sb("cv2", [128, NCH], F32)
        for i in range(4):
            P.dma("sp", lambda e, i=i: e.dma_start(out=cw.t[:, i, :], in_=conv_w[i:i + 1, :].rearrange("o (c p) -> p (o c)", p=128)), writes=[cw.b])
        P.dma("sp", lambda e: e.dma_start(out=cb.t[:], in_=conv_b.rearrange("o (c p) -> p (o c)", p=128)), writes=[cb.b])
        P.dma("sp", lambda e: e.dma_start(out=ap_.t[:], in_=a_param.rearrange("o (c p) -> p (o c)", p=128)), writes=[ap_.b])
        for g in range(2):
            P.dma("sp", lambda e, g=g: e.dma_start(out=gb.t[:, g, :], in_=gate_b[g:g + 1, :].rearrange("o (c p) -> p (o c)", p=128)), writes=[gb.b])
        P.op("act", lambda e: e.activation(out=cv.t[:], in_=ap_.t[:], func=AF.Exp, scale=-1.0), reads=[ap_.b], writes=[cv.b])
        P.op("act", lambda e: e.activation(out=cv.t[:], in_=cv.t[:], func=AF.Ln, bias=1.0), reads=[cv.b], writes=[cv.b])
        P.op("dve", lambda e: e.tensor_scalar(out=cv2.t[:], in0=cv.t[:], scalar1=-16.0, scalar2=None, op0=ALU.mult), reads=[cv.b], writes=[cv2.b])
        P.op("dve", lambda e: e.tensor_scalar(out=cv.t[:], in0=cv.t[:], scalar1=-8.0, scalar2=None, op0=ALU.mult), reads=[cv.b, cv2.b], writes=[cv.b])
        hst = cx.sb("hst", [128, NCH], F32)
        Ast = cx.sb("Ast", [128, NCH], F32)
        hstB = [Buf("hst%d" % c) for c in range(NCH)]
        AstB = [Buf("Ast%d" % c) for c in range(NCH)]
        P.op("pool", lambda e: e.memset(hst.t[:], 0.0), writes=hstB)
        P.op("pool", lambda e: e.memset(Ast.t[:], 1.0), writes=AstB)
        NX = 4
        NH = 3
        xt = [cx.sb("xt", [128, D], F32) for _ in range(NX)]
        junk = None
        ss = [cx.sb("ss", [128, 1], F32) for _ in range(2)]
        rstd = [cx.sb("rstd", [128, 1], F32) for _ in range(2)]
        hb2 = [cx.sb("hb", [128, D], BF16) for _ in range(2)]
        hT = [cx.sb("hT", [128, 8, T], BF16) for _ in range(NH)]
        yb = cx.sb("yb", [128, NCH, T], F32)
        ub = cx.sb("ub", [128, NCH, T + 3], F32)
        uc = cx.sb("uc", [128, NCH, T], F32)
        ucb = cx.sb("ucb", [128, NCH, T], BF16)
        rt = cx.sb("rt", [128, NCH, T], F32)
        it2 = [cx.sb("it", [128, NCH, T], F32) for _ in range(2)]
        at2 = [cx.sb("at", [128, NCH, T], F32) for _ in range(2)]
        hs = [cx.sb("hs", [128, T], F32) for _ in range(2)]
        cA = [cx.sb("cA", [128, T], F32) for _ in range(2)]
        NPQ = 2
        Pc = [cx.sb("Pc", [128, T], F32) for _ in range(NPQ)]
        Qc = [cx.sb("Qc", [128, T], F32) for _ in range(NPQ)]
        ybB = [Buf("yb%d" % c) for c in range(NCH)]
        ucB = [Buf("uc%d" % c) for c in range(NCH)]
        rB = [Buf("r%d" % c) for c in range(NCH)]
        iB = [[Buf("i%d_%d" % (p_, c)) for c in range(NCH)] for p_ in range(2)]
        aB = [[Buf("a%d_%d" % (p_, c)) for c in range(NCH)] for p_ in range(2)]
        p_tr2 = [cx.ps("p_tr", [128, 8, 128], BF16) for _ in range(2)]
        NPP = 4

        class PV:
            def __init__(self, tl):
                self.tl, self.b = tl, tl.b

            def sl(self, a, b_):
                return self.tl.t[:, a:b_]
        p_p = [PV(cx.ps("p_p", [128, 512], F32)) for k in range(NPP)]
        p_g = [cx.ps("p_g", [128, 512], F32) for _ in range(2)]
        pqB = Buf("pqd")
        P.op("pool", lambda e: e.memset(ub.t[:], 0.0), writes=[ub.b])
        nrm = [0]

        def load_x(cidx):
            x_ = xt[cidx % NX]
            P.dma("act", lambda e: e.dma_start(out=x_.t[:], in_=x1[cidx, :, :]), writes=[x_.b])

        def norm_a(cidx, slot):
            x_ = xt[cidx % NX]
            hb = hb2[slot]
            p_tr = p_tr2[slot]
            s2 = nrm[0] % 2
            nrm[0] += 1
            P.op("act", lambda e: e.activation(out=hb.t[:], in_=x_.t[:], func=AF.Square, accum_out=ss[s2].t[:]),
                 reads=[x_.b], writes=[hb.b, ss[s2].b])
            emit_rstd(P, ss[s2], rstd[s2], cm05, D)
            P.op("dve", lambda e: e.scalar_tensor_tensor(out=hb.t[:], in0=x_.t[:], scalar=rstd[s2].t[:, 0:1], in1=g0b.t[:],
                                                         op0=ALU.mult, op1=ALU.mult),
                 reads=[x_.b, rstd[s2].b, g0b.b], writes=[hb.b])

            def tr_h(e):
                for k in range(8):
                    ins = e.transpose(out=p_tr.t[:, k, :], in_=hb.t[:, k * 128:(k + 1) * 128], identity=ident.t[:])
                return ins
            P.op("pe", tr_h, reads=[hb.b, ident.b], writes=[p_tr.b])

        def norm_b(slot, hT_, dst_col):
            p_tr = p_tr2[slot]
            P.op("act", lambda e: e.copy(out=hT_.t[:, :, dst_col:dst_col + 128], in_=p_tr.t[:]), reads=[p_tr.b], writes=[hT_.b])

        def norm_T(cidx, hT_, dst_col):
            norm_a(cidx, 0)
            norm_b(0, hT_, dst_col)

        def proj(fc, n, pp, hT_):
            def f(e):
                for k in range(8):
                    ins = e.matmul(out=pp.sl(0, n), lhsT=wt[:, k, fc * 128:(fc + 1) * 128], rhs=hT_.t[:, k, 0:n],
                                   start=(k == 0), stop=(k == 7))
                return ins
            return f

        load_x(0)
        norm_T(0, hT[NH - 1], 0)
        for c in range(NCH):
            pp = p_p[c % NPP]
            P.op("pe", proj(NCH + c, 128, pp, hT[NH - 1]), reads=[hT[NH - 1].b] + wb, writes=[pp.b])
            P.op("act", lambda e, c=c, pp=pp: e.copy(out=ub.t[:, c, 0:3], in_=pp.sl(125, 128)), reads=[pp.b], writes=[ub.b])

        def stL(ti):
            for ci in range(T // 128):
                load_x(1 + ti * (T // 128) + ci)

        def stA_norm_a(ti):
            for ci in range(T // 128):
                norm_a(1 + ti * (T // 128) + ci, ci)

        def stA_norm_b(ti):
            hT_ = hT[ti % NH]
            for ci in range(T // 128):
                norm_b(ci, hT_, ci * 128)

        def stA_gen(ti):
            hT_ = hT[ti % NH]
            for c in range(NCH):
                pp = p_p[c % NPP]
                P.op("pe", proj(NCH + c, T, pp, hT_), reads=[hT_.b] + wb, writes=[pp.b])
                P.op("act", lambda e, c=c, pp=pp: e.copy(out=ub.t[:, c, 3:3 + T], in_=pp.sl(0, T)),
                     reads=[pp.b], writes=[ub.b])
                yield

        GRP = [list(range(0, NCH // 2)), list(range(NCH // 2, NCH))]

        def conv_act(ti):
            for c in range(NCH):
                P.op("act", lambda e, c=c: e.activation(out=uc.t[:, c, :], in_=ub.t[:, c, 3:3 + T], func=AF.Identity,
                                                        scale=cw.t[:, 3, c:c + 1], bias=cb.t[:, c:c + 1]),
                     reads=[ub.b, cw.b, cb.b], writes=[ucB[c]])

        def y_gelu(ti):
            hT_ = hT[ti % NH]
            for c in range(NCH):
                pp = p_p[c % NPP]
                P.op("pe", proj(c, T, pp, hT_), reads=[hT_.b] + wb, writes=[pp.b])
                P.op("act", lambda e, c=c, pp=pp: e.activation(out=yb.t[:, c, :], in_=pp.sl(0, T), func=AF.Gelu_apprx_tanh),
                     reads=[pp.b], writes=[ybB[c]])

        def conv_gates(ti):
            it_, iB_ = it2[ti % 2], iB[ti % 2]
            for gi, grp in enumerate(GRP):
                for i in range(3):
                    for c in grp:
                        P.op("dve", lambda e, c=c, i=i: e.scalar_tensor_tensor(out=uc.t[:, c, :], in0=ub.t[:, c, i:i + T],
                                                                                scalar=cw.t[:, i, c:c + 1], in1=uc.t[:, c, :],
                                                                                op0=ALU.mult, op1=ALU.add),
                             reads=[ub.b, cw.b, ucB[c]], writes=[ucB[c]])
                c0g, c1g = grp[0], grp[-1] + 1
                P.op("dve", lambda e, c0g=c0g, c1g=c1g: e.tensor_copy(out=ucb.t[:, c0g:c1g, :], in_=uc.t[:, c0g:c1g, :]),
                     reads=[ucB[c] for c in grp], writes=[ucbB[gi]])
                if gi == 1:
                    P.op("pool", lambda e: e.tensor_copy(out=ub.t[:, :, 0:3], in_=ub.t[:, :, T:T + 3]), reads=[ub.b], writes=[ub.b])
                for g in range(2):
                    for c in grp:
                        n, jo = c // 2, c % 2
                        pg = p_g[(g * NCH + c) % 2]

                        def mm_g(e, g=g, n=n, jo=jo, pg=pg):
                            for ki in range(2):
                                ins = e.matmul(out=pg.t[:, 0:T], lhsT=gw[:, (g * 6 + n) * 2 + ki, jo * 128:(jo + 1) * 128],
                                               rhs=ucb.t[:, 2 * n + ki, :], start=(ki == 0), stop=(ki == 1))
                            return ins
                        P.op("pe", mm_g, reads=[ucbB[gi], gwB], writes=[pg.b])
                        dst, dB = (rt, rB) if g == 0 else (it_, iB_)
                        P.op("act", lambda e, g=g, c=c, pg=pg, dst=dst: e.activation(out=dst.t[:, c, :], in_=pg.t[:, 0:T], func=AF.Sigmoid,
                                                                                     bias=gb.t[:, g, c:c + 1]),
                             reads=[pg.b, gb.b], writes=[dB[c]])

        def exp_sqrt_gen(ti):
            at_, aB_ = at2[ti % 2], aB[ti % 2]
            for c in range(NCH):
                P.op("act", lambda e, c=c: e.activation(out=at_.t[:, c, :], in_=rt.t[:, c, :], func=AF.Exp, scale=cv.t[:, c:c + 1]),
                     reads=[rB[c], cv.b], writes=[aB_[c]])
                yield
                P.op("act", lambda e, c=c: e.activation(out=rt.t[:, c, :], in_=rt.t[:, c, :], func=AF.Exp, scale=cv2.t[:, c:c + 1]),
                     reads=[rB[c], cv2.b], writes=[rB[c]])
                yield
            for c in range(NCH):
                P.op("act", lambda e, c=c: e.activation(out=rt.t[:, c, :], in_=rt.t[:, c, :], func=AF.Sqrt, scale=-1.0, bias=1.0),
                     reads=[rB[c]], writes=[rB[c]])
                yield

        def interleave(ga, gb, ratio):
            da = db = False
            while not (da and db):
                for _ in range(ratio):
                    if not da:
                        try:
                            next(ga)
                        except StopIteration:
                            da = True
                if not db:
                    try:
                        next(gb)
                    except StopIteration:
                        db = True

        def empty_gen():
            return
            yield

        def b_mul(ti):
            it_, iB_ = it2[ti % 2], iB[ti % 2]
            for c in range(NCH):
                P.op("dve", lambda e, c=c: e.tensor_tensor(out=it_.t[:, c, :], in0=it_.t[:, c, :], in1=uc.t[:, c, :], op=ALU.mult),
                     reads=[iB_[c], ucB[c]], writes=[iB_[c]])
            for c in range(NCH):
                P.op("dve", lambda e, c=c: e.tensor_tensor(out=it_.t[:, c, :], in0=it_.t[:, c, :], in1=rt.t[:, c, :], op=ALU.mult),
                     reads=[iB_[c], rB[c]], writes=[iB_[c]])

        def scans(ti):
            it_, iB_ = it2[ti % 2], iB[ti % 2]
            at_, aB_ = at2[ti % 2], aB[ti % 2]
            for c in range(NCH):
                s2 = c % 2
                sq = c % NPQ
                P.op("dve", lambda e, c=c, s2=s2: e.tensor_tensor_scan(out=hs[s2].t[:], data0=at_.t[:, c, :], data1=it_.t[:, c, :],
                                                                       initial=hst.t[:, c:c + 1], op0=ALU.mult, op1=ALU.add),
                     reads=[aB_[c], iB_[c], hstB[c]], writes=[hs[s2].b])
                P.op("dve", lambda e, c=c, s2=s2: e.tensor_tensor_scan(out=cA[s2].t[:], data0=at_.t[:, c, :], data1=at_.t[:, c, :],
                                                                       initial=Ast.t[:, c:c + 1], op0=ALU.mult, op1=ALU.min),
                     reads=[aB_[c], AstB[c]], writes=[cA[s2].b])
                P.op("pool", lambda e, c=c, s2=s2: e.tensor_copy(out=hst.t[:, c:c + 1], in_=hs[s2].t[:, T - 1:T]), reads=[hs[s2].b], writes=[hstB[c]])
                P.op("pool", lambda e, c=c, s2=s2: e.tensor_copy(out=Ast.t[:, c:c + 1], in_=cA[s2].t[:, T - 1:T]), reads=[cA[s2].b], writes=[AstB[c]])
                P.op("dve", lambda e, c=c, s2=s2, sq=sq: e.tensor_tensor(out=Pc[sq].t[:], in0=hs[s2].t[:], in1=yb.t[:, c, :], op=ALU.mult),
                     reads=[hs[s2].b, ybB[c]], writes=[Pc[sq].b])
                P.op("pool", lambda e, c=c, s2=s2, sq=sq: e.tensor_tensor(out=Qc[sq].t[:], in0=cA[s2].t[:], in1=yb.t[:, c, :], op=ALU.mult),
                     reads=[cA[s2].b, ybB[c]], writes=[Qc[sq].b])
                P.dma("sp", lambda e, c=c, sq=sq: e.dma_start(out=pqd[0, ti, :, c * T:(c + 1) * T], in_=Pc[sq].t[:]), reads=[Pc[sq].b], writes=[pqB], key=Pc[sq].b)
                P.dma("sp", lambda e, c=c, sq=sq: e.dma_start(out=pqd[1, ti, :, c * T:(c + 1) * T], in_=Qc[sq].t[:]), reads=[Qc[sq].b], writes=[pqB], key=Qc[sq].b)

        ucbB = [Buf("ucb0"), Buf("ucb1")]
        stL(0)
        stA_norm_a(0)
        stA_norm_b(0)
        for it_i in range(1, NT + 3):
            i = it_i - 2
            if 0 <= i + 2 < NT:
                stL(i + 2)
            if 0 <= i < NT:
                conv_act(i)
            if 0 <= i + 1 < NT and i + 1 >= 1:
                stA_norm_a(i + 1)
            if 0 <= i - 1 < NT:
                y_gelu(i - 1)
            if 0 <= i < NT:
                conv_gates(i)
            if 0 <= i + 1 < NT and i + 1 >= 1:
                stA_norm_b(i + 1)
            interleave(exp_sqrt_gen(i) if 0 <= i < NT else empty_gen(),
                       stA_gen(i + 1) if 0 <= i + 1 < NT else empty_gen(), 3)
            if 0 <= i - 1 < NT:
                scans(i - 1)
            if 0 <= i < NT:
                b_mul(i)
        hB = Buf("hend")
        P.dma("sp", lambda e: e.dma_start(out=hend[:, :], in_=hst.t[:]), reads=hstB, writes=[hB], key=hst.b)
        P.emit()


def phase_exchange(nc, hend, hall):
    Prog.uid += 1
    with nc.semaphore("cc_sem%d" % Prog.uid) as cc_sem:
      with nc.Block() as b0:
        @b0.gpsimd
        def _(g):
            g.sem_clear(cc_sem)
      with nc.Block() as block:
        @block.gpsimd
        def _(g):
            g.collective_compute("AllGather", ALU.bypass, replica_groups=[[0, 1], [2, 3], [4, 5], [6, 7]],
                                 ins=[hend.opt()], outs=[hall.opt()]).then_inc(cc_sem)
            g.wait_ge(cc_sem, 1)


def phase_lru_out(nc, x1, pqd, hall, isodd, w_out, g1, xmid, nchunks=None):
    NCH = LW // 128
    T = 256
    with contextlib.ExitStack() as st:
        cx = Ctx(nc, st)
        P = Prog(nc)
        wt, wb = load_weight(P, cx, w_out, NCH, D, "lruwout")
        g1b = cx.sb("g1b", [128, D], F32)
        P.dma("sp", lambda e: e.dma_start(out=g1b.t[:], in_=g1.partition_broadcast(128)), writes=[g1b.b])
        cm05 = cx.sb("cm05", [128, 8], F32)
        P.op("pool", lambda e: e.memset(cm05.t[:], -0.5), writes=[cm05.b])
        h0 = cx.sb("h0", [128, NCH], F32)
        odd = cx.sb("odd", [128, 1], F32)
        P.dma("sp", lambda e: e.dma_start(out=h0.t[:], in_=hall[0:128, :]), writes=[h0.b])
        P.dma("sp", lambda e: e.dma_start(out=odd.t[:], in_=isodd[:, :]), writes=[odd.b])
        P.op("dve", lambda e: e.tensor_scalar(out=h0.t[:], in0=h0.t[:], scalar1=odd.t[:, 0:1], scalar2=None, op0=ALU.mult),
             reads=[h0.b, odd.b], writes=[h0.b])
        NS = 3
        NX = 8
        xt = [cx.sb("xt", [128, D], F32) for _ in range(NX)]
        Pt = [cx.sb("Pt", [128, NCH, T], F32) for _ in range(NS)]
        Qt = [cx.sb("Qt", [128, NCH, T], F32) for _ in range(NS)]
        hy = [cx.sb("hy", [128, NCH, T], BF16) for _ in range(NS)]
        junk = None
        ss = [cx.sb("ss", [128, 4], F32) for _ in range(2)]
        rstd = [cx.sb("rstd", [128, 1], F32) for _ in range(2)]
        tmp = [cx.sb("tmp", [128, D], F32) for _ in range(2)]
        xo = [cx.sb("xo", [128, D], F32) for _ in range(2)]
        p_m = [[cx.ps("p_m", [128, 512], F32) for _ in range(2)] for _ in range(2)]
        xmB = Buf("xmid")
        n_grp = (HALF // T) if nchunks is None else nchunks // 2

        def stL(g):
            s = g % NS
            for ci in range(2):
                c = 2 * g + ci
                x_ = xt[c % NX]
                P.dma("act", lambda e, x_=x_, c=c: e.dma_start(out=x_.t[:], in_=x1[c + 1, :, :]), writes=[x_.b])
            P.dma("act", lambda e: e.dma_start(out=Pt[s].t[:], in_=pqd[0, g].rearrange("p (c t) -> p c t", c=NCH)), writes=[Pt[s].b])
            P.dma("act", lambda e: e.dma_start(out=Qt[s].t[:], in_=pqd[1, g].rearrange("p (c t) -> p c t", c=NCH)), writes=[Qt[s].b])

        def stA(g):
            s = g % NS
            P.op("dve", lambda e: e.tensor_tensor(out=Qt[s].t[:], in0=Qt[s].t[:], in1=h0.t[:].unsqueeze(2).broadcast_to([128, NCH, T]), op=ALU.mult),
                 reads=[Qt[s].b, h0.b], writes=[Qt[s].b])
            P.op("pool", lambda e: e.tensor_tensor(out=hy[s].t[:], in0=Qt[s].t[:], in1=Pt[s].t[:], op=ALU.add),
                 reads=[Qt[s].b, Pt[s].b], writes=[hy[s].b])

        def stB(g):
            s = g % NS
            for ci in range(2):
                c = 2 * g + ci
                pm = p_m[ci]
                so = c % 2
                for n in range(2):
                    def mm(e, n=n, pm=pm, ci=ci):
                        for k in range(NCH):
                            ins = e.matmul(out=pm[n].t[:], lhsT=hy[s].t[:, k, ci * 128:(ci + 1) * 128], rhs=wt[:, k, n * 512:(n + 1) * 512],
                                           start=(k == 0), stop=(k == NCH - 1))
                        return ins
                    P.op("pe", mm, reads=[hy[s].b] + wb, writes=[pm[n].b])
                emit_post_norm_residual(P, pm, xt[c % NX], g1b, xo[so], junk, ss[so], rstd[so], cm05, tmp[so])
                P.dma("sp", lambda e, c=c, so=so: e.dma_start(out=xmid[c + 1, :, :], in_=xo[so].t[:]), reads=[xo[so].b], writes=[xmB], key=xo[so].b)

        run_pipeline(n_grp, [stL, stA, stB])
        P.emit()


def build_program(mode="fused"):
    nc = bass.Bass("TRN2", target_bir_lowering=False)
    dt = lambda name, shape, d=F32, **kw: nc.dram_tensor(name, shape, d, **kw).ap()
    inp = dict(kind="ExternalInput")
    outk = dict(kind="ExternalOutput")
    mid_out = outk if mode == "A" else (inp if mode == "B" else {})
    isodd = dt("isodd", [128, 1], **inp)
    lru_w_out = dt("lru_w_out", [LW, D], **inp)
    ng = dt("norm_g", [8, D], **inp)
    fwi = dt("ffn_w_in", [2, D, 2 * DFF], **inp)
    fwo = dt("ffn_w_out", [2, DFF, D], **inp)
    x1 = dt("x1", [NEXT, 128, D], **mid_out)
    pqd = dt("pqd", [2, HALF // 256, 128, (LW // 128) * 256], **mid_out)
    hall = dt("hall", [256, LW // 128], **({} if mode != "B" else inp))
    xmid = dt("xmid", [NEXT, 128, D])
    if mode != "B":
        xf = dt("xf", [SEQ, D], **inp)
        pos = dt("pos", [1, SEQ], **inp)
        invf = dt("invf", [128, 1], **inp)
        ret_w_in = dt("ret_w_in", [D, 2 * QKW + 2 * VW], **inp)
        ret_w_out = dt("ret_w_out", [VW, D], **inp)
        lru_w_in = dt("lru_w_in", [D, 2 * LW], **inp)
        lru_conv_w = dt("lru_conv_w", [4, LW], **inp)
        lru_conv_b = dt("lru_conv_b", [1, LW], **inp)
        lru_gate_w = dt("lru_gate_w", [2, 6, 256, 256], **inp)
        lru_gate_b = dt("lru_gate_b", [2, LW], **inp)
        lru_a = dt("lru_a_param", [1, LW], **inp)
        tabs = dt("tabs", [SEQ, 2, 128])
        ogd = dt("ogd", [NEXT, 128, VW], BF16)
        hend = dt("hend", [128, LW // 128])
        with contextlib.ExitStack() as o0:
            rw = alloc_weight(nc, o0, 8, 2 * QKW + 2 * VW, "retwin")
            phase_tables(nc, pos, invf, tabs,
                         prefetch=lambda P: issue_weight_load(P, rw, ret_w_in, 8, 2 * QKW + 2 * VW, "retwin"))
            phase_ret(nc, xf, tabs, ret_w_in, ng[0:1, :], ogd, pre_w=rw)
        with contextlib.ExitStack() as o1:
            fw0 = alloc_weight(nc, o1, 8, 2 * DFF, "ffnwin")
            phase_ret_out(nc, xf, ogd, ret_w_out, ng[1:2, :], xmid,
                          prefetch=lambda P: issue_weight_load(P, fw0, fwi[0], 8, 2 * DFF, "ffnwin"))
            phase_ffn(nc, xmid, fwi[0], fwo[0], ng[2:3, :], ng[3:4, :], lambda c: x1[c, :, :], 0, NEXT, pre_wi=fw0)
        phase_lru(nc, x1, lru_w_in, lru_conv_w, lru_conv_b, lru_gate_w, lru_gate_b, lru_a, ng[4:5, :], pqd, hend)
        phase_exchange(nc, hend, hall)
        if mode == "A":
            hallo = dt("hallo", [256, LW // 128], **outk)
            Prog.uid += 1
            with nc.semaphore("cps%d" % Prog.uid) as s:
                with nc.Block() as b0:
                    @b0.gpsimd
                    def _(g):
                        g.sem_clear(s)
                with nc.Block() as blk:
                    @blk.sync
                    def _(e):
                        e.dma_start(out=hallo[:, :], in_=hall[:, :]).then_inc(s, 16)
                        e.wait_ge(s, 16)
            return nc
    out = dt("out", [HALF // C, C, D], **outk)
    phase_lru_out(nc, x1, pqd, hall, isodd, lru_w_out, ng[5:6, :], xmid)
    phase_ffn(nc, xmid, fwi[1], fwo[1], ng[6:7, :], ng[7:8, :], lambda c: out[c - 1, :, :], 1, NEXT)
    return nc


_INV = None


def make_in_maps(x, ret_w_in, ret_w_out, lru_w_in, lru_conv_w, lru_conv_b, lru_gate_w, lru_gate_b,
                 lru_a_param, lru_w_out, norm_g, ffn_w_in, ffn_w_out):
    f = lambda a: np.ascontiguousarray(np.asarray(a, dtype=np.float32))
    inv = (1.0 / (np.float32(10000.0) ** (np.arange(128, dtype=np.float32) / np.float32(128)))).astype(np.float32)
    x = f(x)
    shared = {
        "invf": inv[:, None].copy(),
        "ret_w_in": f(ret_w_in[0]), "ret_w_out": f(ret_w_out[0]), "lru_w_in": f(lru_w_in[0]),
        "lru_conv_w": f(lru_conv_w[0]), "lru_conv_b": f(lru_conv_b), "lru_gate_w": f(lru_gate_w[0]),
        "lru_gate_b": f(np.asarray(lru_gate_b)[0].reshape(2, LW)), "lru_a_param": f(lru_a_param),
        "lru_w_out": f(lru_w_out[0]), "norm_g": f(np.asarray(norm_g).reshape(8, D)),
        "ffn_w_in": f(ffn_w_in), "ffn_w_out": f(ffn_w_out),
    }
    maps = []
    for c in range(NCORES):
        b, half = c // 2, c % 2
        if half:
            xfull = x[b]
            p = np.arange(SEQ)
        else:
            xfull = np.concatenate([np.zeros((HALF, D), np.float32), x[b, :HALF]], 0)
            p = (np.arange(SEQ) + HALF) % SEQ
        m = dict(shared)
        m["xf"] = np.ascontiguousarray(xfull)
        m["pos"] = p[None, :].astype(np.float32)
        m["isodd"] = np.full((128, 1), float(half), np.float32)
        maps.append(m)
    return maps


FUSED = True


def kernel(**inputs):
    maps = make_in_maps(**inputs)
    cores = list(range(NCORES))
    if FUSED:
        res = run_bass_kernel_spmd(build_program("fused"), maps, core_ids=cores)
        outs = [r["out"] for r in res.results]
    else:
        keysB = ("isodd", "lru_w_out", "norm_g", "ffn_w_in", "ffn_w_out")
        resA = run_bass_kernel_spmd(build_program("A"), maps, core_ids=cores)
        mapsB = []
        for c in range(NCORES):
            m = {k: maps[c][k] for k in keysB}
            m["x1"] = np.asarray(resA.results[c]["x1"])
            m["pqd"] = np.asarray(resA.results[c]["pqd"])
            m["hall"] = np.asarray(resA.results[c]["hallo"])
            mapsB.append(m)
        del resA
        resB = run_bass_kernel_spmd(build_program("B"), mapsB, core_ids=cores)
        outs = [r["out"] for r in resB.results]
    B = NCORES // 2
    out = np.empty((B, SEQ, D), np.float32)
    for c in range(NCORES):
        b, half = c // 2, c % 2
        out[b, half * HALF:(half + 1) * HALF] = np.asarray(outs[c]).reshape(HALF, D)
    return out
```

```python
import contextlib
import math
import numpy as np
import concourse.bass as bass
import concourse.mybir as mybir
from concourse.bass_utils import run_bass_kernel_spmd

F32 = mybir.dt.float32
BF16 = mybir.dt.bfloat16
I32 = mybir.dt.int32
AF = mybir.ActivationFunctionType
ALU = mybir.AluOpType

ENGS = ("pe", "act", "dve", "pool", "sp")

NCORES = 8
D = 1024
C = 128
SEQ = 8192
HALF = 4096
NLOC = 64
HALO = 31
NEXT = 33
H = 4
DK = 256
DV = 512
QKW = 1024
VW = 2048
DFF = 2816
LW = 1536
EPS = 1e-6
LG = [math.log1p(-2.0 ** (-5 - h)) for h in range(H)]


class Buf:
    __slots__ = ("name", "last_w", "readers", "sem", "sem_cnt")

    def __init__(self, name=""):
        self.name = name
        self.last_w = None
        self.readers = []
        self.sem = None
        self.sem_cnt = 0

    def reset(self):
        self.last_w = None
        self.readers = []
        self.sem = None
        self.sem_cnt = 0


class Op:
    __slots__ = ("eng", "fn", "deps", "is_dma", "needs_inc", "count", "dsem", "dval", "oid")

    def __init__(self, eng, fn, is_dma):
        self.eng = eng
        self.fn = fn
        self.deps = set()
        self.is_dma = is_dma
        self.needs_inc = False
        self.count = None
        self.dsem = None
        self.dval = None


class Prog:
    uid = 0

    def __init__(self, nc):
        self.nc = nc
        self.ops = []
        self.n_dma_sems = 0
        self.seen = {}
        self.sw_sems = set()

    def _add(self, op, reads, writes):
        op.oid = len(self.ops)
        for b in reads:
            if b.last_w is not None:
                op.deps.add(b.last_w)
        for b in writes:
            cands = list(b.readers)
            if b.last_w is not None:
                cands.append(b.last_w)
            for r in cands:
                p = self.ops[r]
                if op.is_dma or p.is_dma or p.eng != op.eng:
                    op.deps.add(r)
        op.deps.discard(op.oid)
        for b in reads:
            self.seen[id(b)] = b
        for b in writes:
            self.seen[id(b)] = b
        for b in reads:
            b.readers.append(op.oid)
        for b in writes:
            b.last_w = op.oid
            b.readers = []
        self.ops.append(op)
        return op

    def op(self, eng, fn, reads=(), writes=()):
        return self._add(Op(eng, fn, False), reads, writes)

    def dma(self, eng, fn, reads=(), writes=(), key=None):
        op = Op(eng, fn, True)
        kb = key if key is not None else (writes[0] if writes else reads[0])
        if kb.sem is None:
            kb.sem = self.n_dma_sems
            self.n_dma_sems += 1
            self.seen[id(kb)] = kb
        if eng == "pool":
            self.sw_sems.add(kb.sem)
        kb.sem_cnt += 16
        op.dsem = kb.sem
        op.dval = kb.sem_cnt
        return self._add(op, reads, writes)

    def emit(self):
        nc = self.nc
        ops = self.ops
        for o in ops:
            for d in o.deps:
                ops[d].needs_inc = True
        cnt = {e: 0 for e in ENGS}
        for o in ops:
            if not o.is_dma and o.needs_inc:
                cnt[o.eng] += 1
                o.count = cnt[o.eng]
        by_eng = {e: [o for o in ops if o.eng == e] for e in ENGS}
        with contextlib.ExitStack() as st:
            Prog.uid += 1
            u = Prog.uid
            esem = {e: st.enter_context(nc.semaphore("se%d_%s" % (u, e))) for e in ENGS}
            NSW = 4
            assert len(self.sw_sems) <= NSW
            swpool = [st.enter_context(nc.semaphore("sw%d_%d" % (u, i))) for i in range(NSW)]
            dsem = [None] * self.n_dma_sems
            for j, i in enumerate(sorted(self.sw_sems)):
                dsem[i] = swpool[j]
            for i in range(self.n_dma_sems):
                if dsem[i] is None:
                    dsem[i] = st.enter_context(nc.semaphore("sd%d_%d" % (u, i)))
            with nc.Block() as b0:
                @b0.gpsimd
                def _(g):
                    for sm in list(esem.values()) + swpool + [d for d in dsem if d not in swpool]:
                        g.sem_clear(sm)
            block = st.enter_context(nc.Block())

            def body(ename, eng):
                known = {}
                for o in by_eng[ename]:
                    need = {}
                    for d in o.deps:
                        p = ops[d]
                        if p.is_dma:
                            k = ("d", p.dsem)
                            v = p.dval
                        else:
                            k = ("e", p.eng)
                            v = p.count
                        if need.get(k, 0) < v:
                            need[k] = v
                    for k, v in need.items():
                        if known.get(k, 0) >= v:
                            continue
                        known[k] = v
                        s = dsem[k[1]] if k[0] == "d" else esem[k[1]]
                        eng.wait_ge(s, v)
                    ins = o.fn(eng)
                    if o.is_dma:
                        ins.then_inc(dsem[o.dsem], 16)
                    elif o.needs_inc:
                        ins.then_inc(esem[ename], 1)
                last = {}
                for o in by_eng[ename]:
                    if o.is_dma:
                        last[o.dsem] = max(last.get(o.dsem, 0), o.dval)
                for s, v in last.items():
                    if known.get(("d", s), 0) < v:
                        eng.wait_ge(dsem[s], v)

            @block.tensor
            def _(e):
                body("pe", e)

            @block.scalar
            def _(e):
                body("act", e)

            @block.vector
            def _(e):
                body("dve", e)

            @block.gpsimd
            def _(e):
                body("pool", e)

            @block.sync
            def _(e):
                body("sp", e)
        for b in self.seen.values():
            b.reset()


class Tl:
    __slots__ = ("t", "b")

    def __init__(self, t, name):
        self.t = t
        self.b = Buf(name)


class Ctx:
    gn = 0

    def __init__(self, nc, st):
        self.nc = nc
        self.st = st
        self.n = 0

    def sb(self, name, shape, dt):
        Ctx.gn += 1
        nm = "%s_%d" % (name, Ctx.gn)
        return Tl(self.st.enter_context(self.nc.sbuf_tensor(nm, shape, dt)), nm)

    def ps(self, name, shape, dt):
        Ctx.gn += 1
        nm = "%s_%d" % (name, Ctx.gn)
        return Tl(self.st.enter_context(self.nc.psum_tensor(nm, shape, dt)), nm)


def run_pipeline(n, stages, lo=0):
    ns = len(stages)
    for it in range(lo, lo + n + ns - 1):
        for k in reversed(range(ns)):
            c = it - k
            if lo <= c < lo + n:
                stages[k](c)


def alloc_weight(nc, st, kchunks, ncols, name):
    Ctx.gn += 1
    return st.enter_context(nc.sbuf_tensor("%s_%d" % (name, Ctx.gn), [128, kchunks, ncols], BF16))


def issue_weight_load(P, wt, w_ap, kchunks, ncols, name, col0=0):
    bufs = [Buf("%s_k%d" % (name, k)) for k in range(kchunks)]
    keyb = Buf(name + "_sem")
    for k in range(kchunks):
        P.dma("pool", (lambda e, k=k: e.dma_start(out=wt[:, k, :],
                                                  in_=w_ap[k * 128:(k + 1) * 128, col0:col0 + ncols])),
              writes=[bufs[k]], key=keyb)
    return bufs


def load_weight(P, cx, w_ap, kchunks, ncols, name, col0=0):
    wt = alloc_weight(cx.nc, cx.st, kchunks, ncols, name)
    return wt, issue_weight_load(P, wt, w_ap, kchunks, ncols, name, col0)


def make_ident(P, cx):
    nc = cx.nc
    identf = cx.sb("identf", [128, 128], F32)
    ident = cx.sb("ident", [128, 128], BF16)

    P.op("pool", lambda e: e.memset(identf.t[:], 0.0), writes=[identf.b])
    P.op("pool", lambda e: e.affine_select(out=identf.t[:], in_=identf.t[:], pattern=[[-1, 128]],
                                           compare_op=ALU.not_equal, fill=1.0, base=0, channel_multiplier=1),
         reads=[identf.b], writes=[identf.b])
    P.op("pool", lambda e: e.tensor_copy(out=ident.t[:], in_=identf.t[:]), reads=[identf.b], writes=[ident.b])
    return ident


def emit_rstd(P, ss, rstd, cnst, width):
    P.op("dve", lambda e: e.tensor_scalar(out=rstd.t[:], in0=ss.t[:], scalar1=1.0 / width, scalar2=EPS,
                                          op0=ALU.mult, op1=ALU.add), reads=[ss.b], writes=[rstd.b])
    n = rstd.t.shape[1]
    P.op("pool", lambda e: e.tensor_tensor(out=rstd.t[:], in0=rstd.t[:], in1=cnst.t[:, 0:n], op=ALU.pow),
         reads=[rstd.b, cnst.b], writes=[rstd.b])


TWO_PI = 2.0 * math.pi
CW1 = 6.28125
CW2 = 0.0019350051879882812
CW3 = TWO_PI - CW1 - CW2


def phase_tables(nc, pos, invf, tabs, prefetch=None):
    TW = 512
    with contextlib.ExitStack() as st:
        cx = Ctx(nc, st)
        P = Prog(nc)
        posb = cx.sb("posb", [128, SEQ], F32)
        inv = cx.sb("inv", [128, 1], F32)
        P.dma("sp", lambda e: e.dma_start(out=posb.t[:], in_=pos.partition_broadcast(128)), writes=[posb.b])
        P.dma("sp", lambda e: e.dma_start(out=inv.t[:], in_=invf[:, :]), writes=[inv.b])
        if prefetch is not None:
            prefetch(P)
        NS = 4
        ang = [cx.sb("ang", [128, TW], F32) for _ in range(NS)]
        ki = [cx.sb("ki", [128, TW], I32) for _ in range(NS)]
        kf = [cx.sb("kf", [128, TW], F32) for _ in range(NS)]
        rr = [cx.sb("rr", [128, TW], F32) for _ in range(NS)]
        rs = [cx.sb("rs", [128, TW], F32) for _ in range(NS)]
        rc = [cx.sb("rc", [128, TW], F32) for _ in range(NS)]
        cs = [cx.sb("cs", [128, 2, TW], F32) for _ in range(NS)]
        ct = [cx.sb("ct", [128, 2, 128], F32) for _ in range(4)]
        pT = [cx.ps("pT", [128, 2, 128], F32) for _ in range(4)]
        identf = cx.sb("identf", [128, 128], F32)
        P.op("pool", lambda e: e.memset(identf.t[:], 0.0), writes=[identf.b])
        P.op("pool", lambda e: e.affine_select(out=identf.t[:], in_=identf.t[:], pattern=[[-1, 128]],
                                               compare_op=ALU.not_equal, fill=1.0, base=0, channel_multiplier=1),
             reads=[identf.b], writes=[identf.b])
        tabB = Buf("tabs")
        for i in range(SEQ // TW):
            s = i % NS
            sl = slice(i * TW, (i + 1) * TW)
            P.op("dve", lambda e, s=s, sl=sl: e.tensor_scalar(out=ang[s].t[:], in0=posb.t[:, sl], scalar1=inv.t[:, 0:1],
                                                              scalar2=None, op0=ALU.mult),
                 reads=[posb.b, inv.b], writes=[ang[s].b])
            P.op("dve", lambda e, s=s: e.tensor_scalar(out=ki[s].t[:], in0=ang[s].t[:], scalar1=1.0 / TWO_PI,
                                                       scalar2=None, op0=ALU.mult),
                 reads=[ang[s].b], writes=[ki[s].b])
            P.op("pool", lambda e, s=s: e.tensor_copy(out=kf[s].t[:], in_=ki[s].t[:]), reads=[ki[s].b], writes=[kf[s].b])
            P.op("dve", lambda e, s=s: e.scalar_tensor_tensor(out=rr[s].t[:], in0=kf[s].t[:], scalar=-CW1, in1=ang[s].t[:], op0=ALU.mult, op1=ALU.add),
                 reads=[ang[s].b, kf[s].b], writes=[rr[s].b])
            P.op("dve", lambda e, s=s: e.scalar_tensor_tensor(out=rs[s].t[:], in0=kf[s].t[:], scalar=-CW2, in1=rr[s].t[:], op0=ALU.mult, op1=ALU.add),
                 reads=[rr[s].b, kf[s].b], writes=[rs[s].b])
            P.op("dve", lambda e, s=s: e.scalar_tensor_tensor(out=rr[s].t[:], in0=kf[s].t[:], scalar=-CW3, in1=rs[s].t[:], op0=ALU.mult, op1=ALU.add),
                 reads=[rs[s].b, kf[s].b], writes=[rr[s].b])
            P.op("dve", lambda e, s=s: e.tensor_scalar(out=ang[s].t[:], in0=rr[s].t[:], scalar1=math.pi, scalar2=-TWO_PI,
                                                       op0=ALU.is_gt, op1=ALU.mult), reads=[rr[s].b], writes=[ang[s].b])
            P.op("dve", lambda e, s=s: e.tensor_tensor(out=rs[s].t[:], in0=rr[s].t[:], in1=ang[s].t[:], op=ALU.add),
                 reads=[rr[s].b, ang[s].b], writes=[rs[s].b])
            P.op("dve", lambda e, s=s: e.tensor_scalar(out=ang[s].t[:], in0=rr[s].t[:], scalar1=math.pi / 2, scalar2=-TWO_PI,
                                                       op0=ALU.is_gt, op1=ALU.mult), reads=[rr[s].b, rs[s].b], writes=[ang[s].b])
            P.op("dve", lambda e, s=s: e.scalar_tensor_tensor(out=rc[s].t[:], in0=rr[s].t[:], scalar=math.pi / 2, in1=ang[s].t[:],
                                                              op0=ALU.add, op1=ALU.add),
                 reads=[rr[s].b, ang[s].b], writes=[rc[s].b])
            P.op("act", lambda e, s=s: e.activation(out=cs[s].t[:, 0, :], in_=rc[s].t[:], func=AF.Sin), reads=[rc[s].b], writes=[cs[s].b])
            P.op("act", lambda e, s=s: e.activation(out=cs[s].t[:, 1, :], in_=rs[s].t[:], func=AF.Sin), reads=[rs[s].b], writes=[cs[s].b])
            for tbk in range(TW // 128):
                pt_ = pT[(i * (TW // 128) + tbk) % 4]
                ct_ = ct[(i * (TW // 128) + tbk) % 4]

                def trc(e, s=s, tbk=tbk, pt_=pt_):
                    e.transpose(out=pt_.t[:, 0, :], in_=cs[s].t[:, 0, tbk * 128:(tbk + 1) * 128], identity=identf.t[:])
                    return e.transpose(out=pt_.t[:, 1, :], in_=cs[s].t[:, 1, tbk * 128:(tbk + 1) * 128], identity=identf.t[:])
                P.op("pe", trc, reads=[cs[s].b, identf.b], writes=[pt_.b])
                P.op("act", lambda e, pt_=pt_, ct_=ct_: e.copy(out=ct_.t[:], in_=pt_.t[:]), reads=[pt_.b], writes=[ct_.b])
                t0 = i * TW + tbk * 128
                P.dma("sp", lambda e, t0=t0, ct_=ct_: e.dma_start(out=tabs[t0:t0 + 128, :, :], in_=ct_.t[:]), reads=[ct_.b], writes=[tabB], key=ct_.b)
        P.emit()


def phase_ret(nc, xf, tabs, w_in, g0, ogd, c_begin=0, c_end=NLOC, dbg=None, pre_w=None):
    with contextlib.ExitStack() as st:
        cx = Ctx(nc, st)
        P = Prog(nc)
        if pre_w is not None:
            wt, wb = pre_w, []
        else:
            wt, wb = load_weight(P, cx, w_in, 8, 2 * QKW + 2 * VW, "retwin")
        ident = make_ident(P, cx)
        g0b = cx.sb("g0b", [128, D], F32)
        P.dma("sp", lambda e: e.dma_start(out=g0b.t[:], in_=g0.partition_broadcast(128)), writes=[g0b.b])
        cm05 = cx.sb("cm05", [128, 8], F32)
        P.op("pool", lambda e: e.memset(cm05.t[:], -0.5), writes=[cm05.b])
        dif_i = cx.sb("dif_i", [128, 128], I32)
        dif = cx.sb("dif", [128, 128], F32)
        dpos = cx.sb("dpos", [128, 128], F32)
        dge = cx.sb("dge", [128, 128], F32)
        maskT = cx.sb("maskT", [128, H, 128], F32)
        qdec = cx.sb("qdec", [128, H, 128], F32)
        kdec = cx.sb("kdec", [128, H], F32)
        qi_i = cx.sb("qi_i", [128, 128], I32)
        qi = cx.sb("qi", [128, 128], F32)
        kk_i = cx.sb("kk_i", [128, 1], I32)
        kk = cx.sb("kk", [128, 1], F32)
        P.op("pool", lambda e: e.iota(dif_i.t[:], pattern=[[1, 128]], base=0, channel_multiplier=-1), writes=[dif_i.b])
        P.op("pool", lambda e: e.tensor_copy(out=dif.t[:], in_=dif_i.t[:]), reads=[dif_i.b], writes=[dif.b])
        P.op("pool", lambda e: e.iota(qi_i.t[:], pattern=[[1, 128]], base=1, channel_multiplier=0), writes=[qi_i.b])
        P.op("pool", lambda e: e.tensor_copy(out=qi.t[:], in_=qi_i.t[:]), reads=[qi_i.b], writes=[qi.b])
        P.op("pool", lambda e: e.iota(kk_i.t[:], pattern=[[0, 1]], base=127, channel_multiplier=-1), writes=[kk_i.b])
        P.op("pool", lambda e: e.tensor_copy(out=kk.t[:], in_=kk_i.t[:]), reads=[kk_i.b], writes=[kk.b])
        P.op("dve", lambda e: e.tensor_scalar(out=dpos.t[:], in0=dif.t[:], scalar1=0.0, scalar2=None, op0=ALU.max),
             reads=[dif.b], writes=[dpos.b])
        P.op("dve", lambda e: e.tensor_scalar(out=dge.t[:], in0=dif.t[:], scalar1=0.0, scalar2=1.0 / 16.0, op0=ALU.is_ge, op1=ALU.mult),
             reads=[dif.b], writes=[dge.b])
        for h in range(H):
            P.op("act", lambda e, h=h: e.activation(out=maskT.t[:, h, :], in_=dpos.t[:], func=AF.Exp, scale=LG[h]),
                 reads=[dpos.b], writes=[maskT.b])
            P.op("act", lambda e, h=h: e.activation(out=qdec.t[:, h, :], in_=qi.t[:], func=AF.Exp, scale=LG[h]),
                 reads=[qi.b], writes=[qdec.b])
            P.op("act", lambda e, h=h: e.activation(out=kdec.t[:, h:h + 1], in_=kk.t[:], func=AF.Exp, scale=LG[h]),
                 reads=[kk.b], writes=[kdec.b])
        P.op("dve", lambda e: e.tensor_tensor(out=maskT.t[:], in0=maskT.t[:], in1=dge.t[:].unsqueeze(1).broadcast_to([128, H, 128]), op=ALU.mult),
             reads=[maskT.b, dge.b], writes=[maskT.b])
        P.op("dve", lambda e: e.tensor_scalar(out=kdec.t[:], in0=kdec.t[:], scalar1=1.0 / 16.0, scalar2=None, op0=ALU.mult),
             reads=[kdec.b], writes=[kdec.b])
        S = cx.sb("S", [128, 2 * H, DV], F32)
        Sb = cx.sb("Sb", [128, 2 * H, DV], BF16)
        Sbufs = [Buf("S%d" % i) for i in range(2 * H)]
        P.op("pool", lambda e: e.memset(S.t[:], 0.0), writes=Sbufs)
        P.op("pool", lambda e: e.memset(Sb.t[:], 0.0), writes=[Sb.b])
        NS = 2
        NTB = 4
        xt = [cx.sb("xt", [128, D], F32) for _ in range(NS)]
        tb = [cx.sb("tb", [128, 2, 128], F32) for _ in range(NTB)]
        junk = None
        ss = [cx.sb("ss", [128, 1], F32) for _ in range(2)]
        rstd = [cx.sb("rstd", [128, 1], F32) for _ in range(2)]
        hb = cx.sb("hb", [128, D], BF16)
        hT = [cx.sb("hT", [128, 8, 128], BF16) for _ in range(2)]
        t1 = cx.sb("t1", [128, 2, 128], F32)
        t2 = cx.sb("t2", [128, 2, 128], F32)
        t3 = cx.sb("t3", [128, 2, 128], F32)
        t4 = cx.sb("t4", [128, 2, 128], F32)
        ktm = [cx.sb("ktm", [128, H, 2, 128], BF16) for _ in range(1)]
        qtm = cx.sb("qtm", [128, H, 2, 128], BF16)
        qT = [cx.sb("qT", [128, H, 2, 128], BF16) for _ in range(2)]
        kT = [cx.sb("kT", [128, H, 2, 128], BF16) for _ in range(2)]
        qd = [cx.sb("qd", [128, H, 2, 128], BF16) for _ in range(2)]
        kd = [cx.sb("kd", [128, H, 2 * 128], BF16) for _ in range(2)]
        vb = [cx.sb("vb", [128, VW], BF16) for _ in range(2)]
        sg = cx.sb("sg", [128, VW], F32)
        sT = cx.sb("sT", [128, H, 128], BF16)
        stats = cx.sb("stats", [128, H, 6], F32)
        mv = cx.sb("mv", [128, H, 2], F32)
        var = cx.sb("var", [128, H], F32)
        rs = cx.sb("rs", [128, H], F32)
        nb = cx.sb("nb", [128, H], F32)
        on = [cx.sb("on", [128, DV], F32) for _ in range(2)]
        og = [cx.sb("og", [128, VW], BF16) for _ in range(NS)]
        p_tr = cx.ps("p_tr", [128, 8, 128], BF16)
        p_qa = cx.ps("p_qa", [128, H, 128], F32)
        p_qb = cx.ps("p_qb", [128, H, 128], F32)
        p_ka = cx.ps("p_ka", [128, H, 128], F32)
        p_kb = cx.ps("p_kb", [128, H, 128], F32)
        p_w = [cx.ps("p_w", [128, 512], F32) for _ in range(2)]
        p_sc = cx.ps("p_sc", [128, H, 128], F32)
        p_o = [p_qa, p_qb, p_ka, p_kb]
        ogB = Buf("ogd")
        cosb = lambda tbt: tbt.t[:, 0, :].unsqueeze(1).broadcast_to([128, H, 128])
        sinb = lambda tbt: tbt.t[:, 1, :].unsqueeze(1).broadcast_to([128, H, 128])

        def stL(c):
            tok = slice(c * C, (c + 1) * C)
            x_ = xt[c % NS]
            tb_ = tb[c % NTB]
            P.dma("act", lambda e: e.dma_start(out=x_.t[:], in_=xf[tok, :]), writes=[x_.b])
            P.dma("act", lambda e: e.dma_start(out=tb_.t[:], in_=tabs[tok, :, :]), writes=[tb_.b])

        def stA(c):
            x_ = xt[c % NS]
            s2 = c % 2
            P.op("act", lambda e: e.activation(out=hb.t[:], in_=x_.t[:], func=AF.Square, accum_out=ss[s2].t[:]),
                 reads=[x_.b], writes=[hb.b, ss[s2].b])
            emit_rstd(P, ss[s2], rstd[s2], cm05, D)
            P.op("dve", lambda e: e.scalar_tensor_tensor(out=hb.t[:], in0=x_.t[:], scalar=rstd[s2].t[:, 0:1], in1=g0b.t[:],
                                                         op0=ALU.mult, op1=ALU.mult),
                 reads=[x_.b, rstd[s2].b, g0b.b], writes=[hb.b])

        def stB_g(c, full, hT_):
            if full:
                for n in range(4):
                    pw = p_w[n % 2]

                    def mm_g(e, n=n, pw=pw):
                        for k in range(8):
                            ins = e.matmul(out=pw.t[:], lhsT=hT_.t[:, k, :],
                                           rhs=wt[:, k, 2 * QKW + VW + n * 512:2 * QKW + VW + (n + 1) * 512],
                                           start=(k == 0), stop=(k == 7))
                        return ins
                    P.op("pe", mm_g, reads=[hT_.b] + wb, writes=[pw.b])
                    P.op("act", lambda e, n=n, pw=pw: e.activation(out=sg.t[:, n * 512:(n + 1) * 512], in_=pw.t[:], func=AF.Silu),
                         reads=[pw.b], writes=[sg.b])


        def stB_t(c, full, ktm_, qtm_, kT_, qT_, qd_):
            if full:
                def tr_q(e):
                    for h in range(H):
                        for half in range(2):
                            ins = e.transpose(out=p_tr.t[:, h * 2 + half, :], in_=qtm_.t[:, h, half, :], identity=ident.t[:])
                    return ins
                P.op("pe", tr_q, reads=[qtm_.b, ident.b], writes=[p_tr.b])
                P.op("act", lambda e: e.copy(out=qT_.t[:].rearrange("p h a t -> p (h a) t"), in_=p_tr.t[:]), reads=[p_tr.b], writes=[qT_.b])
                P.op("pool", lambda e: e.tensor_tensor(out=qd_.t[:], in0=qT_.t[:],
                                                       in1=qdec.t[:].unsqueeze(2).broadcast_to([128, H, 2, 128]), op=ALU.mult),
                     reads=[qT_.b, qdec.b], writes=[qd_.b])

                def tr_k(e):
                    for h in range(H):
                        for half in range(2):
                            ins = e.transpose(out=p_tr.t[:, h * 2 + half, :], in_=ktm_.t[:, h, half, :], identity=ident.t[:])
                    return ins
                P.op("pe", tr_k, reads=[ktm_.b, ident.b], writes=[p_tr.b])
                P.op("dve", lambda e: e.tensor_copy(out=kT_.t[:].rearrange("p h a t -> p (h a) t"), in_=p_tr.t[:]), reads=[p_tr.b], writes=[kT_.b])


        def stA2(c):
            s2 = c % 2

            def tr_h(e):
                for k in range(8):
                    ins = e.transpose(out=p_tr.t[:, k, :], in_=hb.t[:, k * 128:(k + 1) * 128], identity=ident.t[:])
                return ins
            P.op("pe", tr_h, reads=[hb.b, ident.b], writes=[p_tr.b])
            P.op("act", lambda e: e.copy(out=hT[s2].t[:], in_=p_tr.t[:]), reads=[p_tr.b], writes=[hT[s2].b])

        def stB(c, part):
            full = c >= HALO
            s2 = c % 2
            tb_ = tb[c % NTB]
            hT_ = hT[s2]
            kT_, qT_, qd_, kd_, vb_ = kT[s2], qT[s2], qd[s2], kd[s2], vb[s2]

            cos2 = tb_.t[:, 0, :].unsqueeze(1).broadcast_to([128, 2, 128])
            sin2 = tb_.t[:, 1, :].unsqueeze(1).broadcast_to([128, 2, 128])

            def proj_tm(n, pp):
                def f(e):
                    for k in range(8):
                        ins = e.matmul(out=pp.t[:].rearrange("p a b -> p (a b)"), lhsT=hT_.t[:, k, :], rhs=wt[:, k, n * 512:(n + 1) * 512],
                                       start=(k == 0), stop=(k == 7))
                    return ins
                return f

            def rope_tm(pp, dst, hh0):
                pv = pp.t[:].rearrange("p a b -> p (a b)").rearrange("p (h a j) -> p h a j", h=2, a=2)
                A = pv[:, :, 0, :]
                B = pv[:, :, 1, :]
                P.op("dve", lambda e: e.tensor_tensor(out=t1.t[:, 0:2, :], in0=A, in1=cos2, op=ALU.mult), reads=[pp.b, tb_.b], writes=[t1.b])
                P.op("dve", lambda e: e.tensor_tensor(out=t2.t[:, 0:2, :], in0=B, in1=sin2, op=ALU.mult), reads=[pp.b, tb_.b], writes=[t2.b])
                P.op("pool", lambda e: e.tensor_tensor(out=dst.t[:, hh0:hh0 + 2, 0, :], in0=t1.t[:, 0:2, :], in1=t2.t[:, 0:2, :], op=ALU.subtract),
                     reads=[t1.b, t2.b], writes=[dst.b])
                P.op("dve", lambda e: e.tensor_tensor(out=t3.t[:, 0:2, :], in0=A, in1=sin2, op=ALU.mult), reads=[pp.b, tb_.b], writes=[t3.b])
                P.op("dve", lambda e: e.tensor_tensor(out=t4.t[:, 0:2, :], in0=B, in1=cos2, op=ALU.mult), reads=[pp.b, tb_.b], writes=[t4.b])
                P.op("pool", lambda e: e.tensor_tensor(out=dst.t[:, hh0:hh0 + 2, 1, :], in0=t3.t[:, 0:2, :], in1=t4.t[:, 0:2, :], op=ALU.add),
                     reads=[t3.b, t4.b], writes=[dst.b])

            ktm_, qtm_ = ktm[0], qtm
            if part == "g":
                stB_g(c, full, hT_)
                return
            if part == "t":
                stB_t(c, full, ktm_, qtm_, kT_, qT_, qd_)
                return
            P.op("pe", proj_tm(2, p_ka), reads=[hT_.b] + wb, writes=[p_ka.b])
            P.op("pe", proj_tm(3, p_kb), reads=[hT_.b] + wb, writes=[p_kb.b])
            if full:
                P.op("pe", proj_tm(0, p_qa), reads=[hT_.b] + wb, writes=[p_qa.b])
                P.op("pe", proj_tm(1, p_qb), reads=[hT_.b] + wb, writes=[p_qb.b])
            rope_tm(p_ka, ktm_, 0)
            rope_tm(p_kb, ktm_, 2)
            P.op("dve", lambda e: e.tensor_tensor(out=kd_.t[:], in0=ktm_.t[:].rearrange("p h a j -> p h (a j)"),
                                                  in1=kdec.t[:].unsqueeze(2).broadcast_to([128, H, 256]), op=ALU.mult),
                 reads=[ktm_.b, kdec.b], writes=[kd_.b])
            if full:
                rope_tm(p_qa, qtm_, 0)
                rope_tm(p_qb, qtm_, 2)
            for n in range(4):
                pw = p_w[n % 2]

                def mm_v(e, n=n, pw=pw):
                    for k in range(8):
                        ins = e.matmul(out=pw.t[:], lhsT=hT_.t[:, k, :], rhs=wt[:, k, 2 * QKW + n * 512:2 * QKW + (n + 1) * 512],
                                       start=(k == 0), stop=(k == 7))
                    return ins
                P.op("pe", mm_v, reads=[hT_.b] + wb, writes=[pw.b])
                P.op("act", lambda e, n=n, pw=pw: e.copy(out=vb_.t[:, n * 512:(n + 1) * 512], in_=pw.t[:]), reads=[pw.b], writes=[vb_.b])
        def stC(c):
            full = c >= HALO
            s2 = c % 2
            s = c % NS
            kT_, qT_, qd_, kd_, vb_ = kT[s2], qT[s2], qd[s2], kd[s2], vb[s2]
            if full:
                def mm_sc(e):
                    for h in range(H):
                        for half in range(2):
                            ins = e.matmul(out=p_sc.t[:, h, :], lhsT=kT_.t[:, h, half, :], rhs=qT_.t[:, h, half, :],
                                           start=(half == 0), stop=(half == 1))
                    return ins
                P.op("pe", mm_sc, reads=[kT_.b, qT_.b], writes=[p_sc.b])
                P.op("dve", lambda e: e.tensor_tensor(out=sT.t[:], in0=p_sc.t[:], in1=maskT.t[:], op=ALU.mult),
                     reads=[p_sc.b, maskT.b], writes=[sT.b])
                for h in range(H):
                    def mm_o(e, h=h):
                        e.matmul(out=p_o[h].t[:].rearrange("p a b -> p (a b)"), lhsT=sT.t[:, h, :], rhs=vb_.t[:, h * DV:(h + 1) * DV],
                                 start=True, stop=False)
                        e.matmul(out=p_o[h].t[:].rearrange("p a b -> p (a b)"), lhsT=qd_.t[:, h, 0, :], rhs=Sb.t[:, 2 * h, :],
                                 start=False, stop=False)
                        return e.matmul(out=p_o[h].t[:].rearrange("p a b -> p (a b)"), lhsT=qd_.t[:, h, 1, :], rhs=Sb.t[:, 2 * h + 1, :],
                                        start=False, stop=True)
                    P.op("pe", mm_o, reads=[sT.b, vb_.b, qd_.b, Sb.b], writes=[p_o[h].b])
                    P.op("dve", lambda e, h=h: e.bn_stats(out=stats.t[:, h, :], in_=p_o[h].t[:].rearrange("p a b -> p (a b)")),
                         reads=[p_o[h].b], writes=[stats.b])
            if full:
                P.op("dve", lambda e: [e.bn_aggr(out=mv.t[:, h, :], in_=stats.t[:, h, :]) for h in range(H)][-1],
                     reads=[stats.b], writes=[mv.b])
                P.op("dve", lambda e: e.tensor_scalar(out=var.t[:], in0=mv.t[:, :, 1], scalar1=EPS, scalar2=None, op0=ALU.add),
                     reads=[mv.b], writes=[var.b])
                P.op("pool", lambda e: e.tensor_tensor(out=rs.t[:], in0=var.t[:], in1=cm05.t[:, 0:H], op=ALU.pow),
                     reads=[var.b, cm05.b], writes=[rs.b])
                P.op("dve", lambda e: e.scalar_tensor_tensor(out=nb.t[:], in0=mv.t[:, :, 0], scalar=-1.0, in1=rs.t[:],
                                                             op0=ALU.mult, op1=ALU.mult),
                     reads=[mv.b, rs.b], writes=[nb.b])
                for h in range(H):
                    o_n = on[h % 2]
                    P.op("act", lambda e, h=h, o_n=o_n: e.activation(out=o_n.t[:], in_=p_o[h].t[:].rearrange("p a b -> p (a b)"),
                                                                     func=AF.Identity, scale=rs.t[:, h:h + 1], bias=nb.t[:, h:h + 1]),
                         reads=[p_o[h].b, rs.b, nb.b], writes=[o_n.b])
                    P.op("pool", lambda e, h=h, o_n=o_n: e.tensor_tensor(out=og[s].t[:, h * DV:(h + 1) * DV], in0=o_n.t[:],
                                                                         in1=sg.t[:, h * DV:(h + 1) * DV], op=ALU.mult),
                         reads=[o_n.b, sg.b], writes=[og[s].b])
                P.dma("sp", lambda e: e.dma_start(out=ogd[c - HALO, :, :], in_=og[s].t[:]), reads=[og[s].b], writes=[ogB], key=og[s].b)

        def stC2(c):
            s2 = c % 2
            kd_, vb_ = kd[s2], vb[s2]
            for h in range(H):
                for half in range(2):
                    i = 2 * h + half
                    pw = p_w[i % 2]

                    def mm_s(e, h=h, half=half, pw=pw):
                        return e.matmul(out=pw.t[:], lhsT=kd_.t[:, h, half * 128:(half + 1) * 128], rhs=vb_.t[:, h * DV:(h + 1) * DV],
                                        start=True, stop=True)
                    P.op("pe", mm_s, reads=[kd_.b, vb_.b], writes=[pw.b])
                    P.op("dve", lambda e, i=i, h=h, pw=pw: e.scalar_tensor_tensor(out=S.t[:, i, :], in0=S.t[:, i, :],
                                                                                  scalar=math.exp(C * LG[h]), in1=pw.t[:],
                                                                                  op0=ALU.mult, op1=ALU.add),
                         reads=[Sbufs[i], pw.b], writes=[Sbufs[i]])
            P.op("act", lambda e: e.copy(out=Sb.t[:], in_=S.t[:]), reads=Sbufs, writes=[Sb.b])

        n_ = c_end - c_begin
        lo_ = c_begin
        ok_ = lambda c: lo_ <= c < lo_ + n_
        for it in range(lo_, lo_ + n_ + 3):
            if ok_(it - 1):
                stA(it - 1)
            if ok_(it - 2):
                stB(it - 2, "kqv")
            if ok_(it - 1):
                stA2(it - 1)
            if ok_(it - 3):
                stC(it - 3)
            if ok_(it - 2):
                stB(it - 2, "t")
            if ok_(it - 3):
                stC2(it - 3)
            if ok_(it - 2):
                stB(it - 2, "g")
            if ok_(it):
                stL(it)
        qT, kT = qT[(c_end - 1) % 2], kT[(c_end - 1) % 2]
        if dbg is not None:
            dB = Buf("dbg")
            P.dma("sp", lambda e: e.dma_start(out=dbg["qT"][:, :], in_=qT.t[:].rearrange("p h a t -> p (h a t)")), reads=[qT.b], writes=[dB], key=qT.b)
            P.dma("sp", lambda e: e.dma_start(out=dbg["kT"][:, :], in_=kT.t[:].rearrange("p h a t -> p (h a t)")), reads=[kT.b], writes=[dB], key=kT.b)
            P.dma("sp", lambda e: e.dma_start(out=dbg["S"][:, :], in_=S.t[:].rearrange("p i e -> p (i e)")), reads=Sbufs, writes=[dB], key=S.b)
        P.emit()


def emit_post_norm_residual(P, pm, xres, gb, outt, junk, ss, rstd, cm05, tmp):
    def sq(e):
        e.activation(out=tmp.t[:, 0:512], in_=pm[0].t[:], func=AF.Square, accum_out=ss.t[:, 0:1])
        return e.activation(out=tmp.t[:, 512:1024], in_=pm[1].t[:], func=AF.Square, accum_out=ss.t[:, 1:2])
    P.op("act", sq, reads=[pm[0].b, pm[1].b], writes=[tmp.b, ss.b])
    P.op("dve", lambda e: e.tensor_tensor(out=ss.t[:, 2:3], in0=ss.t[:, 0:1], in1=ss.t[:, 1:2], op=ALU.add),
         reads=[ss.b], writes=[ss.b])
    P.op("dve", lambda e: e.tensor_scalar(out=rstd.t[:], in0=ss.t[:, 2:3], scalar1=1.0 / D, scalar2=EPS,
                                          op0=ALU.mult, op1=ALU.add), reads=[ss.b], writes=[rstd.b])
    P.op("pool", lambda e: e.tensor_tensor(out=rstd.t[:], in0=rstd.t[:], in1=cm05.t[:, 0:1], op=ALU.pow),
         reads=[rstd.b, cm05.b], writes=[rstd.b])
    for n in range(2):
        P.op("dve", lambda e, n=n: e.scalar_tensor_tensor(out=tmp.t[:, n * 512:(n + 1) * 512], in0=pm[n].t[:], scalar=rstd.t[:, 0:1],
                                                          in1=gb.t[:, n * 512:(n + 1) * 512], op0=ALU.mult, op1=ALU.mult),
             reads=[pm[n].b, rstd.b, gb.b], writes=[tmp.b])
    P.op("pool", lambda e: e.tensor_tensor(out=outt.t[:], in0=tmp.t[:], in1=xres.t[:], op=ALU.add),
         reads=[tmp.b, xres.b], writes=[outt.b])


def phase_ret_out(nc, xf, ogd, w_out, g1, xmid, prefetch=None):
    with contextlib.ExitStack() as st:
        cx = Ctx(nc, st)
        P = Prog(nc)
        wt, wb = load_weight(P, cx, w_out, 16, D, "retwout")
        if prefetch is not None:
            prefetch(P)
        ident = make_ident(P, cx)
        g1b = cx.sb("g1b", [128, D], F32)
        P.dma("sp", lambda e: e.dma_start(out=g1b.t[:], in_=g1.partition_broadcast(128)), writes=[g1b.b])
        cm05 = cx.sb("cm05", [128, 8], F32)
        P.op("pool", lambda e: e.memset(cm05.t[:], -0.5), writes=[cm05.b])
        NX = 5
        xt = [cx.sb("xt", [128, D], F32) for _ in range(NX)]
        ogt = [cx.sb("ogt", [128, VW], BF16) for _ in range(4)]
        ogT = [cx.sb("ogT", [128, 16, 128], BF16) for _ in range(3)]
        junk = None
        ss = [cx.sb("ss", [128, 4], F32) for _ in range(2)]
        rstd = [cx.sb("rstd", [128, 1], F32) for _ in range(2)]
        tmp = [cx.sb("tmp", [128, D], F32) for _ in range(2)]
        xo = [cx.sb("xo", [128, D], F32) for _ in range(2)]
        p_tr = [cx.ps("p_tr", [128, 8, 128], BF16) for _ in range(2)]
        p_m = [[cx.ps("p_m", [128, 512], F32) for _ in range(2)] for _ in range(2)]
        xmB = Buf("xmid")

        def stL(c):
            tok = slice((HALO + c) * C, (HALO + c + 1) * C)
            x_ = xt[c % NX]
            o_ = ogt[c % 4]
            P.dma("act", lambda e: e.dma_start(out=x_.t[:], in_=xf[tok, :]), writes=[x_.b])
            P.dma("act", lambda e: e.dma_start(out=o_.t[:], in_=ogd[c, :, :]), writes=[o_.b])

        def stA(c):
            o_ = ogt[c % 4]
            oT = ogT[c % 3]
            for half in range(2):
                def tr(e, half=half):
                    for k in range(8):
                        kk = half * 8 + k
                        ins = e.transpose(out=p_tr[half].t[:, k, :], in_=o_.t[:, kk * 128:(kk + 1) * 128], identity=ident.t[:])
                    return ins
                P.op("pe", tr, reads=[o_.b, ident.b], writes=[p_tr[half].b])
                if half == 0:
                    P.op("act", lambda e: e.copy(out=oT.t[:, 0:8, :], in_=p_tr[0].t[:]), reads=[p_tr[0].b], writes=[oT.b])
                else:
                    P.op("dve", lambda e: e.tensor_copy(out=oT.t[:, 8:16, :], in_=p_tr[1].t[:]), reads=[p_tr[1].b], writes=[oT.b])

        def stB(c):
            s = c % 2
            oT = ogT[c % 3]
            pm = p_m[s]
            for n in range(2):
                def mm(e, n=n):
                    for k in range(16):
                        ins = e.matmul(out=pm[n].t[:], lhsT=oT.t[:, k, :], rhs=wt[:, k, n * 512:(n + 1) * 512],
                                       start=(k == 0), stop=(k == 15))
                    return ins
                P.op("pe", mm, reads=[oT.b] + wb, writes=[pm[n].b])
            emit_post_norm_residual(P, pm, xt[c % NX], g1b, xo[s], junk, ss[s], rstd[s], cm05, tmp[s])
            P.dma("sp", lambda e: e.dma_start(out=xmid[c, :, :], in_=xo[s].t[:]), reads=[xo[s].b], writes=[xmB], key=xo[s].b)

        run_pipeline(NEXT, [stL, stA, (lambda c: None), stB])
        P.emit()


def phase_ffn(nc, xin, w_in, w_out, g2, g3, xout_fn, c0, c1, pre_wi=None):
    with contextlib.ExitStack() as st:
        cx = Ctx(nc, st)
        P = Prog(nc)
        if pre_wi is not None:
            wi, wib = pre_wi, []
        else:
            wi, wib = load_weight(P, cx, w_in, 8, 2 * DFF, "ffnwin")
        wo, wob = load_weight(P, cx, w_out, 22, D, "ffnwout")
        ident = make_ident(P, cx)
        g2b = cx.sb("g2b", [128, D], F32)
        g3b = cx.sb("g3b", [128, D], F32)
        P.dma("sp", lambda e: e.dma_start(out=g2b.t[:], in_=g2.partition_broadcast(128)), writes=[g2b.b])
        P.dma("sp", lambda e: e.dma_start(out=g3b.t[:], in_=g3.partition_broadcast(128)), writes=[g3b.b])
        cm05 = cx.sb("cm05", [128, 8], F32)
        P.op("pool", lambda e: e.memset(cm05.t[:], -0.5), writes=[cm05.b])
        NX = 5
        xt = [cx.sb("xt", [128, D], F32) for _ in range(NX)]
        junk = None
        ss = cx.sb("ss", [128, 4], F32)
        ss1 = [cx.sb("ss1", [128, 1], F32) for _ in range(2)]
        rstd = cx.sb("rstd", [128, 1], F32)
        rstd2 = [cx.sb("rstd2", [128, 1], F32) for _ in range(2)]
        hb = [cx.sb("hb", [128, D], BF16) for _ in range(2)]
        hT = [cx.sb("hT", [128, 8, 128], BF16) for _ in range(2)]
        sgl = [cx.sb("sgl", [128, 512], F32) for _ in range(2)]
        ab = [cx.sb("ab", [128, DFF], BF16) for _ in range(2)]
        aT = cx.sb("aT", [128, 22, 128], BF16)
        tmp = cx.sb("tmp", [128, D], F32)
        xo = [cx.sb("xo", [128, D], F32) for _ in range(2)]
        p_tr = [cx.ps("p_tr", [128, 8, 128], BF16) for _ in range(2)]
        p_g = [cx.ps("p_g", [128, 512], F32) for _ in range(2)]
        p_u = [cx.ps("p_u", [128, 512], F32) for _ in range(2)]
        p_f = [cx.ps("p_f", [128, 512], F32) for _ in range(2)]
        xoB = Buf("xout")
        tiles = [(j * 512, 512) for j in range(5)] + [(2560, 256)]

        def stL(c):
            x_ = xt[c % NX]
            P.dma("act", lambda e: e.dma_start(out=x_.t[:], in_=xin[c, :, :]), writes=[x_.b])

        def stA(c):
            x_ = xt[c % NX]
            s = c % 2
            P.op("act", lambda e: e.activation(out=hb[s].t[:], in_=x_.t[:], func=AF.Square, accum_out=ss1[s].t[:]),
                 reads=[x_.b], writes=[hb[s].b, ss1[s].b])
            emit_rstd(P, ss1[s], rstd2[s], cm05, D)
            P.op("dve", lambda e: e.scalar_tensor_tensor(out=hb[s].t[:], in0=x_.t[:], scalar=rstd2[s].t[:, 0:1], in1=g2b.t[:],
                                                         op0=ALU.mult, op1=ALU.mult),
                 reads=[x_.b, rstd2[s].b, g2b.b], writes=[hb[s].b])

        def stA2(c):
            s = c % 2

            def tr_h(e):
                for k in range(8):
                    ins = e.transpose(out=p_tr[0].t[:, k, :], in_=hb[s].t[:, k * 128:(k + 1) * 128], identity=ident.t[:])
                return ins
            P.op("pe", tr_h, reads=[hb[s].b, ident.b], writes=[p_tr[0].b])
            P.op("act", lambda e: e.copy(out=hT[s].t[:], in_=p_tr[0].t[:]), reads=[p_tr[0].b], writes=[hT[s].b])

        def stB(c):
            s = c % 2
            for j, (c0h, wd) in enumerate(tiles):
                pg = p_g[j % 2]
                pu = p_u[j % 2]
                sl_ = sgl[j % 2]

                def mm_gu(e, c0h=c0h, wd=wd, pg=pg, pu=pu):
                    for k in range(8):
                        e.matmul(out=pg.t[:, 0:wd], lhsT=hT[s].t[:, k, :], rhs=wi[:, k, c0h:c0h + wd], start=(k == 0), stop=(k == 7))
                    for k in range(8):
                        ins = e.matmul(out=pu.t[:, 0:wd], lhsT=hT[s].t[:, k, :], rhs=wi[:, k, DFF + c0h:DFF + c0h + wd],
                                       start=(k == 0), stop=(k == 7))
                    return ins
                P.op("pe", mm_gu, reads=[hT[s].b] + wib, writes=[pg.b, pu.b])
                P.op("act", lambda e, wd=wd, pg=pg, sl_=sl_: e.activation(out=sl_.t[:, 0:wd], in_=pg.t[:, 0:wd], func=AF.Silu),
                     reads=[pg.b], writes=[sl_.b])
                P.op("dve", lambda e, c0h=c0h, wd=wd, pu=pu, sl_=sl_: e.tensor_tensor(out=ab[s].t[:, c0h:c0h + wd], in0=pu.t[:, 0:wd],
                                                                                      in1=sl_.t[:, 0:wd], op=ALU.mult),
                     reads=[pu.b, sl_.b], writes=[ab[s].b])

        def stC(c):
            s = c % 2
            for gi, (k0, nk) in enumerate(((0, 8), (8, 8), (16, 6))):
                pt = p_tr[(gi + 1) % 2]

                def tr_a(e, k0=k0, nk=nk, pt=pt):
                    for k in range(nk):
                        ins = e.transpose(out=pt.t[:, k, :], in_=ab[s].t[:, (k0 + k) * 128:(k0 + k + 1) * 128], identity=ident.t[:])
                    return ins
                P.op("pe", tr_a, reads=[ab[s].b, ident.b], writes=[pt.b])
                if gi % 2 == 0:
                    P.op("act", lambda e, k0=k0, nk=nk, pt=pt: e.copy(out=aT.t[:, k0:k0 + nk, :], in_=pt.t[:, 0:nk, :]),
                         reads=[pt.b], writes=[aT.b])
                else:
                    P.op("dve", lambda e, k0=k0, nk=nk, pt=pt: e.tensor_copy(out=aT.t[:, k0:k0 + nk, :], in_=pt.t[:, 0:nk, :]),
                         reads=[pt.b], writes=[aT.b])

        def stC2(c):
            s = c % 2
            for n in range(2):
                def mm_f(e, n=n):
                    for k in range(22):
                        ins = e.matmul(out=p_f[n].t[:], lhsT=aT.t[:, k, :], rhs=wo[:, k, n * 512:(n + 1) * 512],
                                       start=(k == 0), stop=(k == 21))
                    return ins
                P.op("pe", mm_f, reads=[aT.b] + wob, writes=[p_f[n].b])
            emit_post_norm_residual(P, p_f, xt[c % NX], g3b, xo[s], junk, ss, rstd, cm05, tmp)
            P.dma("sp", lambda e: e.dma_start(out=xout_fn(c), in_=xo[s].t[:]), reads=[xo[s].b], writes=[xoB], key=xo[s].b)

        n_ = c1 - c0
        for it in range(c0, c0 + n_ + 3):
            if c0 <= it - 3 < c0 + n_:
                stC(it - 3)
            if c0 <= it - 1 < c0 + n_:
                stA(it - 1)
            if c0 <= it - 2 < c0 + n_:
                stB(it - 2)
            if c0 <= it - 1 < c0 + n_:
                stA2(it - 1)
            if c0 <= it - 3 < c0 + n_:
                stC2(it - 3)
            if c0 <= it < c0 + n_:
                stL(it)
        P.emit()


def phase_lru(nc, x1, w_in, conv_w, conv_b, gate_w, gate_b, a_param, g0, pqd, hend, dbg=None, ntiles=None):
    T = 256
    NT = HALF // T
    if ntiles is not None:
        NT = ntiles
    NCH = LW // 128
    with contextlib.ExitStack() as st:
        cx = Ctx(nc, st)
        P = Prog(nc)
        st.enter_context(nc.allow_non_contiguous_dma(reason="tiny per-channel parameter vectors"))
        wt, wb = load_weight(P, cx, w_in, 8, 2 * LW, "lruwin")
        Ctx.gn += 1
        gw = st.enter_context(nc.sbuf_tensor("gw_%d" % Ctx.gn, [128, 24, 256], BF16))
        gwB = Buf("gw")
        P.dma("pool", lambda e: e.dma_start(out=gw[:, :, :], in_=gate_w.rearrange("g n (ki p) j -> p (g n ki) j", p=128)), writes=[gwB])
        ident = make_ident(P, cx)
        g0b = cx.sb("g0b", [128, D], F32)
        P.dma("sp", lambda e: e.dma_start(out=g0b.t[:], in_=g0.partition_broadcast(128)), writes=[g0b.b])
        cm05 = cx.sb("cm05", [128, 8], F32)
        P.op("pool", lambda e: e.memset(cm05.t[:], -0.5), writes=[cm05.b])
        cw = cx.sb("cw", [128, 4, NCH], F32)
        cb = cx.sb("cb", [128, NCH], F32)
        gb = cx.sb("gb", [128, 2, NCH], F32)
        ap_ = cx.sb("ap", [128, NCH], F32)
        cv = cx.sb("cv", [128, NCH], F32)
        cv2 = cx.sb("cv2", [128, NCH], F32)
        for i in range(4):
            P.dma("sp", lambda e, i=i: e.dma_start(out=cw.t[:, i, :], in_=conv_w[i:i + 1, :].rearrange("o (c p) -> p (o c)", p=128)), writes=[cw.b])
        P.dma("sp", lambda e: e.dma_start(out=cb.t[:], in_=conv_b.rearrange("o (c p) -> p (o c)", p=128)), writes=[cb.b])
        P.dma("sp", lambda e: e.dma_start(out=ap_.t[:], in_=a_param.rearrange("o (c p) -> p (o c)", p=128)), writes=[ap_.b])
        for g in range(2):
            P.dma("sp", lambda e, g=g: e.dma_start(out=gb.t[:, g, :], in_=gate_b[g:g + 1, :].rearrange("o (c p) -> p (o c)", p=128)), writes=[gb.b])
        P.op("act", lambda e: e.activation(out=cv.t[:], in_=ap_.t[:], func=AF.Exp, scale=-1.0), reads=[ap_.b], writes=[cv.b])
        P.op("act", lambda e: e.activation(out=cv.t[:], in_=cv.t[:], func=AF.Ln, bias=1.0), reads=[cv.b], writes=[cv.b])
        P.op("dve", lambda e: e.tensor_scalar(out=cv2.t[:], in0=cv.t[:], scalar1=-16.0, scalar2=None, op0=ALU.mult), reads=[cv.b], writes=[cv2.b])
        P.op("dve", lambda e: e.tensor_scalar(out=cv.t[:], in0=cv.t[:], scalar1=-8.0, scalar2=None, op0=ALU.mult), reads=[cv.b, cv2.b], writes=[cv.b])
        hst = cx.sb("hst", [128, NCH], F32)
        Ast = cx.sb("Ast", [128, NCH], F32)
        hstB = [Buf("hst%d" % c) for c in range(NCH)]
        AstB = [Buf("Ast%d" % c) for c in range(NCH)]
        P.op("pool", lambda e: e.memset(hst.t[:], 0.0), writes=hstB)
        P.op("pool", lambda e: e.memset(Ast.t[:], 1.0), writes=AstB)
        NX = 4
        NH = 3
        xt = [cx.sb("xt", [128, D], F32) for _ in range(NX)]
        junk = None
        ss = [cx.sb("ss", [128, 1], F32) for _ in range(2)]
        rstd = [cx.sb("rstd", [128, 1], F32) for _ in range(2)]
        hb2 = [cx.sb("hb", [128, D], BF16) for _ in range(2)]
        hT = [cx.sb("hT", [128, 8, T], BF16) for _ in range(NH)]
        yb = cx.sb("yb", [128, NCH, T], F32)
        ub = cx.sb("ub", [128, NCH, T + 3], F32)
        uc = cx.sb("uc", [128, NCH, T], F32)
        ucb = cx.sb("ucb", [128, NCH, T], BF16)
        rt = cx.sb("rt", [128, NCH, T], F32)
        it2 = [cx.sb("it", [128, NCH, T], F32) for _ in range(2)]
        at2 = [cx.sb("at", [128, NCH, T], F32) for _ in range(2)]
        hs = [cx.sb("hs", [128, T], F32) for _ in range(2)]
        cA = [cx.sb("cA", [128, T], F32) for _ in range(2)]
        NPQ = 2
        Pc = [cx.sb("Pc", [128, T], F32) for _ in range(NPQ)]
        Qc = [cx.sb("Qc", [128, T], F32) for _ in range(NPQ)]
        ybB = [Buf("yb%d" % c) for c in range(NCH)]
        ucB = [Buf("uc%d" % c) for c in range(NCH)]
        rB = [Buf("r%d" % c) for c in range(NCH)]
        iB = [[Buf("i%d_%d" % (p_, c)) for c in range(NCH)] for p_ in range(2)]
        aB = [[Buf("a%d_%d" % (p_, c)) for c in range(NCH)] for p_ in range(2)]
        p_tr2 = [cx.ps("p_tr", [128, 8, 128], BF16) for _ in range(2)]
        NPP = 4

        class PV:
            def __init__(self, tl):
                self.tl, self.b = tl, tl.b

            def sl(self, a, b_):
                return self.tl.t[:, a:b_]
        p_p = [PV(cx.ps("p_p", [128, 512], F32)) for k in range(NPP)]
        p_g = [cx.ps("p_g", [128, 512], F32) for _ in range(2)]
        pqB = Buf("pqd")
        P.op("pool", lambda e: e.memset(ub.t[:], 0.0), writes=[ub.b])
        nrm = [0]

        def load_x(cidx):
            x_ = xt[cidx % NX]
            P.dma("act", lambda e: e.dma_start(out=x_.t[:], in_=x1[cidx, :, :]), writes=[x_.b])

        def norm_a(cidx, slot):
            x_ = xt[cidx % NX]
            hb = hb2[slot]
            p_tr = p_tr2[slot]
            s2 = nrm[0] % 2
            nrm[0] += 1
            P.op("act", lambda e: e.activation(out=hb.t[:], in_=x_.t[:], func=AF.Square, accum_out=ss[s2].t[:]),
                 reads=[x_.b], writes=[hb.b, ss[s2].b])
            emit_rstd(P, ss[s2], rstd[s2], cm05, D)
            P.op("dve", lambda e: e.scalar_tensor_tensor(out=hb.t[:], in0=x_.t[:], scalar=rstd[s2].t[:, 0:1], in1=g0b.t[:],
                                                         op0=ALU.mult, op1=ALU.mult),
                 reads=[x_.b, rstd[s2].b, g0b.b], writes=[hb.b])

            def tr_h(e):
                for k in range(8):
                    ins = e.transpose(out=p_tr.t[:, k, :], in_=hb.t[:, k * 128:(k + 1) * 128], identity=ident.t[:])
                return ins
            P.op("pe", tr_h, reads=[hb.b, ident.b], writes=[p_tr.b])

        def norm_b(slot, hT_, dst_col):
            p_tr = p_tr2[slot]
            P.op("act", lambda e: e.copy(out=hT_.t[:, :, dst_col:dst_col + 128], in_=p_tr.t[:]), reads=[p_tr.b], writes=[hT_.b])

        def norm_T(cidx, hT_, dst_col):
            norm_a(cidx, 0)
            norm_b(0, hT_, dst_col)

        def proj(fc, n, pp, hT_):
            def f(e):
                for k in range(8):
                    ins = e.matmul(out=pp.sl(0, n), lhsT=wt[:, k, fc * 128:(fc + 1) * 128], rhs=hT_.t[:, k, 0:n],
                                   start=(k == 0), stop=(k == 7))
                return ins
            return f

        load_x(0)
        norm_T(0, hT[NH - 1], 0)
        for c in range(NCH):
            pp = p_p[c % NPP]
            P.op("pe", proj(NCH + c, 128, pp, hT[NH - 1]), reads=[hT[NH - 1].b] + wb, writes=[pp.b])
            P.op("act", lambda e, c=c, pp=pp: e.copy(out=ub.t[:, c, 0:3], in_=pp.sl(125, 128)), reads=[pp.b], writes=[ub.b])

        def stL(ti):
            for ci in range(T // 128):
                load_x(1 + ti * (T // 128) + ci)

        def stA_norm_a(ti):
            for ci in range(T // 128):
                norm_a(1 + ti * (T // 128) + ci, ci)

        def stA_norm_b(ti):
            hT_ = hT[ti % NH]
            for ci in range(T // 128):
                norm_b(ci, hT_, ci * 128)

        def stA_gen(ti):
            hT_ = hT[ti % NH]
            for c in range(NCH):
                pp = p_p[c % NPP]
                P.op("pe", proj(NCH + c, T, pp, hT_), reads=[hT_.b] + wb, writes=[pp.b])
                P.op("act", lambda e, c=c, pp=pp: e.copy(out=ub.t[:, c, 3:3 + T], in_=pp.sl(0, T)),
                     reads=[pp.b], writes=[ub.b])
                yield

        GRP = [list(range(0, NCH // 2)), list(range(NCH // 2, NCH))]

        def conv_act(ti):
            for c in range(NCH):
                P.op("act", lambda e, c=c: e.activation(out=uc.t[:, c, :], in_=ub.t[:, c, 3:3 + T], func=AF.Identity,
                                                        scale=cw.t[:, 3, c:c + 1], bias=cb.t[:, c:c + 1]),
                     reads=[ub.b, cw.b, cb.b], writes=[ucB[c]])

        def y_gelu(ti):
            hT_ = hT[ti % NH]
            for c in range(NCH):
                pp = p_p[c % NPP]
                P.op("pe", proj(c, T, pp, hT_), reads=[hT_.b] + wb, writes=[pp.b])
                P.op("act", lambda e, c=c, pp=pp: e.activation(out=yb.t[:, c, :], in_=pp.sl(0, T), func=AF.Gelu_apprx_tanh),
                     reads=[pp.b], writes=[ybB[c]])

        def conv_gates(ti):
            it_, iB_ = it2[ti % 2], iB[ti % 2]
            for gi, grp in enumerate(GRP):
                for i in range(3):
                    for c in grp:
                        P.op("dve", lambda e, c=c, i=i: e.scalar_tensor_tensor(out=uc.t[:, c, :], in0=ub.t[:, c, i:i + T],
                                                                                scalar=cw.t[:, i, c:c + 1], in1=uc.t[:, c, :],
                                                                                op0=ALU.mult, op1=ALU.add),
                             reads=[ub.b, cw.b, ucB[c]], writes=[ucB[c]])
                c0g, c1g = grp[0], grp[-1] + 1
                P.op("dve", lambda e, c0g=c0g, c1g=c1g: e.tensor_copy(out=ucb.t[:, c0g:c1g, :], in_=uc.t[:, c0g:c1g, :]),
                     reads=[ucB[c] for c in grp], writes=[ucbB[gi]])
                if gi == 1:
                    P.op("pool", lambda e: e.tensor_copy(out=ub.t[:, :, 0:3], in_=ub.t[:, :, T:T + 3]), reads=[ub.b], writes=[ub.b])
                for g in range(2):
                    for c in grp:
                        n, jo = c // 2, c % 2
                        pg = p_g[(g * NCH + c) % 2]

                        def mm_g(e, g=g, n=n, jo=jo, pg=pg):
                            for ki in range(2):
                                ins = e.matmul(out=pg.t[:, 0:T], lhsT=gw[:, (g * 6 + n) * 2 + ki, jo * 128:(jo + 1) * 128],
                                               rhs=ucb.t[:, 2 * n + ki, :], start=(ki == 0), stop=(ki == 1))
                            return ins
                        P.op("pe", mm_g, reads=[ucbB[gi], gwB], writes=[pg.b])
                        dst, dB = (rt, rB) if g == 0 else (it_, iB_)
                        P.op("act", lambda e, g=g, c=c, pg=pg, dst=dst: e.activation(out=dst.t[:, c, :], in_=pg.t[:, 0:T], func=AF.Sigmoid,
                                                                                     bias=gb.t[:, g, c:c + 1]),
                             reads=[pg.b, gb.b], writes=[dB[c]])

        def exp_sqrt_gen(ti):
            at_, aB_ = at2[ti % 2], aB[ti % 2]
            for c in range(NCH):
                P.op("act", lambda e, c=c: e.activation(out=at_.t[:, c, :], in_=rt.t[:, c, :], func=AF.Exp, scale=cv.t[:, c:c + 1]),
                     reads=[rB[c], cv.b], writes=[aB_[c]])
                yield
                P.op("act", lambda e, c=c: e.activation(out=rt.t[:, c, :], in_=rt.t[:, c, :], func=AF.Exp, scale=cv2.t[:, c:c + 1]),
                     reads=[rB[c], cv2.b], writes=[rB[c]])
                yield
            for c in range(NCH):
                P.op("act", lambda e, c=c: e.activation(out=rt.t[:, c, :], in_=rt.t[:, c, :], func=AF.Sqrt, scale=-1.0, bias=1.0),
                     reads=[rB[c]], writes=[rB[c]])
                yield

        def interleave(ga, gb, ratio):
            da = db = False
            while not (da and db):
                for _ in range(ratio):
                    if not da:
                        try:
                            next(ga)
                        except StopIteration:
                            da = True
                if not db:
                    try:
                        next(gb)
                    except StopIteration:
                        db = True

        def empty_gen():
            return
            yield

        def b_mul(ti):
            it_, iB_ = it2[ti % 2], iB[ti % 2]
            for c in range(NCH):
                P.op("dve", lambda e, c=c: e.tensor_tensor(out=it_.t[:, c, :], in0=it_.t[:, c, :], in1=uc.t[:, c, :], op=ALU.mult),
                     reads=[iB_[c], ucB[c]], writes=[iB_[c]])
            for c in range(NCH):
                P.op("dve", lambda e, c=c: e.tensor_tensor(out=it_.t[:, c, :], in0=it_.t[:, c, :], in1=rt.t[:, c, :], op=ALU.mult),
                     reads=[iB_[c], rB[c]], writes=[iB_[c]])

        def scans(ti):
            it_, iB_ = it2[ti % 2], iB[ti % 2]
            at_, aB_ = at2[ti % 2], aB[ti % 2]
            for c in range(NCH):
                s2 = c % 2
                sq = c % NPQ
                P.op("dve", lambda e, c=c, s2=s2: e.tensor_tensor_scan(out=hs[s2].t[:], data0=at_.t[:, c, :], data1=it_.t[:, c, :],
                                                                       initial=hst.t[:, c:c + 1], op0=ALU.mult, op1=ALU.add),
                     reads=[aB_[c], iB_[c], hstB[c]], writes=[hs[s2].b])
                P.op("dve", lambda e, c=c, s2=s2: e.tensor_tensor_scan(out=cA[s2].t[:], data0=at_.t[:, c, :], data1=at_.t[:, c, :],
                                                                       initial=Ast.t[:, c:c + 1], op0=ALU.mult, op1=ALU.min),
                     reads=[aB_[c], AstB[c]], writes=[cA[s2].b])
                P.op("pool", lambda e, c=c, s2=s2: e.tensor_copy(out=hst.t[:, c:c + 1], in_=hs[s2].t[:, T - 1:T]), reads=[hs[s2].b], writes=[hstB[c]])
                P.op("pool", lambda e, c=c, s2=s2: e.tensor_copy(out=Ast.t[:, c:c + 1], in_=cA[s2].t[:, T - 1:T]), reads=[cA[s2].b], writes=[AstB[c]])
                P.op("dve", lambda e, c=c, s2=s2, sq=sq: e.tensor_tensor(out=Pc[sq].t[:], in0=hs[s2].t[:], in1=yb.t[:, c, :], op=ALU.mult),
                     reads=[hs[s2].b, ybB[c]], writes=[Pc[sq].b])
                P.op("pool", lambda e, c=c, s2=s2, sq=sq: e.tensor_tensor(out=Qc[sq].t[:], in0=cA[s2].t[:], in1=yb.t[:, c, :], op=ALU.mult),
                     reads=[cA[s2].b, ybB[c]], writes=[Qc[sq].b])
                P.dma("sp", lambda e, c=c, sq=sq: e.dma_start(out=pqd[0, ti, :, c * T:(c + 1) * T], in_=Pc[sq].t[:]), reads=[Pc[sq].b], writes=[pqB], key=Pc[sq].b)
                P.dma("sp", lambda e, c=c, sq=sq: e.dma_start(out=pqd[1, ti, :, c * T:(c + 1) * T], in_=Qc[sq].t[:]), reads=[Qc[sq].b], writes=[pqB], key=Qc[sq].b)

        ucbB = [Buf("ucb0"), Buf("ucb1")]
        stL(0)
        stA_norm_a(0)
        stA_norm_b(0)
        for it_i in range(1, NT + 3):
            i = it_i - 2
            if 0 <= i + 2 < NT:
                stL(i + 2)
            if 0 <= i < NT:
                conv_act(i)
            if 0 <= i + 1 < NT and i + 1 >= 1:
                stA_norm_a(i + 1)
            if 0 <= i - 1 < NT:
                y_gelu(i - 1)
            if 0 <= i < NT:
                conv_gates(i)
            if 0 <= i + 1 < NT and i + 1 >= 1:
                stA_norm_b(i + 1)
            interleave(exp_sqrt_gen(i) if 0 <= i < NT else empty_gen(),
                       stA_gen(i + 1) if 0 <= i + 1 < NT else empty_gen(), 3)
            if 0 <= i - 1 < NT:
                scans(i - 1)
            if 0 <= i < NT:
                b_mul(i)
        hB = Buf("hend")
        P.dma("sp", lambda e: e.dma_start(out=hend[:, :], in_=hst.t[:]), reads=hstB, writes=[hB], key=hst.b)
        P.emit()


def phase_exchange(nc, hend, hall):
    Prog.uid += 1
    with nc.semaphore("cc_sem%d" % Prog.uid) as cc_sem:
      with nc.Block() as b0:
        @b0.gpsimd
        def _(g):
            g.sem_clear(cc_sem)
      with nc.Block() as block:
        @block.gpsimd
        def _(g):
            g.collective_compute("AllGather", ALU.bypass, replica_groups=[[0, 1], [2, 3], [4, 5], [6, 7]],
                                 ins=[hend.opt()], outs=[hall.opt()]).then_inc(cc_sem)
            g.wait_ge(cc_sem, 1)


def phase_lru_out(nc, x1, pqd, hall, isodd, w_out, g1, xmid, nchunks=None):
    NCH = LW // 128
    T = 256
    with contextlib.ExitStack() as st:
        cx = Ctx(nc, st)
        P = Prog(nc)
        wt, wb = load_weight(P, cx, w_out, NCH, D, "lruwout")
        g1b = cx.sb("g1b", [128, D], F32)
        P.dma("sp", lambda e: e.dma_start(out=g1b.t[:], in_=g1.partition_broadcast(128)), writes=[g1b.b])
        cm05 = cx.sb("cm05", [128, 8], F32)
        P.op("pool", lambda e: e.memset(cm05.t[:], -0.5), writes=[cm05.b])
        h0 = cx.sb("h0", [128, NCH], F32)
        odd = cx.sb("odd", [128, 1], F32)
        P.dma("sp", lambda e: e.dma_start(out=h0.t[:], in_=hall[0:128, :]), writes=[h0.b])
        P.dma("sp", lambda e: e.dma_start(out=odd.t[:], in_=isodd[:, :]), writes=[odd.b])
        P.op("dve", lambda e: e.tensor_scalar(out=h0.t[:], in0=h0.t[:], scalar1=odd.t[:, 0:1], scalar2=None, op0=ALU.mult),
             reads=[h0.b, odd.b], writes=[h0.b])
        NS = 3
        NX = 8
        xt = [cx.sb("xt", [128, D], F32) for _ in range(NX)]
        Pt = [cx.sb("Pt", [128, NCH, T], F32) for _ in range(NS)]
        Qt = [cx.sb("Qt", [128, NCH, T], F32) for _ in range(NS)]
        hy = [cx.sb("hy", [128, NCH, T], BF16) for _ in range(NS)]
        junk = None
        ss = [cx.sb("ss", [128, 4], F32) for _ in range(2)]
        rstd = [cx.sb("rstd", [128, 1], F32) for _ in range(2)]
        tmp = [cx.sb("tmp", [128, D], F32) for _ in range(2)]
        xo = [cx.sb("xo", [128, D], F32) for _ in range(2)]
        p_m = [[cx.ps("p_m", [128, 512], F32) for _ in range(2)] for _ in range(2)]
        xmB = Buf("xmid")
        n_grp = (HALF // T) if nchunks is None else nchunks // 2

        def stL(g):
            s = g % NS
            for ci in range(2):
                c = 2 * g + ci
                x_ = xt[c % NX]
                P.dma("act", lambda e, x_=x_, c=c: e.dma_start(out=x_.t[:], in_=x1[c + 1, :, :]), writes=[x_.b])
            P.dma("act", lambda e: e.dma_start(out=Pt[s].t[:], in_=pqd[0, g].rearrange("p (c t) -> p c t", c=NCH)), writes=[Pt[s].b])
            P.dma("act", lambda e: e.dma_start(out=Qt[s].t[:], in_=pqd[1, g].rearrange("p (c t) -> p c t", c=NCH)), writes=[Qt[s].b])

        def stA(g):
            s = g % NS
            def fix(e):
                for cc in range(NCH):
                    ins = e.scalar_tensor_tensor(out=hy[s].t[:, cc, :], in0=Qt[s].t[:, cc, :], scalar=h0.t[:, cc:cc + 1],
                                                 in1=Pt[s].t[:, cc, :], op0=ALU.mult, op1=ALU.add)
                return ins
            P.op("dve", fix, reads=[Qt[s].b, Pt[s].b, h0.b], writes=[hy[s].b])

        def stB(g):
            s = g % NS
            for ci in range(2):
                c = 2 * g + ci
                pm = p_m[ci]
                so = c % 2
                for n in range(2):
                    def mm(e, n=n, pm=pm, ci=ci):
                        for k in range(NCH):
                            ins = e.matmul(out=pm[n].t[:], lhsT=hy[s].t[:, k, ci * 128:(ci + 1) * 128], rhs=wt[:, k, n * 512:(n + 1) * 512],
                                           start=(k == 0), stop=(k == NCH - 1))
                        return ins
                    P.op("pe", mm, reads=[hy[s].b] + wb, writes=[pm[n].b])
                emit_post_norm_residual(P, pm, xt[c % NX], g1b, xo[so], junk, ss[so], rstd[so], cm05, tmp[so])
                P.dma("sp", lambda e, c=c, so=so: e.dma_start(out=xmid[c + 1, :, :], in_=xo[so].t[:]), reads=[xo[so].b], writes=[xmB], key=xo[so].b)

        run_pipeline(n_grp, [stL, stA, stB])
        P.emit()


def build_program(mode="fused"):
    nc = bass.Bass("TRN2", target_bir_lowering=False)
    dt = lambda name, shape, d=F32, **kw: nc.dram_tensor(name, shape, d, **kw).ap()
    inp = dict(kind="ExternalInput")
    outk = dict(kind="ExternalOutput")
    mid_out = outk if mode == "A" else (inp if mode == "B" else {})
    isodd = dt("isodd", [128, 1], **inp)
    lru_w_out = dt("lru_w_out", [LW, D], **inp)
    ng = dt("norm_g", [8, D], **inp)
    fwi = dt("ffn_w_in", [2, D, 2 * DFF], **inp)
    fwo = dt("ffn_w_out", [2, DFF, D], **inp)
    x1 = dt("x1", [NEXT, 128, D], **mid_out)
    pqd = dt("pqd", [2, HALF // 256, 128, (LW // 128) * 256], **mid_out)
    hall = dt("hall", [256, LW // 128], **({} if mode != "B" else inp))
    xmid = dt("xmid", [NEXT, 128, D])
    if mode != "B":
        xf = dt("xf", [SEQ, D], **inp)
        pos = dt("pos", [1, SEQ], **inp)
        invf = dt("invf", [128, 1], **inp)
        ret_w_in = dt("ret_w_in", [D, 2 * QKW + 2 * VW], **inp)
        ret_w_out = dt("ret_w_out", [VW, D], **inp)
        lru_w_in = dt("lru_w_in", [D, 2 * LW], **inp)
        lru_conv_w = dt("lru_conv_w", [4, LW], **inp)
        lru_conv_b = dt("lru_conv_b", [1, LW], **inp)
        lru_gate_w = dt("lru_gate_w", [2, 6, 256, 256], **inp)
        lru_gate_b = dt("lru_gate_b", [2, LW], **inp)
        lru_a = dt("lru_a_param", [1, LW], **inp)
        tabs = dt("tabs", [SEQ, 2, 128])
        ogd = dt("ogd", [NEXT, 128, VW], BF16)
        hend = dt("hend", [128, LW // 128])
        with contextlib.ExitStack() as o0:
            rw = alloc_weight(nc, o0, 8, 2 * QKW + 2 * VW, "retwin")
            phase_tables(nc, pos, invf, tabs,
                         prefetch=lambda P: issue_weight_load(P, rw, ret_w_in, 8, 2 * QKW + 2 * VW, "retwin"))
            phase_ret(nc, xf, tabs, ret_w_in, ng[0:1, :], ogd, pre_w=rw)
        with contextlib.ExitStack() as o1:
            fw0 = alloc_weight(nc, o1, 8, 2 * DFF, "ffnwin")
            phase_ret_out(nc, xf, ogd, ret_w_out, ng[1:2, :], xmid,
                          prefetch=lambda P: issue_weight_load(P, fw0, fwi[0], 8, 2 * DFF, "ffnwin"))
            phase_ffn(nc, xmid, fwi[0], fwo[0], ng[2:3, :], ng[3:4, :], lambda c: x1[c, :, :], 0, NEXT, pre_wi=fw0)
        phase_lru(nc, x1, lru_w_in, lru_conv_w, lru_conv_b, lru_gate_w, lru_gate_b, lru_a, ng[4:5, :], pqd, hend)
        phase_exchange(nc, hend, hall)
        if mode == "A":
            hallo = dt("hallo", [256, LW // 128], **outk)
            Prog.uid += 1
            with nc.semaphore("cps%d" % Prog.uid) as s:
                with nc.Block() as b0:
                    @b0.gpsimd
                    def _(g):
                        g.sem_clear(s)
                with nc.Block() as blk:
                    @blk.sync
                    def _(e):
                        e.dma_start(out=hallo[:, :], in_=hall[:, :]).then_inc(s, 16)
                        e.wait_ge(s, 16)
            return nc
    out = dt("out", [HALF // C, C, D], **outk)
    phase_lru_out(nc, x1, pqd, hall, isodd, lru_w_out, ng[5:6, :], xmid)
    phase_ffn(nc, xmid, fwi[1], fwo[1], ng[6:7, :], ng[7:8, :], lambda c: out[c - 1, :, :], 1, NEXT)
    return nc


_INV = None


def make_in_maps(x, ret_w_in, ret_w_out, lru_w_in, lru_conv_w, lru_conv_b, lru_gate_w, lru_gate_b,
                 lru_a_param, lru_w_out, norm_g, ffn_w_in, ffn_w_out):
    f = lambda a: np.ascontiguousarray(np.asarray(a, dtype=np.float32))
    inv = (1.0 / (np.float32(10000.0) ** (np.arange(128, dtype=np.float32) / np.float32(128)))).astype(np.float32)
    x = f(x)
    shared = {
        "invf": inv[:, None].copy(),
        "ret_w_in": f(ret_w_in[0]), "ret_w_out": f(ret_w_out[0]), "lru_w_in": f(lru_w_in[0]),
        "lru_conv_w": f(lru_conv_w[0]), "lru_conv_b": f(lru_conv_b), "lru_gate_w": f(lru_gate_w[0]),
        "lru_gate_b": f(np.asarray(lru_gate_b)[0].reshape(2, LW)), "lru_a_param": f(lru_a_param),
        "lru_w_out": f(lru_w_out[0]), "norm_g": f(np.asarray(norm_g).reshape(8, D)),
        "ffn_w_in": f(ffn_w_in), "ffn_w_out": f(ffn_w_out),
    }
    maps = []
    for c in range(NCORES):
        b, half = c // 2, c % 2
        if half:
            xfull = x[b]
            p = np.arange(SEQ)
        else:
            xfull = np.concatenate([np.zeros((HALF, D), np.float32), x[b, :HALF]], 0)
            p = (np.arange(SEQ) + HALF) % SEQ
        m = dict(shared)
        m["xf"] = np.ascontiguousarray(xfull)
        m["pos"] = p[None, :].astype(np.float32)
        m["isodd"] = np.full((128, 1), float(half), np.float32)
        maps.append(m)
    return maps


FUSED = True


def kernel(**inputs):
    maps = make_in_maps(**inputs)
    cores = list(range(NCORES))
    if FUSED:
        res = run_bass_kernel_spmd(build_program("fused"), maps, core_ids=cores)
        outs = [r["out"] for r in res.results]
    else:
        keysB = ("isodd", "lru_w_out", "norm_g", "ffn_w_in", "ffn_w_out")
        resA = run_bass_kernel_spmd(build_program("A"), maps, core_ids=cores)
        mapsB = []
        for c in range(NCORES):
            m = {k: maps[c][k] for k in keysB}
            m["x1"] = np.asarray(resA.results[c]["x1"])
            m["pqd"] = np.asarray(resA.results[c]["pqd"])
            m["hall"] = np.asarray(resA.results[c]["hallo"])
            mapsB.append(m)
        del resA
        resB = run_bass_kernel_spmd(build_program("B"), mapsB, core_ids=cores)
        outs = [r["out"] for r in resB.results]
    B = NCORES // 2
    out = np.empty((B, SEQ, D), np.float32)
    for c in range(NCORES):
        b, half = c // 2, c % 2
        out[b, half * HALF:(half + 1) * HALF] = np.asarray(outs[c]).reshape(HALF, D)
    return out
```

```python
import contextlib
import math
import numpy as np
import concourse.bass as bass
import concourse.mybir as mybir
from concourse.bass_utils import run_bass_kernel_spmd

F32 = mybir.dt.float32
BF16 = mybir.dt.bfloat16
I32 = mybir.dt.int32
AF = mybir.ActivationFunctionType
ALU = mybir.AluOpType

ENGS = ("pe", "act", "dve", "pool", "sp")

NCORES = 8
D = 1024
C = 128
SEQ = 8192
HALF = 4096
NLOC = 64
HALO = 31
NEXT = 33
H = 4
DK = 256
DV = 512
QKW = 1024
VW = 2048
DFF = 2816
LW = 1536
EPS = 1e-6
LG = [math.log1p(-2.0 ** (-5 - h)) for h in range(H)]


class Buf:
    __slots__ = ("name", "last_w", "readers", "sem", "sem_cnt")

    def __init__(self, name=""):
        self.name = name
        self.last_w = None
        self.readers = []
        self.sem = None
        self.sem_cnt = 0

    def reset(self):
        self.last_w = None
        self.readers = []
        self.sem = None
        self.sem_cnt = 0


class Op:
    __slots__ = ("eng", "fn", "deps", "is_dma", "needs_inc", "count", "dsem", "dval", "oid")

    def __init__(self, eng, fn, is_dma):
        self.eng = eng
        self.fn = fn
        self.deps = set()
        self.is_dma = is_dma
        self.needs_inc = False
        self.count = None
        self.dsem = None
        self.dval = None


class Prog:
    uid = 0

    def __init__(self, nc):
        self.nc = nc
        self.ops = []
        self.n_dma_sems = 0
        self.seen = {}
        self.sw_sems = set()

    def _add(self, op, reads, writes):
        op.oid = len(self.ops)
        for b in reads:
            if b.last_w is not None:
                op.deps.add(b.last_w)
        for b in writes:
            cands = list(b.readers)
            if b.last_w is not None:
                cands.append(b.last_w)
            for r in cands:
                p = self.ops[r]
                if op.is_dma or p.is_dma or p.eng != op.eng:
                    op.deps.add(r)
        op.deps.discard(op.oid)
        for b in reads:
            self.seen[id(b)] = b
        for b in writes:
            self.seen[id(b)] = b
        for b in reads:
            b.readers.append(op.oid)
        for b in writes:
            b.last_w = op.oid
            b.readers = []
        self.ops.append(op)
        return op

    def op(self, eng, fn, reads=(), writes=()):
        return self._add(Op(eng, fn, False), reads, writes)

    def dma(self, eng, fn, reads=(), writes=(), key=None):
        op = Op(eng, fn, True)
        kb = key if key is not None else (writes[0] if writes else reads[0])
        if kb.sem is None:
            kb.sem = self.n_dma_sems
            self.n_dma_sems += 1
            self.seen[id(kb)] = kb
        if eng == "pool":
            self.sw_sems.add(kb.sem)
        kb.sem_cnt += 16
        op.dsem = kb.sem
        op.dval = kb.sem_cnt
        return self._add(op, reads, writes)

    def emit(self):
        nc = self.nc
        ops = self.ops
        for o in ops:
            for d in o.deps:
                ops[d].needs_inc = True
        cnt = {e: 0 for e in ENGS}
        for o in ops:
            if not o.is_dma and o.needs_inc:
                cnt[o.eng] += 1
                o.count = cnt[o.eng]
        by_eng = {e: [o for o in ops if o.eng == e] for e in ENGS}
        with contextlib.ExitStack() as st:
            Prog.uid += 1
            u = Prog.uid
            esem = {e: st.enter_context(nc.semaphore("se%d_%s" % (u, e))) for e in ENGS}
            NSW = 4
            assert len(self.sw_sems) <= NSW
            swpool = [st.enter_context(nc.semaphore("sw%d_%d" % (u, i))) for i in range(NSW)]
            dsem = [None] * self.n_dma_sems
            for j, i in enumerate(sorted(self.sw_sems)):
                dsem[i] = swpool[j]
            for i in range(self.n_dma_sems):
                if dsem[i] is None:
                    dsem[i] = st.enter_context(nc.semaphore("sd%d_%d" % (u, i)))
            with nc.Block() as b0:
                @b0.gpsimd
                def _(g):
                    for sm in list(esem.values()) + swpool + [d for d in dsem if d not in swpool]:
                        g.sem_clear(sm)
            block = st.enter_context(nc.Block())

            def body(ename, eng):
                known = {}
                for o in by_eng[ename]:
                    need = {}
                    for d in o.deps:
                        p = ops[d]
                        if p.is_dma:
                            k = ("d", p.dsem)
                            v = p.dval
                        else:
                            k = ("e", p.eng)
                            v = p.count
                        if need.get(k, 0) < v:
                            need[k] = v
                    for k, v in need.items():
                        if known.get(k, 0) >= v:
                            continue
                        known[k] = v
                        s = dsem[k[1]] if k[0] == "d" else esem[k[1]]
                        eng.wait_ge(s, v)
                    ins = o.fn(eng)
                    if o.is_dma:
                        ins.then_inc(dsem[o.dsem], 16)
                    elif o.needs_inc:
                        ins.then_inc(esem[ename], 1)
                last = {}
                for o in by_eng[ename]:
                    if o.is_dma:
                        last[o.dsem] = max(last.get(o.dsem, 0), o.dval)
                for s, v in last.items():
                    if known.get(("d", s), 0) < v:
                        eng.wait_ge(dsem[s], v)

            @block.tensor
            def _(e):
                body("pe", e)

            @block.scalar
            def _(e):
                body("act", e)

            @block.vector
            def _(e):
                body("dve", e)

            @block.gpsimd
            def _(e):
                body("pool", e)

            @block.sync
            def _(e):
                body("sp", e)
        for b in self.seen.values():
            b.reset()


class Tl:
    __slots__ = ("t", "b")

    def __init__(self, t, name):
        self.t = t
        self.b = Buf(name)


class Ctx:
    gn = 0

    def __init__(self, nc, st):
        self.nc = nc
        self.st = st
        self.n = 0

    def sb(self, name, shape, dt):
        Ctx.gn += 1
        nm = "%s_%d" % (name, Ctx.gn)
        return Tl(self.st.enter_context(self.nc.sbuf_tensor(nm, shape, dt)), nm)

    def ps(self, name, shape, dt):
        Ctx.gn += 1
        nm = "%s_%d" % (name, Ctx.gn)
        return Tl(self.st.enter_context(self.nc.psum_tensor(nm, shape, dt)), nm)


def run_pipeline(n, stages, lo=0):
    ns = len(stages)
    for it in range(lo, lo + n + ns - 1):
        for k in reversed(range(ns)):
            c = it - k
            if lo <= c < lo + n:
                stages[k](c)


def alloc_weight(nc, st, kchunks, ncols, name):
    Ctx.gn += 1
    return st.enter_context(nc.sbuf_tensor("%s_%d" % (name, Ctx.gn), [128, kchunks, ncols], BF16))


def issue_weight_load(P, wt, w_ap, kchunks, ncols, name, col0=0):
    bufs = [Buf("%s_k%d" % (name, k)) for k in range(kchunks)]
    keyb = Buf(name + "_sem")
    for k in range(kchunks):
        P.dma("pool", (lambda e, k=k: e.dma_start(out=wt[:, k, :],
                                                  in_=w_ap[k * 128:(k + 1) * 128, col0:col0 + ncols])),
              writes=[bufs[k]], key=keyb)
    return bufs


def load_weight(P, cx, w_ap, kchunks, ncols, name, col0=0):
    wt = alloc_weight(cx.nc, cx.st, kchunks, ncols, name)
    return wt, issue_weight_load(P, wt, w_ap, kchunks, ncols, name, col0)


def make_ident(P, cx):
    nc = cx.nc
    identf = cx.sb("identf", [128, 128], F32)
    ident = cx.sb("ident", [128, 128], BF16)

    P.op("pool", lambda e: e.memset(identf.t[:], 0.0), writes=[identf.b])
    P.op("pool", lambda e: e.affine_select(out=identf.t[:], in_=identf.t[:], pattern=[[-1, 128]],
                                           compare_op=ALU.not_equal, fill=1.0, base=0, channel_multiplier=1),
         reads=[identf.b], writes=[identf.b])
    P.op("pool", lambda e: e.tensor_copy(out=ident.t[:], in_=identf.t[:]), reads=[identf.b], writes=[ident.b])
    return ident


def emit_rstd(P, ss, rstd, cnst, width):
    P.op("dve", lambda e: e.tensor_scalar(out=rstd.t[:], in0=ss.t[:], scalar1=1.0 / width, scalar2=EPS,
                                          op0=ALU.mult, op1=ALU.add), reads=[ss.b], writes=[rstd.b])
    n = rstd.t.shape[1]
    P.op("pool", lambda e: e.tensor_tensor(out=rstd.t[:], in0=rstd.t[:], in1=cnst.t[:, 0:n], op=ALU.pow),
         reads=[rstd.b, cnst.b], writes=[rstd.b])


TWO_PI = 2.0 * math.pi
CW1 = 6.28125
CW2 = 0.0019350051879882812
CW3 = TWO_PI - CW1 - CW2


def phase_tables(nc, pos, invf, tabs, prefetch=None):
    TW = 512
    with contextlib.ExitStack() as st:
        cx = Ctx(nc, st)
        P = Prog(nc)
        posb = cx.sb("posb", [128, SEQ], F32)
        inv = cx.sb("inv", [128, 1], F32)
        P.dma("sp", lambda e: e.dma_start(out=posb.t[:], in_=pos.partition_broadcast(128)), writes=[posb.b])
        P.dma("sp", lambda e: e.dma_start(out=inv.t[:], in_=invf[:, :]), writes=[inv.b])
        if prefetch is not None:
            prefetch(P)
        NS = 4
        ang = [cx.sb("ang", [128, TW], F32) for _ in range(NS)]
        ki = [cx.sb("ki", [128, TW], I32) for _ in range(NS)]
        kf = [cx.sb("kf", [128, TW], F32) for _ in range(NS)]
        rr = [cx.sb("rr", [128, TW], F32) for _ in range(NS)]
        rs = [cx.sb("rs", [128, TW], F32) for _ in range(NS)]
        rc = [cx.sb("rc", [128, TW], F32) for _ in range(NS)]
        cs = [cx.sb("cs", [128, 2, TW], F32) for _ in range(NS)]
        ct = [cx.sb("ct", [128, 2, 128], F32) for _ in range(4)]
        pT = [cx.ps("pT", [128, 2, 128], F32) for _ in range(4)]
        identf = cx.sb("identf", [128, 128], F32)
        P.op("pool", lambda e: e.memset(identf.t[:], 0.0), writes=[identf.b])
        P.op("pool", lambda e: e.affine_select(out=identf.t[:], in_=identf.t[:], pattern=[[-1, 128]],
                                               compare_op=ALU.not_equal, fill=1.0, base=0, channel_multiplier=1),
             reads=[identf.b], writes=[identf.b])
        tabB = Buf("tabs")
        def blk(i):
            s = i % NS
            sl = slice(i * TW, (i + 1) * TW)
            P.op("dve", lambda e, s=s, sl=sl: e.tensor_scalar(out=ang[s].t[:], in0=posb.t[:, sl], scalar1=inv.t[:, 0:1],
                                                              scalar2=None, op0=ALU.mult),
                 reads=[posb.b, inv.b], writes=[ang[s].b])
            yield
            P.op("dve", lambda e, s=s: e.tensor_scalar(out=ki[s].t[:], in0=ang[s].t[:], scalar1=1.0 / TWO_PI,
                                                       scalar2=None, op0=ALU.mult),
                 reads=[ang[s].b], writes=[ki[s].b])
            yield
            P.op("dve", lambda e, s=s: e.tensor_copy(out=kf[s].t[:], in_=ki[s].t[:]), reads=[ki[s].b], writes=[kf[s].b])
            yield
            P.op("dve", lambda e, s=s: e.scalar_tensor_tensor(out=rr[s].t[:], in0=kf[s].t[:], scalar=-CW1, in1=ang[s].t[:], op0=ALU.mult, op1=ALU.add),
                 reads=[ang[s].b, kf[s].b], writes=[rr[s].b])
            yield
            P.op("dve", lambda e, s=s: e.scalar_tensor_tensor(out=rs[s].t[:], in0=kf[s].t[:], scalar=-CW2, in1=rr[s].t[:], op0=ALU.mult, op1=ALU.add),
                 reads=[rr[s].b, kf[s].b], writes=[rs[s].b])
            yield
            P.op("dve", lambda e, s=s: e.scalar_tensor_tensor(out=rr[s].t[:], in0=kf[s].t[:], scalar=-CW3, in1=rs[s].t[:], op0=ALU.mult, op1=ALU.add),
                 reads=[rs[s].b, kf[s].b], writes=[rr[s].b])
            yield
            P.op("dve", lambda e, s=s: e.tensor_scalar(out=ang[s].t[:], in0=rr[s].t[:], scalar1=math.pi, scalar2=-TWO_PI,
                                                       op0=ALU.is_gt, op1=ALU.mult), reads=[rr[s].b], writes=[ang[s].b])
            yield
            P.op("dve", lambda e, s=s: e.tensor_tensor(out=rs[s].t[:], in0=rr[s].t[:], in1=ang[s].t[:], op=ALU.add),
                 reads=[rr[s].b, ang[s].b], writes=[rs[s].b])
            yield
            P.op("dve", lambda e, s=s: e.tensor_scalar(out=ang[s].t[:], in0=rr[s].t[:], scalar1=math.pi / 2, scalar2=-TWO_PI,
                                                       op0=ALU.is_gt, op1=ALU.mult), reads=[rr[s].b, rs[s].b], writes=[ang[s].b])
            yield
            P.op("dve", lambda e, s=s: e.scalar_tensor_tensor(out=rc[s].t[:], in0=rr[s].t[:], scalar=math.pi / 2, in1=ang[s].t[:],
                                                              op0=ALU.add, op1=ALU.add),
                 reads=[rr[s].b, ang[s].b], writes=[rc[s].b])
            yield
            P.op("act", lambda e, s=s: e.activation(out=cs[s].t[:, 0, :], in_=rc[s].t[:], func=AF.Sin), reads=[rc[s].b], writes=[cs[s].b])
            yield
            P.op("act", lambda e, s=s: e.activation(out=cs[s].t[:, 1, :], in_=rs[s].t[:], func=AF.Sin), reads=[rs[s].b], writes=[cs[s].b])
            yield
            for tbk in range(TW // 128):
                pt_ = pT[(i * (TW // 128) + tbk) % 4]
                ct_ = ct[(i * (TW // 128) + tbk) % 4]

                def trc(e, s=s, tbk=tbk, pt_=pt_):
                    e.transpose(out=pt_.t[:, 0, :], in_=cs[s].t[:, 0, tbk * 128:(tbk + 1) * 128], identity=identf.t[:])
                    return e.transpose(out=pt_.t[:, 1, :], in_=cs[s].t[:, 1, tbk * 128:(tbk + 1) * 128], identity=identf.t[:])
                P.op("pe", trc, reads=[cs[s].b, identf.b], writes=[pt_.b])
                P.op("act", lambda e, pt_=pt_, ct_=ct_: e.copy(out=ct_.t[:], in_=pt_.t[:]), reads=[pt_.b], writes=[ct_.b])
                t0 = i * TW + tbk * 128
                P.dma("sp", lambda e, t0=t0, ct_=ct_: e.dma_start(out=tabs[t0:t0 + 128, :, :], in_=ct_.t[:]), reads=[ct_.b], writes=[tabB], key=ct_.b)
                yield

        nblk = SEQ // TW
        for g0 in range(0, nblk, NS):
            gens = [blk(i) for i in range(g0, min(g0 + NS, nblk))]
            alive = list(gens)
            while alive:
                for g_ in list(alive):
                    try:
                        next(g_)
                    except StopIteration:
                        alive.remove(g_)
        P.emit()


def phase_ret(nc, xf, tabs, w_in, g0, ogd, c_begin=0, c_end=NLOC, dbg=None, pre_w=None):
    with contextlib.ExitStack() as st:
        cx = Ctx(nc, st)
        P = Prog(nc)
        if pre_w is not None:
            wt, wb = pre_w, []
        else:
            wt, wb = load_weight(P, cx, w_in, 8, 2 * QKW + 2 * VW, "retwin")
        ident = make_ident(P, cx)
        g0b = cx.sb("g0b", [128, D], F32)
        P.dma("sp", lambda e: e.dma_start(out=g0b.t[:], in_=g0.partition_broadcast(128)), writes=[g0b.b])
        cm05 = cx.sb("cm05", [128, 8], F32)
        P.op("pool", lambda e: e.memset(cm05.t[:], -0.5), writes=[cm05.b])
        dif_i = cx.sb("dif_i", [128, 128], I32)
        dif = cx.sb("dif", [128, 128], F32)
        dpos = cx.sb("dpos", [128, 128], F32)
        dge = cx.sb("dge", [128, 128], F32)
        maskT = cx.sb("maskT", [128, H, 128], F32)
        qdec = cx.sb("qdec", [128, H, 128], F32)
        kdec = cx.sb("kdec", [128, H], F32)
        qi_i = cx.sb("qi_i", [128, 128], I32)
        qi = cx.sb("qi", [128, 128], F32)
        kk_i = cx.sb("kk_i", [128, 1], I32)
        kk = cx.sb("kk", [128, 1], F32)
        P.op("pool", lambda e: e.iota(dif_i.t[:], pattern=[[1, 128]], base=0, channel_multiplier=-1), writes=[dif_i.b])
        P.op("pool", lambda e: e.tensor_copy(out=dif.t[:], in_=dif_i.t[:]), reads=[dif_i.b], writes=[dif.b])
        P.op("pool", lambda e: e.iota(qi_i.t[:], pattern=[[1, 128]], base=1, channel_multiplier=0), writes=[qi_i.b])
        P.op("pool", lambda e: e.tensor_copy(out=qi.t[:], in_=qi_i.t[:]), reads=[qi_i.b], writes=[qi.b])
        P.op("pool", lambda e: e.iota(kk_i.t[:], pattern=[[0, 1]], base=127, channel_multiplier=-1), writes=[kk_i.b])
        P.op("pool", lambda e: e.tensor_copy(out=kk.t[:], in_=kk_i.t[:]), reads=[kk_i.b], writes=[kk.b])
        P.op("dve", lambda e: e.tensor_scalar(out=dpos.t[:], in0=dif.t[:], scalar1=0.0, scalar2=None, op0=ALU.max),
             reads=[dif.b], writes=[dpos.b])
        P.op("dve", lambda e: e.tensor_scalar(out=dge.t[:], in0=dif.t[:], scalar1=0.0, scalar2=1.0 / 16.0, op0=ALU.is_ge, op1=ALU.mult),
             reads=[dif.b], writes=[dge.b])
        for h in range(H):
            P.op("act", lambda e, h=h: e.activation(out=maskT.t[:, h, :], in_=dpos.t[:], func=AF.Exp, scale=LG[h]),
                 reads=[dpos.b], writes=[maskT.b])
            P.op("act", lambda e, h=h: e.activation(out=qdec.t[:, h, :], in_=qi.t[:], func=AF.Exp, scale=LG[h]),
                 reads=[qi.b], writes=[qdec.b])
            P.op("act", lambda e, h=h: e.activation(out=kdec.t[:, h:h + 1], in_=kk.t[:], func=AF.Exp, scale=LG[h]),
                 reads=[kk.b], writes=[kdec.b])
        P.op("dve", lambda e: e.tensor_tensor(out=maskT.t[:], in0=maskT.t[:], in1=dge.t[:].unsqueeze(1).broadcast_to([128, H, 128]), op=ALU.mult),
             reads=[maskT.b, dge.b], writes=[maskT.b])
        P.op("dve", lambda e: e.tensor_scalar(out=kdec.t[:], in0=kdec.t[:], scalar1=1.0 / 16.0, scalar2=None, op0=ALU.mult),
             reads=[kdec.b], writes=[kdec.b])
        S = cx.sb("S", [128, 2 * H, DV], F32)
        Sb = cx.sb("Sb", [128, 2 * H, DV], BF16)
        Sbufs = [Buf("S%d" % i) for i in range(2 * H)]
        P.op("pool", lambda e: e.memset(S.t[:], 0.0), writes=Sbufs)
        P.op("pool", lambda e: e.memset(Sb.t[:], 0.0), writes=[Sb.b])
        NS = 2
        NTB = 4
        xt = [cx.sb("xt", [128, D], F32) for _ in range(NS)]
        tb = [cx.sb("tb", [128, 2, 128], F32) for _ in range(NTB)]
        junk = None
        ss = [cx.sb("ss", [128, 1], F32) for _ in range(2)]
        rstd = [cx.sb("rstd", [128, 1], F32) for _ in range(2)]
        hb = cx.sb("hb", [128, D], BF16)
        hT = [cx.sb("hT", [128, 8, 128], BF16) for _ in range(2)]
        t1 = cx.sb("t1", [128, 2, 128], F32)
        t2 = cx.sb("t2", [128, 2, 128], F32)
        t3 = cx.sb("t3", [128, 2, 128], F32)
        t4 = cx.sb("t4", [128, 2, 128], F32)
        ktm = [cx.sb("ktm", [128, H, 2, 128], BF16) for _ in range(1)]
        qtm = cx.sb("qtm", [128, H, 2, 128], BF16)
        qT = [cx.sb("qT", [128, H, 2, 128], BF16) for _ in range(2)]
        kT = [cx.sb("kT", [128, H, 2, 128], BF16) for _ in range(2)]
        qd = [cx.sb("qd", [128, H, 2, 128], BF16) for _ in range(2)]
        kd = [cx.sb("kd", [128, H, 2 * 128], BF16) for _ in range(2)]
        vb = [cx.sb("vb", [128, VW], BF16) for _ in range(2)]
        sg = cx.sb("sg", [128, VW], F32)
        sT = cx.sb("sT", [128, H, 128], BF16)
        stats = cx.sb("stats", [128, H, 6], F32)
        mv = cx.sb("mv", [128, H, 2], F32)
        var = cx.sb("var", [128, H], F32)
        rs = cx.sb("rs", [128, H], F32)
        nb = cx.sb("nb", [128, H], F32)
        on = [cx.sb("on", [128, DV], F32) for _ in range(2)]
        og = [cx.sb("og", [128, VW], BF16) for _ in range(NS)]
        p_tr = cx.ps("p_tr", [128, 8, 128], BF16)
        p_qa = cx.ps("p_qa", [128, H, 128], F32)
        p_qb = cx.ps("p_qb", [128, H, 128], F32)
        p_ka = cx.ps("p_ka", [128, H, 128], F32)
        p_kb = cx.ps("p_kb", [128, H, 128], F32)
        p_w = [cx.ps("p_w", [128, 512], F32) for _ in range(2)]
        p_sc = cx.ps("p_sc", [128, H, 128], F32)
        p_o = [p_qa, p_qb, p_ka, p_kb]
        ogB = Buf("ogd")
        cosb = lambda tbt: tbt.t[:, 0, :].unsqueeze(1).broadcast_to([128, H, 128])
        sinb = lambda tbt: tbt.t[:, 1, :].unsqueeze(1).broadcast_to([128, H, 128])

        def stL(c):
            tok = slice(c * C, (c + 1) * C)
            x_ = xt[c % NS]
            tb_ = tb[c % NTB]
            P.dma("act", lambda e: e.dma_start(out=x_.t[:], in_=xf[tok, :]), writes=[x_.b])
            P.dma("act", lambda e: e.dma_start(out=tb_.t[:], in_=tabs[tok, :, :]), writes=[tb_.b])

        def stA(c):
            x_ = xt[c % NS]
            s2 = c % 2
            P.op("act", lambda e: e.activation(out=hb.t[:], in_=x_.t[:], func=AF.Square, accum_out=ss[s2].t[:]),
                 reads=[x_.b], writes=[hb.b, ss[s2].b])
            emit_rstd(P, ss[s2], rstd[s2], cm05, D)
            P.op("dve", lambda e: e.scalar_tensor_tensor(out=hb.t[:], in0=x_.t[:], scalar=rstd[s2].t[:, 0:1], in1=g0b.t[:],
                                                         op0=ALU.mult, op1=ALU.mult),
                 reads=[x_.b, rstd[s2].b, g0b.b], writes=[hb.b])

        def stB_g(c, full, hT_):
            if full:
                for n in range(4):
                    pw = p_w[n % 2]

                    def mm_g(e, n=n, pw=pw):
                        for k in range(8):
                            ins = e.matmul(out=pw.t[:], lhsT=hT_.t[:, k, :],
                                           rhs=wt[:, k, 2 * QKW + VW + n * 512:2 * QKW + VW + (n + 1) * 512],
                                           start=(k == 0), stop=(k == 7))
                        return ins
                    P.op("pe", mm_g, reads=[hT_.b] + wb, writes=[pw.b])
                    P.op("act", lambda e, n=n, pw=pw: e.activation(out=sg.t[:, n * 512:(n + 1) * 512], in_=pw.t[:], func=AF.Silu),
                         reads=[pw.b], writes=[sg.b])


        def stB_t(c, full, ktm_, qtm_, kT_, qT_, qd_):
            if full:
                def tr_q(e):
                    for h in range(H):
                        for half in range(2):
                            ins = e.transpose(out=p_tr.t[:, h * 2 + half, :], in_=qtm_.t[:, h, half, :], identity=ident.t[:])
                    return ins
                P.op("pe", tr_q, reads=[qtm_.b, ident.b], writes=[p_tr.b])
                P.op("act", lambda e: e.copy(out=qT_.t[:].rearrange("p h a t -> p (h a) t"), in_=p_tr.t[:]), reads=[p_tr.b], writes=[qT_.b])
                P.op("pool", lambda e: e.tensor_tensor(out=qd_.t[:], in0=qT_.t[:],
                                                       in1=qdec.t[:].unsqueeze(2).broadcast_to([128, H, 2, 128]), op=ALU.mult),
                     reads=[qT_.b, qdec.b], writes=[qd_.b])

                def tr_k(e):
                    for h in range(H):
                        for half in range(2):
                            ins = e.transpose(out=p_tr.t[:, h * 2 + half, :], in_=ktm_.t[:, h, half, :], identity=ident.t[:])
                    return ins
                P.op("pe", tr_k, reads=[ktm_.b, ident.b], writes=[p_tr.b])
                P.op("dve", lambda e: e.tensor_copy(out=kT_.t[:].rearrange("p h a t -> p (h a) t"), in_=p_tr.t[:]), reads=[p_tr.b], writes=[kT_.b])


        def stA2(c):
            s2 = c % 2

            def tr_h(e):
                for k in range(8):
                    ins = e.transpose(out=p_tr.t[:, k, :], in_=hb.t[:, k * 128:(k + 1) * 128], identity=ident.t[:])
                return ins
            P.op("pe", tr_h, reads=[hb.b, ident.b], writes=[p_tr.b])
            P.op("act", lambda e: e.copy(out=hT[s2].t[:], in_=p_tr.t[:]), reads=[p_tr.b], writes=[hT[s2].b])

        def stB(c, part):
            full = c >= HALO
            s2 = c % 2
            tb_ = tb[c % NTB]
            hT_ = hT[s2]
            kT_, qT_, qd_, kd_, vb_ = kT[s2], qT[s2], qd[s2], kd[s2], vb[s2]

            cos2 = tb_.t[:, 0, :].unsqueeze(1).broadcast_to([128, 2, 128])
            sin2 = tb_.t[:, 1, :].unsqueeze(1).broadcast_to([128, 2, 128])

            def proj_tm(n, pp):
                def f(e):
                    for k in range(8):
                        ins = e.matmul(out=pp.t[:].rearrange("p a b -> p (a b)"), lhsT=hT_.t[:, k, :], rhs=wt[:, k, n * 512:(n + 1) * 512],
                                       start=(k == 0), stop=(k == 7))
                    return ins
                return f

            def rope_tm(pp, dst, hh0):
                pv = pp.t[:].rearrange("p a b -> p (a b)").rearrange("p (h a j) -> p h a j", h=2, a=2)
                A = pv[:, :, 0, :]
                B = pv[:, :, 1, :]
                P.op("dve", lambda e: e.tensor_tensor(out=t1.t[:, 0:2, :], in0=A, in1=cos2, op=ALU.mult), reads=[pp.b, tb_.b], writes=[t1.b])
                P.op("dve", lambda e: e.tensor_tensor(out=t2.t[:, 0:2, :], in0=B, in1=sin2, op=ALU.mult), reads=[pp.b, tb_.b], writes=[t2.b])
                P.op("pool", lambda e: e.tensor_tensor(out=dst.t[:, hh0:hh0 + 2, 0, :], in0=t1.t[:, 0:2, :], in1=t2.t[:, 0:2, :], op=ALU.subtract),
                     reads=[t1.b, t2.b], writes=[dst.b])
                P.op("dve", lambda e: e.tensor_tensor(out=t3.t[:, 0:2, :], in0=A, in1=sin2, op=ALU.mult), reads=[pp.b, tb_.b], writes=[t3.b])
                P.op("dve", lambda e: e.tensor_tensor(out=t4.t[:, 0:2, :], in0=B, in1=cos2, op=ALU.mult), reads=[pp.b, tb_.b], writes=[t4.b])
                P.op("pool", lambda e: e.tensor_tensor(out=dst.t[:, hh0:hh0 + 2, 1, :], in0=t3.t[:, 0:2, :], in1=t4.t[:, 0:2, :], op=ALU.add),
                     reads=[t3.b, t4.b], writes=[dst.b])

            ktm_, qtm_ = ktm[0], qtm
            if part == "g":
                stB_g(c, full, hT_)
                return
            if part == "t":
                stB_t(c, full, ktm_, qtm_, kT_, qT_, qd_)
                return
            P.op("pe", proj_tm(2, p_ka), reads=[hT_.b] + wb, writes=[p_ka.b])
            P.op("pe", proj_tm(3, p_kb), reads=[hT_.b] + wb, writes=[p_kb.b])
            if full:
                P.op("pe", proj_tm(0, p_qa), reads=[hT_.b] + wb, writes=[p_qa.b])
                P.op("pe", proj_tm(1, p_qb), reads=[hT_.b] + wb, writes=[p_qb.b])
            rope_tm(p_ka, ktm_, 0)
            rope_tm(p_kb, ktm_, 2)
            P.op("dve", lambda e: e.tensor_tensor(out=kd_.t[:], in0=ktm_.t[:].rearrange("p h a j -> p h (a j)"),
                                                  in1=kdec.t[:].unsqueeze(2).broadcast_to([128, H, 256]), op=ALU.mult),
                 reads=[ktm_.b, kdec.b], writes=[kd_.b])
            if full:
                rope_tm(p_qa, qtm_, 0)
                rope_tm(p_qb, qtm_, 2)
            for n in range(4):
                pw = p_w[n % 2]

                def mm_v(e, n=n, pw=pw):
                    for k in range(8):
                        ins = e.matmul(out=pw.t[:], lhsT=hT_.t[:, k, :], rhs=wt[:, k, 2 * QKW + n * 512:2 * QKW + (n + 1) * 512],
                                       start=(k == 0), stop=(k == 7))
                    return ins
                P.op("pe", mm_v, reads=[hT_.b] + wb, writes=[pw.b])
                P.op("act", lambda e, n=n, pw=pw: e.copy(out=vb_.t[:, n * 512:(n + 1) * 512], in_=pw.t[:]), reads=[pw.b], writes=[vb_.b])
        def stC(c):
            full = c >= HALO
            s2 = c % 2
            s = c % NS
            kT_, qT_, qd_, kd_, vb_ = kT[s2], qT[s2], qd[s2], kd[s2], vb[s2]
            if full:
                def mm_sc(e):
                    for h in range(H):
                        for half in range(2):
                            ins = e.matmul(out=p_sc.t[:, h, :], lhsT=kT_.t[:, h, half, :], rhs=qT_.t[:, h, half, :],
                                           start=(half == 0), stop=(half == 1))
                    return ins
                P.op("pe", mm_sc, reads=[kT_.b, qT_.b], writes=[p_sc.b])
                P.op("dve", lambda e: e.tensor_tensor(out=sT.t[:], in0=p_sc.t[:], in1=maskT.t[:], op=ALU.mult),
                     reads=[p_sc.b, maskT.b], writes=[sT.b])
                for h in range(H):
                    def mm_o(e, h=h):
                        e.matmul(out=p_o[h].t[:].rearrange("p a b -> p (a b)"), lhsT=sT.t[:, h, :], rhs=vb_.t[:, h * DV:(h + 1) * DV],
                                 start=True, stop=False)
                        e.matmul(out=p_o[h].t[:].rearrange("p a b -> p (a b)"), lhsT=qd_.t[:, h, 0, :], rhs=Sb.t[:, 2 * h, :],
                                 start=False, stop=False)
                        return e.matmul(out=p_o[h].t[:].rearrange("p a b -> p (a b)"), lhsT=qd_.t[:, h, 1, :], rhs=Sb.t[:, 2 * h + 1, :],
                                        start=False, stop=True)
                    P.op("pe", mm_o, reads=[sT.b, vb_.b, qd_.b, Sb.b], writes=[p_o[h].b])
                    P.op("dve", lambda e, h=h: e.bn_stats(out=stats.t[:, h, :], in_=p_o[h].t[:].rearrange("p a b -> p (a b)")),
                         reads=[p_o[h].b], writes=[stats.b])
            if full:
                P.op("dve", lambda e: [e.bn_aggr(out=mv.t[:, h, :], in_=stats.t[:, h, :]) for h in range(H)][-1],
                     reads=[stats.b], writes=[mv.b])
                P.op("dve", lambda e: e.tensor_scalar(out=var.t[:], in0=mv.t[:, :, 1], scalar1=EPS, scalar2=None, op0=ALU.add),
                     reads=[mv.b], writes=[var.b])
                P.op("pool", lambda e: e.tensor_tensor(out=rs.t[:], in0=var.t[:], in1=cm05.t[:, 0:H], op=ALU.pow),
                     reads=[var.b, cm05.b], writes=[rs.b])
                P.op("dve", lambda e: e.scalar_tensor_tensor(out=nb.t[:], in0=mv.t[:, :, 0], scalar=-1.0, in1=rs.t[:],
                                                             op0=ALU.mult, op1=ALU.mult),
                     reads=[mv.b, rs.b], writes=[nb.b])
                for h in range(H):
                    o_n = on[h % 2]
                    P.op("act", lambda e, h=h, o_n=o_n: e.activation(out=o_n.t[:], in_=p_o[h].t[:].rearrange("p a b -> p (a b)"),
                                                                     func=AF.Identity, scale=rs.t[:, h:h + 1], bias=nb.t[:, h:h + 1]),
                         reads=[p_o[h].b, rs.b, nb.b], writes=[o_n.b])
                    P.op("pool", lambda e, h=h, o_n=o_n: e.tensor_tensor(out=og[s].t[:, h * DV:(h + 1) * DV], in0=o_n.t[:],
                                                                         in1=sg.t[:, h * DV:(h + 1) * DV], op=ALU.mult),
                         reads=[o_n.b, sg.b], writes=[og[s].b])
                P.dma("sp", lambda e: e.dma_start(out=ogd[c - HALO, :, :], in_=og[s].t[:]), reads=[og[s].b], writes=[ogB], key=og[s].b)

        def stC2(c):
            s2 = c % 2
            kd_, vb_ = kd[s2], vb[s2]
            for h in range(H):
                for half in range(2):
                    i = 2 * h + half
                    pw = p_w[i % 2]

                    def mm_s(e, h=h, half=half, pw=pw):
                        return e.matmul(out=pw.t[:], lhsT=kd_.t[:, h, half * 128:(half + 1) * 128], rhs=vb_.t[:, h * DV:(h + 1) * DV],
                                        start=True, stop=True)
                    P.op("pe", mm_s, reads=[kd_.b, vb_.b], writes=[pw.b])
                    P.op("dve", lambda e, i=i, h=h, pw=pw: e.scalar_tensor_tensor(out=S.t[:, i, :], in0=S.t[:, i, :],
                                                                                  scalar=math.exp(C * LG[h]), in1=pw.t[:],
                                                                                  op0=ALU.mult, op1=ALU.add),
                         reads=[Sbufs[i], pw.b], writes=[Sbufs[i]])
            P.op("act", lambda e: e.copy(out=Sb.t[:], in_=S.t[:]), reads=Sbufs, writes=[Sb.b])

        n_ = c_end - c_begin
        lo_ = c_begin
        ok_ = lambda c: lo_ <= c < lo_ + n_
        for it in range(lo_, lo_ + n_ + 3):
            if ok_(it - 1):
                stA(it - 1)
            if ok_(it - 2):
                stB(it - 2, "kqv")
            if ok_(it - 1):
                stA2(it - 1)
            if ok_(it - 3):
                stC(it - 3)
            if ok_(it - 2):
                stB(it - 2, "t")
            if ok_(it - 3):
                stC2(it - 3)
            if ok_(it - 2):
                stB(it - 2, "g")
            if ok_(it):
                stL(it)
        qT, kT = qT[(c_end - 1) % 2], kT[(c_end - 1) % 2]
        if dbg is not None:
            dB = Buf("dbg")
            P.dma("sp", lambda e: e.dma_start(out=dbg["qT"][:, :], in_=qT.t[:].rearrange("p h a t -> p (h a t)")), reads=[qT.b], writes=[dB], key=qT.b)
            P.dma("sp", lambda e: e.dma_start(out=dbg["kT"][:, :], in_=kT.t[:].rearrange("p h a t -> p (h a t)")), reads=[kT.b], writes=[dB], key=kT.b)
            P.dma("sp", lambda e: e.dma_start(out=dbg["S"][:, :], in_=S.t[:].rearrange("p i e -> p (i e)")), reads=Sbufs, writes=[dB], key=S.b)
        P.emit()


def emit_post_norm_residual(P, pm, xres, gb, outt, junk, ss, rstd, cm05, tmp):
    def sq(e):
        e.activation(out=tmp.t[:, 0:512], in_=pm[0].t[:], func=AF.Square, accum_out=ss.t[:, 0:1])
        return e.activation(out=tmp.t[:, 512:1024], in_=pm[1].t[:], func=AF.Square, accum_out=ss.t[:, 1:2])
    P.op("act", sq, reads=[pm[0].b, pm[1].b], writes=[tmp.b, ss.b])
    P.op("dve", lambda e: e.tensor_tensor(out=ss.t[:, 2:3], in0=ss.t[:, 0:1], in1=ss.t[:, 1:2], op=ALU.add),
         reads=[ss.b], writes=[ss.b])
    P.op("dve", lambda e: e.tensor_scalar(out=rstd.t[:], in0=ss.t[:, 2:3], scalar1=1.0 / D, scalar2=EPS,
                                          op0=ALU.mult, op1=ALU.add), reads=[ss.b], writes=[rstd.b])
    P.op("pool", lambda e: e.tensor_tensor(out=rstd.t[:], in0=rstd.t[:], in1=cm05.t[:, 0:1], op=ALU.pow),
         reads=[rstd.b, cm05.b], writes=[rstd.b])
    for n in range(2):
        P.op("dve", lambda e, n=n: e.scalar_tensor_tensor(out=tmp.t[:, n * 512:(n + 1) * 512], in0=pm[n].t[:], scalar=rstd.t[:, 0:1],
                                                          in1=gb.t[:, n * 512:(n + 1) * 512], op0=ALU.mult, op1=ALU.mult),
             reads=[pm[n].b, rstd.b, gb.b], writes=[tmp.b])
    P.op("pool", lambda e: e.tensor_tensor(out=outt.t[:], in0=tmp.t[:], in1=xres.t[:], op=ALU.add),
         reads=[tmp.b, xres.b], writes=[outt.b])


def phase_ret_out(nc, xf, ogd, w_out, g1, xmid, prefetch=None):
    with contextlib.ExitStack() as st:
        cx = Ctx(nc, st)
        P = Prog(nc)
        wt, wb = load_weight(P, cx, w_out, 16, D, "retwout")
        if prefetch is not None:
            prefetch(P)
        ident = make_ident(P, cx)
        g1b = cx.sb("g1b", [128, D], F32)
        P.dma("sp", lambda e: e.dma_start(out=g1b.t[:], in_=g1.partition_broadcast(128)), writes=[g1b.b])
        cm05 = cx.sb("cm05", [128, 8], F32)
        P.op("pool", lambda e: e.memset(cm05.t[:], -0.5), writes=[cm05.b])
        NX = 5
        xt = [cx.sb("xt", [128, D], F32) for _ in range(NX)]
        ogt = [cx.sb("ogt", [128, VW], BF16) for _ in range(4)]
        ogT = [cx.sb("ogT", [128, 16, 128], BF16) for _ in range(3)]
        junk = None
        ss = [cx.sb("ss", [128, 4], F32) for _ in range(2)]
        rstd = [cx.sb("rstd", [128, 1], F32) for _ in range(2)]
        tmp = [cx.sb("tmp", [128, D], F32) for _ in range(2)]
        xo = [cx.sb("xo", [128, D], F32) for _ in range(2)]
        p_tr = [cx.ps("p_tr", [128, 8, 128], BF16) for _ in range(2)]
        p_m = [[cx.ps("p_m", [128, 512], F32) for _ in range(2)] for _ in range(2)]
        xmB = Buf("xmid")

        def stL(c):
            tok = slice((HALO + c) * C, (HALO + c + 1) * C)
            x_ = xt[c % NX]
            o_ = ogt[c % 4]
            P.dma("act", lambda e: e.dma_start(out=x_.t[:], in_=xf[tok, :]), writes=[x_.b])
            P.dma("act", lambda e: e.dma_start(out=o_.t[:], in_=ogd[c, :, :]), writes=[o_.b])

        def stA(c):
            o_ = ogt[c % 4]
            oT = ogT[c % 3]
            for half in range(2):
                def tr(e, half=half):
                    for k in range(8):
                        kk = half * 8 + k
                        ins = e.transpose(out=p_tr[half].t[:, k, :], in_=o_.t[:, kk * 128:(kk + 1) * 128], identity=ident.t[:])
                    return ins
                P.op("pe", tr, reads=[o_.b, ident.b], writes=[p_tr[half].b])
                if half == 0:
                    P.op("act", lambda e: e.copy(out=oT.t[:, 0:8, :], in_=p_tr[0].t[:]), reads=[p_tr[0].b], writes=[oT.b])
                else:
                    P.op("dve", lambda e: e.tensor_copy(out=oT.t[:, 8:16, :], in_=p_tr[1].t[:]), reads=[p_tr[1].b], writes=[oT.b])

        def stB(c):
            s = c % 2
            oT = ogT[c % 3]
            pm = p_m[s]
            for n in range(2):
                def mm(e, n=n):
                    for k in range(16):
                        ins = e.matmul(out=pm[n].t[:], lhsT=oT.t[:, k, :], rhs=wt[:, k, n * 512:(n + 1) * 512],
                                       start=(k == 0), stop=(k == 15))
                    return ins
                P.op("pe", mm, reads=[oT.b] + wb, writes=[pm[n].b])
            emit_post_norm_residual(P, pm, xt[c % NX], g1b, xo[s], junk, ss[s], rstd[s], cm05, tmp[s])
            P.dma("sp", lambda e: e.dma_start(out=xmid[c, :, :], in_=xo[s].t[:]), reads=[xo[s].b], writes=[xmB], key=xo[s].b)

        run_pipeline(NEXT, [stL, stA, (lambda c: None), stB])
        P.emit()


def phase_ffn(nc, xin, w_in, w_out, g2, g3, xout_fn, c0, c1, pre_wi=None):
    with contextlib.ExitStack() as st:
        cx = Ctx(nc, st)
        P = Prog(nc)
        if pre_wi is not None:
            wi, wib = pre_wi, []
        else:
            wi, wib = load_weight(P, cx, w_in, 8, 2 * DFF, "ffnwin")
        wo, wob = load_weight(P, cx, w_out, 22, D, "ffnwout")
        ident = make_ident(P, cx)
        g2b = cx.sb("g2b", [128, D], F32)
        g3b = cx.sb("g3b", [128, D], F32)
        P.dma("sp", lambda e: e.dma_start(out=g2b.t[:], in_=g2.partition_broadcast(128)), writes=[g2b.b])
        P.dma("sp", lambda e: e.dma_start(out=g3b.t[:], in_=g3.partition_broadcast(128)), writes=[g3b.b])
        cm05 = cx.sb("cm05", [128, 8], F32)
        P.op("pool", lambda e: e.memset(cm05.t[:], -0.5), writes=[cm05.b])
        NX = 5
        xt = [cx.sb("xt", [128, D], F32) for _ in range(NX)]
        junk = None
        ss = cx.sb("ss", [128, 4], F32)
        ss1 = [cx.sb("ss1", [128, 1], F32) for _ in range(2)]
        rstd = cx.sb("rstd", [128, 1], F32)
        rstd2 = [cx.sb("rstd2", [128, 1], F32) for _ in range(2)]
        hb = [cx.sb("hb", [128, D], BF16) for _ in range(2)]
        hT = [cx.sb("hT", [128, 8, 128], BF16) for _ in range(2)]
        sgl = [cx.sb("sgl", [128, 512], F32) for _ in range(2)]
        ab = [cx.sb("ab", [128, DFF], BF16) for _ in range(2)]
        aT = cx.sb("aT", [128, 22, 128], BF16)
        tmp = cx.sb("tmp", [128, D], F32)
        xo = [cx.sb("xo", [128, D], F32) for _ in range(2)]
        p_tr = [cx.ps("p_tr", [128, 8, 128], BF16) for _ in range(2)]
        p_g = [cx.ps("p_g", [128, 512], F32) for _ in range(2)]
        p_u = [cx.ps("p_u", [128, 512], F32) for _ in range(2)]
        p_f = [cx.ps("p_f", [128, 512], F32) for _ in range(2)]
        xoB = Buf("xout")
        tiles = [(j * 512, 512) for j in range(5)] + [(2560, 256)]

        def stL(c):
            x_ = xt[c % NX]
            P.dma("act", lambda e: e.dma_start(out=x_.t[:], in_=xin[c, :, :]), writes=[x_.b])

        def stA(c):
            x_ = xt[c % NX]
            s = c % 2
            P.op("act", lambda e: e.activation(out=hb[s].t[:], in_=x_.t[:], func=AF.Square, accum_out=ss1[s].t[:]),
                 reads=[x_.b], writes=[hb[s].b, ss1[s].b])
            emit_rstd(P, ss1[s], rstd2[s], cm05, D)
            P.op("dve", lambda e: e.scalar_tensor_tensor(out=hb[s].t[:], in0=x_.t[:], scalar=rstd2[s].t[:, 0:1], in1=g2b.t[:],
                                                         op0=ALU.mult, op1=ALU.mult),
                 reads=[x_.b, rstd2[s].b, g2b.b], writes=[hb[s].b])

        def stA2(c):
            s = c % 2

            def tr_h(e):
                for k in range(8):
                    ins = e.transpose(out=p_tr[0].t[:, k, :], in_=hb[s].t[:, k * 128:(k + 1) * 128], identity=ident.t[:])
                return ins
            P.op("pe", tr_h, reads=[hb[s].b, ident.b], writes=[p_tr[0].b])
            P.op("act", lambda e: e.copy(out=hT[s].t[:], in_=p_tr[0].t[:]), reads=[p_tr[0].b], writes=[hT[s].b])

        def stB(c):
            s = c % 2
            for j, (c0h, wd) in enumerate(tiles):
                pg = p_g[j % 2]
                pu = p_u[j % 2]
                sl_ = sgl[j % 2]

                def mm_gu(e, c0h=c0h, wd=wd, pg=pg, pu=pu):
                    for k in range(8):
                        e.matmul(out=pg.t[:, 0:wd], lhsT=hT[s].t[:, k, :], rhs=wi[:, k, c0h:c0h + wd], start=(k == 0), stop=(k == 7))
                    for k in range(8):
                        ins = e.matmul(out=pu.t[:, 0:wd], lhsT=hT[s].t[:, k, :], rhs=wi[:, k, DFF + c0h:DFF + c0h + wd],
                                       start=(k == 0), stop=(k == 7))
                    return ins
                P.op("pe", mm_gu, reads=[hT[s].b] + wib, writes=[pg.b, pu.b])
                P.op("act", lambda e, wd=wd, pg=pg, sl_=sl_: e.activation(out=sl_.t[:, 0:wd], in_=pg.t[:, 0:wd], func=AF.Silu),
                     reads=[pg.b], writes=[sl_.b])
                P.op("dve", lambda e, c0h=c0h, wd=wd, pu=pu, sl_=sl_: e.tensor_tensor(out=ab[s].t[:, c0h:c0h + wd], in0=pu.t[:, 0:wd],
                                                                                      in1=sl_.t[:, 0:wd], op=ALU.mult),
                     reads=[pu.b, sl_.b], writes=[ab[s].b])

        def stC(c):
            s = c % 2
            for gi, (k0, nk) in enumerate(((0, 8), (8, 8), (16, 6))):
                pt = p_tr[(gi + 1) % 2]

                def tr_a(e, k0=k0, nk=nk, pt=pt):
                    for k in range(nk):
                        ins = e.transpose(out=pt.t[:, k, :], in_=ab[s].t[:, (k0 + k) * 128:(k0 + k + 1) * 128], identity=ident.t[:])
                    return ins
                P.op("pe", tr_a, reads=[ab[s].b, ident.b], writes=[pt.b])
                if gi % 2 == 0:
                    P.op("act", lambda e, k0=k0, nk=nk, pt=pt: e.copy(out=aT.t[:, k0:k0 + nk, :], in_=pt.t[:, 0:nk, :]),
                         reads=[pt.b], writes=[aT.b])
                else:
                    P.op("dve", lambda e, k0=k0, nk=nk, pt=pt: e.tensor_copy(out=aT.t[:, k0:k0 + nk, :], in_=pt.t[:, 0:nk, :]),
                         reads=[pt.b], writes=[aT.b])

        def stC2(c):
            s = c % 2
            for n in range(2):
                def mm_f(e, n=n):
                    for k in range(22):
                        ins = e.matmul(out=p_f[n].t[:], lhsT=aT.t[:, k, :], rhs=wo[:, k, n * 512:(n + 1) * 512],
                                       start=(k == 0), stop=(k == 21))
                    return ins
                P.op("pe", mm_f, reads=[aT.b] + wob, writes=[p_f[n].b])
            emit_post_norm_residual(P, p_f, xt[c % NX], g3b, xo[s], junk, ss, rstd, cm05, tmp)
            P.dma("sp", lambda e: e.dma_start(out=xout_fn(c), in_=xo[s].t[:]), reads=[xo[s].b], writes=[xoB], key=xo[s].b)

        n_ = c1 - c0
        for it in range(c0, c0 + n_ + 3):
            if c0 <= it - 3 < c0 + n_:
                stC(it - 3)
            if c0 <= it - 1 < c0 + n_:
                stA(it - 1)
            if c0 <= it - 2 < c0 + n_:
                stB(it - 2)
            if c0 <= it - 1 < c0 + n_:
                stA2(it - 1)
            if c0 <= it - 3 < c0 + n_:
                stC2(it - 3)
            if c0 <= it < c0 + n_:
                stL(it)
        P.emit()


def phase_lru(nc, x1, w_in, conv_w, conv_b, gate_w, gate_b, a_param, g0, pqd, hend, dbg=None, ntiles=None):
    T = 256
    NT = HALF // T
    if ntiles is not None:
        NT = ntiles
    NCH = LW // 128
    with contextlib.ExitStack() as st:
        cx = Ctx(nc, st)
        P = Prog(nc)
        st.enter_context(nc.allow_non_contiguous_dma(reason="tiny per-channel parameter vectors"))
        wt, wb = load_weight(P, cx, w_in, 8, 2 * LW, "lruwin")
        Ctx.gn += 1
        gw = st.enter_context(nc.sbuf_tensor("gw_%d" % Ctx.gn, [128, 24, 256], BF16))
        gwB = Buf("gw")
        P.dma("pool", lambda e: e.dma_start(out=gw[:, :, :], in_=gate_w.rearrange("g n (ki p) j -> p (g n ki) j", p=128)), writes=[gwB])
        ident = make_ident(P, cx)
        g0b = cx.sb("g0b", [128, D], F32)
        P.dma("sp", lambda e: e.dma_start(out=g0b.t[:], in_=g0.partition_broadcast(128)), writes=[g0b.b])
        cm05 = cx.sb("cm05", [128, 8], F32)
        P.op("pool", lambda e: e.memset(cm05.t[:], -0.5), writes=[cm05.b])
        cw = cx.sb("cw", [128, 4, NCH], F32)
        cb = cx.sb("cb", [128, NCH], F32)
        gb = cx.sb("gb", [128, 2, NCH], F32)
        ap_ = cx.sb("ap", [128, NCH], F32)
        cv = cx.sb("cv", [128, NCH], F32)
        cv2 = cx.sb("cv2", [128, NCH], F32)
        for i in range(4):
            P.dma("sp", lambda e, i=i: e.dma_start(out=cw.t[:, i, :], in_=conv_w[i:i + 1, :].rearrange("o (c p) -> p (o c)", p=128)), writes=[cw.b])
        P.dma("sp", lambda e: e.dma_start(out=cb.t[:], in_=conv_b.rearrange("o (c p) -> p (o c)", p=128)), writes=[cb.b])
        P.dma("sp", lambda e: e.dma_start(out=ap_.t[:], in_=a_param.rearrange("o (c p) -> p (o c)", p=128)), writes=[ap_.b])
        for g in range(2):
            P.dma("sp", lambda e, g=g: e.dma_start(out=gb.t[:, g, :], in_=gate_b[g:g + 1, :].rearrange("o (c p) -> p (o c)", p=128)), writes=[gb.b])
        P.op("act", lambda e: e.activation(out=cv.t[:], in_=ap_.t[:], func=AF.Exp, scale=-1.0), reads=[ap_.b], writes=[cv.b])
        P.op("act", lambda e: e.activation(out=cv.t[:], in_=cv.t[:], func=AF.Ln, bias=1.0), reads=[cv.b], writes=[cv.b])
        P.op("dve", lambda e: e.tensor_scalar(out=cv2.t[:], in0=cv.t[:], scalar1=-16.0, scalar2=None, op0=ALU.mult), reads=[cv.b], writes=[cv2.b])
        P.op("dve", lambda e: e.tensor_scalar(out=cv.t[:], in0=cv.t[:], scalar1=-8.0, scalar2=None, op0=ALU.mult), reads=[cv.b, cv2.b], writes=[cv.b])
        hst = cx.sb("hst", [128, NCH], F32)
        Ast = cx.sb("Ast", [128, NCH], F32)
        hstB = [Buf("hst%d" % c) for c in range(NCH)]
        AstB = [Buf("Ast%d" % c) for c in range(NCH)]
        P.op("pool", lambda e: e.memset(hst.t[:], 0.0), writes=hstB)
        P.op("pool", lambda e: e.memset(Ast.t[:], 1.0), writes=AstB)
        NX = 4
        NH = 3
        xt = [cx.sb("xt", [128, D], F32) for _ in range(NX)]
        junk = None
        ss = [cx.sb("ss", [128, 1], F32) for _ in range(2)]
        rstd = [cx.sb("rstd", [128, 1], F32) for _ in range(2)]
        hb2 = [cx.sb("hb", [128, D], BF16) for _ in range(2)]
        hT = [cx.sb("hT", [128, 8, T], BF16) for _ in range(NH)]
        yb = cx.sb("yb", [128, NCH, T], F32)
        ub = cx.sb("ub", [128, NCH, T + 3], F32)
        uc = cx.sb("uc", [128, NCH, T], F32)
        ucb = cx.sb("ucb", [128, NCH, T], BF16)
        rt = cx.sb("rt", [128, NCH, T], F32)
        it2 = [cx.sb("it", [128, NCH, T], F32) for _ in range(2)]
        at2 = [cx.sb("at", [128, NCH, T], F32) for _ in range(2)]
        hs = [cx.sb("hs", [128, T], F32) for _ in range(2)]
        cA = [cx.sb("cA", [128, T], F32) for _ in range(2)]
        NPQ = 2
        Pc = [cx.sb("Pc", [128, T], F32) for _ in range(NPQ)]
        Qc = [cx.sb("Qc", [128, T], F32) for _ in range(NPQ)]
        ybB = [Buf("yb%d" % c) for c in range(NCH)]
        ucB = [Buf("uc%d" % c) for c in range(NCH)]
        rB = [Buf("r%d" % c) for c in range(NCH)]
        iB = [[Buf("i%d_%d" % (p_, c)) for c in range(NCH)] for p_ in range(2)]
        aB = [[Buf("a%d_%d" % (p_, c)) for c in range(NCH)] for p_ in range(2)]
        p_tr2 = [cx.ps("p_tr", [128, 8, 128], BF16) for _ in range(2)]
        NPP = 4

        class PV:
            def __init__(self, tl):
                self.tl, self.b = tl, tl.b

            def sl(self, a, b_):
                return self.tl.t[:, a:b_]
        p_p = [PV(cx.ps("p_p", [128, 512], F32)) for k in range(NPP)]
        p_g = [cx.ps("p_g", [128, 512], F32) for _ in range(2)]
        pqB = Buf("pqd")
        P.op("pool", lambda e: e.memset(ub.t[:], 0.0), writes=[ub.b])
        nrm = [0]

        def load_x(cidx):
            x_ = xt[cidx % NX]
            P.dma("act", lambda e: e.dma_start(out=x_.t[:], in_=x1[cidx, :, :]), writes=[x_.b])

        def norm_a(cidx, slot):
            x_ = xt[cidx % NX]
            hb = hb2[slot]
            p_tr = p_tr2[slot]
            s2 = nrm[0] % 2
            nrm[0] += 1
            P.op("act", lambda e: e.activation(out=hb.t[:], in_=x_.t[:], func=AF.Square, accum_out=ss[s2].t[:]),
                 reads=[x_.b], writes=[hb.b, ss[s2].b])
            emit_rstd(P, ss[s2], rstd[s2], cm05, D)
            P.op("dve", lambda e: e.scalar_tensor_tensor(out=hb.t[:], in0=x_.t[:], scalar=rstd[s2].t[:, 0:1], in1=g0b.t[:],
                                                         op0=ALU.mult, op1=ALU.mult),
                 reads=[x_.b, rstd[s2].b, g0b.b], writes=[hb.b])

            def tr_h(e):
                for k in range(8):
                    ins = e.transpose(out=p_tr.t[:, k, :], in_=hb.t[:, k * 128:(k + 1) * 128], identity=ident.t[:])
                return ins
            P.op("pe", tr_h, reads=[hb.b, ident.b], writes=[p_tr.b])

        def norm_b(slot, hT_, dst_col):
            p_tr = p_tr2[slot]
            P.op("act", lambda e: e.copy(out=hT_.t[:, :, dst_col:dst_col + 128], in_=p_tr.t[:]), reads=[p_tr.b], writes=[hT_.b])

        def norm_T(cidx, hT_, dst_col):
            norm_a(cidx, 0)
            norm_b(0, hT_, dst_col)

        def proj(fc, n, pp, hT_):
            def f(e):
                for k in range(8):
                    ins = e.matmul(out=pp.sl(0, n), lhsT=wt[:, k, fc * 128:(fc + 1) * 128], rhs=hT_.t[:, k, 0:n],
                                   start=(k == 0), stop=(k == 7))
                return ins
            return f

        load_x(0)
        norm_T(0, hT[NH - 1], 0)
        for c in range(NCH):
            pp = p_p[c % NPP]
            P.op("pe", proj(NCH + c, 128, pp, hT[NH - 1]), reads=[hT[NH - 1].b] + wb, writes=[pp.b])
            P.op("act", lambda e, c=c, pp=pp: e.copy(out=ub.t[:, c, 0:3], in_=pp.sl(125, 128)), reads=[pp.b], writes=[ub.b])

        def stL(ti):
            for ci in range(T // 128):
                load_x(1 + ti * (T // 128) + ci)

        def stA_norm_a(ti):
            for ci in range(T // 128):
                norm_a(1 + ti * (T // 128) + ci, ci)

        def stA_norm_b(ti):
            hT_ = hT[ti % NH]
            for ci in range(T // 128):
                norm_b(ci, hT_, ci * 128)

        def stA_gen(ti):
            hT_ = hT[ti % NH]
            for c in range(NCH):
                pp = p_p[c % NPP]
                P.op("pe", proj(NCH + c, T, pp, hT_), reads=[hT_.b] + wb, writes=[pp.b])
                P.op("act", lambda e, c=c, pp=pp: e.copy(out=ub.t[:, c, 3:3 + T], in_=pp.sl(0, T)),
                     reads=[pp.b], writes=[ub.b])
                yield

        GRP = [list(range(0, NCH // 2)), list(range(NCH // 2, NCH))]

        def conv_act(ti):
            for c in range(NCH):
                P.op("act", lambda e, c=c: e.activation(out=uc.t[:, c, :], in_=ub.t[:, c, 3:3 + T], func=AF.Identity,
                                                        scale=cw.t[:, 3, c:c + 1], bias=cb.t[:, c:c + 1]),
                     reads=[ub.b, cw.b, cb.b], writes=[ucB[c]])

        def y_gelu(ti):
            hT_ = hT[ti % NH]
            for c in range(NCH):
                pp = p_p[c % NPP]
                P.op("pe", proj(c, T, pp, hT_), reads=[hT_.b] + wb, writes=[pp.b])
                P.op("act", lambda e, c=c, pp=pp: e.activation(out=yb.t[:, c, :], in_=pp.sl(0, T), func=AF.Gelu_apprx_tanh),
                     reads=[pp.b], writes=[ybB[c]])

        def conv_gates(ti):
            it_, iB_ = it2[ti % 2], iB[ti % 2]
            for gi, grp in enumerate(GRP):
                for i in range(3):
                    for c in grp:
                        P.op("dve", lambda e, c=c, i=i: e.scalar_tensor_tensor(out=uc.t[:, c, :], in0=ub.t[:, c, i:i + T],
                                                                                scalar=cw.t[:, i, c:c + 1], in1=uc.t[:, c, :],
                                                                                op0=ALU.mult, op1=ALU.add),
                             reads=[ub.b, cw.b, ucB[c]], writes=[ucB[c]])
                c0g, c1g = grp[0], grp[-1] + 1
                P.op("dve", lambda e, c0g=c0g, c1g=c1g: e.tensor_copy(out=ucb.t[:, c0g:c1g, :], in_=uc.t[:, c0g:c1g, :]),
                     reads=[ucB[c] for c in grp], writes=[ucbB[gi]])
                if gi == 1:
                    P.op("pool", lambda e: e.tensor_copy(out=ub.t[:, :, 0:3], in_=ub.t[:, :, T:T + 3]), reads=[ub.b], writes=[ub.b])
                for g in range(2):
                    for c in grp:
                        n, jo = c // 2, c % 2
                        pg = p_g[(g * NCH + c) % 2]

                        def mm_g(e, g=g, n=n, jo=jo, pg=pg):
                            for ki in range(2):
                                ins = e.matmul(out=pg.t[:, 0:T], lhsT=gw[:, (g * 6 + n) * 2 + ki, jo * 128:(jo + 1) * 128],
                                               rhs=ucb.t[:, 2 * n + ki, :], start=(ki == 0), stop=(ki == 1))
                            return ins
                        P.op("pe", mm_g, reads=[ucbB[gi], gwB], writes=[pg.b])
                        dst, dB = (rt, rB) if g == 0 else (it_, iB_)
                        P.op("act", lambda e, g=g, c=c, pg=pg, dst=dst: e.activation(out=dst.t[:, c, :], in_=pg.t[:, 0:T], func=AF.Sigmoid,
                                                                                     bias=gb.t[:, g, c:c + 1]),
                             reads=[pg.b, gb.b], writes=[dB[c]])

        def exp_sqrt_gen(ti):
            at_, aB_ = at2[ti % 2], aB[ti % 2]
            for c in range(NCH):
                P.op("act", lambda e, c=c: e.activation(out=at_.t[:, c, :], in_=rt.t[:, c, :], func=AF.Exp, scale=cv.t[:, c:c + 1]),
                     reads=[rB[c], cv.b], writes=[aB_[c]])
                yield
                P.op("act", lambda e, c=c: e.activation(out=rt.t[:, c, :], in_=rt.t[:, c, :], func=AF.Exp, scale=cv2.t[:, c:c + 1]),
                     reads=[rB[c], cv2.b], writes=[rB[c]])
                yield
            for c in range(NCH):
                P.op("act", lambda e, c=c: e.activation(out=rt.t[:, c, :], in_=rt.t[:, c, :], func=AF.Sqrt, scale=-1.0, bias=1.0),
                     reads=[rB[c]], writes=[rB[c]])
                yield

        def interleave(ga, gb, ratio):
            da = db = False
            while not (da and db):
                for _ in range(ratio):
                    if not da:
                        try:
                            next(ga)
                        except StopIteration:
                            da = True
                if not db:
                    try:
                        next(gb)
                    except StopIteration:
                        db = True

        def empty_gen():
            return
            yield

        def b_mul(ti):
            it_, iB_ = it2[ti % 2], iB[ti % 2]
            for c in range(NCH):
                P.op("dve", lambda e, c=c: e.tensor_tensor(out=it_.t[:, c, :], in0=it_.t[:, c, :], in1=uc.t[:, c, :], op=ALU.mult),
                     reads=[iB_[c], ucB[c]], writes=[iB_[c]])
            for c in range(NCH):
                P.op("dve", lambda e, c=c: e.tensor_tensor(out=it_.t[:, c, :], in0=it_.t[:, c, :], in1=rt.t[:, c, :], op=ALU.mult),
                     reads=[iB_[c], rB[c]], writes=[iB_[c]])

        def scans(ti):
            it_, iB_ = it2[ti % 2], iB[ti % 2]
            at_, aB_ = at2[ti % 2], aB[ti % 2]
            for c in range(NCH):
                s2 = c % 2
                sq = c % NPQ
                P.op("dve", lambda e, c=c, s2=s2: e.tensor_tensor_scan(out=hs[s2].t[:], data0=at_.t[:, c, :], data1=it_.t[:, c, :],
                                                                       initial=hst.t[:, c:c + 1], op0=ALU.mult, op1=ALU.add),
                     reads=[aB_[c], iB_[c], hstB[c]], writes=[hs[s2].b])
                P.op("dve", lambda e, c=c, s2=s2: e.tensor_tensor_scan(out=cA[s2].t[:], data0=at_.t[:, c, :], data1=at_.t[:, c, :],
                                                                       initial=Ast.t[:, c:c + 1], op0=ALU.mult, op1=ALU.min),
                     reads=[aB_[c], AstB[c]], writes=[cA[s2].b])
                P.op("pool", lambda e, c=c, s2=s2: e.tensor_copy(out=hst.t[:, c:c + 1], in_=hs[s2].t[:, T - 1:T]), reads=[hs[s2].b], writes=[hstB[c]])
                P.op("pool", lambda e, c=c, s2=s2: e.tensor_copy(out=Ast.t[:, c:c + 1], in_=cA[s2].t[:, T - 1:T]), reads=[cA[s2].b], writes=[AstB[c]])
                P.op("dve", lambda e, c=c, s2=s2, sq=sq: e.tensor_tensor(out=Pc[sq].t[:], in0=hs[s2].t[:], in1=yb.t[:, c, :], op=ALU.mult),
                     reads=[hs[s2].b, ybB[c]], writes=[Pc[sq].b])
                P.op("pool", lambda e, c=c, s2=s2, sq=sq: e.tensor_tensor(out=Qc[sq].t[:], in0=cA[s2].t[:], in1=yb.t[:, c, :], op=ALU.mult),
                     reads=[cA[s2].b, ybB[c]], writes=[Qc[sq].b])
                P.dma("sp", lambda e, c=c, sq=sq: e.dma_start(out=pqd[0, ti, :, c * T:(c + 1) * T], in_=Pc[sq].t[:]), reads=[Pc[sq].b], writes=[pqB], key=Pc[sq].b)
                P.dma("sp", lambda e, c=c, sq=sq: e.dma_start(out=pqd[1, ti, :, c * T:(c + 1) * T], in_=Qc[sq].t[:]), reads=[Qc[sq].b], writes=[pqB], key=Qc[sq].b)

        ucbB = [Buf("ucb0"), Buf("ucb1")]
        stL(0)
        stA_norm_a(0)
        stA_norm_b(0)
        for it_i in range(1, NT + 3):
            i = it_i - 2
            if 0 <= i + 2 < NT:
                stL(i + 2)
            if 0 <= i < NT:
                conv_act(i)
            if 0 <= i + 1 < NT and i + 1 >= 1:
                stA_norm_a(i + 1)
            if 0 <= i - 1 < NT:
                y_gelu(i - 1)
            if 0 <= i < NT:
                conv_gates(i)
            if 0 <= i + 1 < NT and i + 1 >= 1:
                stA_norm_b(i + 1)
            interleave(exp_sqrt_gen(i) if 0 <= i < NT else empty_gen(),
                       stA_gen(i + 1) if 0 <= i + 1 < NT else empty_gen(), 3)
            if 0 <= i - 1 < NT:
                scans(i - 1)
            if 0 <= i < NT:
                b_mul(i)
        hB = Buf("hend")
        P.dma("sp", lambda e: e.dma_start(out=hend[:, :], in_=hst.t[:]), reads=hstB, writes=[hB], key=hst.b)
        P.emit()


def phase_exchange(nc, hend, hall):
    Prog.uid += 1
    with nc.semaphore("cc_sem%d" % Prog.uid) as cc_sem:
      with nc.Block() as b0:
        @b0.gpsimd
        def _(g):
            g.sem_clear(cc_sem)
      with nc.Block() as block:
        @block.gpsimd
        def _(g):
            g.collective_compute("AllGather", ALU.bypass, replica_groups=[[0, 1], [2, 3], [4, 5], [6, 7]],
                                 ins=[hend.opt()], outs=[hall.opt()]).then_inc(cc_sem)
            g.wait_ge(cc_sem, 1)


def phase_lru_out(nc, x1, pqd, hall, isodd, w_out, g1, xmid, nchunks=None):
    NCH = LW // 128
    T = 256
    with contextlib.ExitStack() as st:
        cx = Ctx(nc, st)
        P = Prog(nc)
        wt, wb = load_weight(P, cx, w_out, NCH, D, "lruwout")
        g1b = cx.sb("g1b", [128, D], F32)
        P.dma("sp", lambda e: e.dma_start(out=g1b.t[:], in_=g1.partition_broadcast(128)), writes=[g1b.b])
        cm05 = cx.sb("cm05", [128, 8], F32)
        P.op("pool", lambda e: e.memset(cm05.t[:], -0.5), writes=[cm05.b])
        h0 = cx.sb("h0", [128, NCH], F32)
        odd = cx.sb("odd", [128, 1], F32)
        P.dma("sp", lambda e: e.dma_start(out=h0.t[:], in_=hall[0:128, :]), writes=[h0.b])
        P.dma("sp", lambda e: e.dma_start(out=odd.t[:], in_=isodd[:, :]), writes=[odd.b])
        P.op("dve", lambda e: e.tensor_scalar(out=h0.t[:], in0=h0.t[:], scalar1=odd.t[:, 0:1], scalar2=None, op0=ALU.mult),
             reads=[h0.b, odd.b], writes=[h0.b])
        NS = 3
        NX = 8
        xt = [cx.sb("xt", [128, D], F32) for _ in range(NX)]
        Pt = [cx.sb("Pt", [128, NCH, T], F32) for _ in range(NS)]
        Qt = [cx.sb("Qt", [128, NCH, T], F32) for _ in range(NS)]
        hy = [cx.sb("hy", [128, NCH, T], BF16) for _ in range(NS)]
        junk = None
        ss = [cx.sb("ss", [128, 4], F32) for _ in range(2)]
        rstd = [cx.sb("rstd", [128, 1], F32) for _ in range(2)]
        tmp = [cx.sb("tmp", [128, D], F32) for _ in range(2)]
        xo = [cx.sb("xo", [128, D], F32) for _ in range(2)]
        p_m = [[cx.ps("p_m", [128, 512], F32) for _ in range(2)] for _ in range(2)]
        xmB = Buf("xmid")
        n_grp = (HALF // T) if nchunks is None else nchunks // 2

        def stL(g):
            s = g % NS
            for ci in range(2):
                c = 2 * g + ci
                x_ = xt[c % NX]
                P.dma("act", lambda e, x_=x_, c=c: e.dma_start(out=x_.t[:], in_=x1[c + 1, :, :]), writes=[x_.b])
            P.dma("act", lambda e: e.dma_start(out=Pt[s].t[:], in_=pqd[0, g].rearrange("p (c t) -> p c t", c=NCH)), writes=[Pt[s].b])
            P.dma("act", lambda e: e.dma_start(out=Qt[s].t[:], in_=pqd[1, g].rearrange("p (c t) -> p c t", c=NCH)), writes=[Qt[s].b])

        def stA(g):
            s = g % NS
            def fix(e):
                for cc in range(NCH):
                    ins = e.scalar_tensor_tensor(out=hy[s].t[:, cc, :], in0=Qt[s].t[:, cc, :], scalar=h0.t[:, cc:cc + 1],
                                                 in1=Pt[s].t[:, cc, :], op0=ALU.mult, op1=ALU.add)
                return ins
            P.op("dve", fix, reads=[Qt[s].b, Pt[s].b, h0.b], writes=[hy[s].b])

        def stB(g):
            s = g % NS
            for ci in range(2):
                c = 2 * g + ci
                pm = p_m[ci]
                so = c % 2
                for n in range(2):
                    def mm(e, n=n, pm=pm, ci=ci):
                        for k in range(NCH):
                            ins = e.matmul(out=pm[n].t[:], lhsT=hy[s].t[:, k, ci * 128:(ci + 1) * 128], rhs=wt[:, k, n * 512:(n + 1) * 512],
                                           start=(k == 0), stop=(k == NCH - 1))
                        return ins
                    P.op("pe", mm, reads=[hy[s].b] + wb, writes=[pm[n].b])
                emit_post_norm_residual(P, pm, xt[c % NX], g1b, xo[so], junk, ss[so], rstd[so], cm05, tmp[so])
                P.dma("sp", lambda e, c=c, so=so: e.dma_start(out=xmid[c + 1, :, :], in_=xo[so].t[:]), reads=[xo[so].b], writes=[xmB], key=xo[so].b)

        run_pipeline(n_grp, [stL, stA, stB])
        P.emit()


def build_program(mode="fused"):
    nc = bass.Bass("TRN2", target_bir_lowering=False)
    dt = lambda name, shape, d=F32, **kw: nc.dram_tensor(name, shape, d, **kw).ap()
    inp = dict(kind="ExternalInput")
    outk = dict(kind="ExternalOutput")
    mid_out = outk if mode == "A" else (inp if mode == "B" else {})
    isodd = dt("isodd", [128, 1], **inp)
    lru_w_out = dt("lru_w_out", [LW, D], **inp)
    ng = dt("norm_g", [8, D], **inp)
    fwi = dt("ffn_w_in", [2, D, 2 * DFF], **inp)
    fwo = dt("ffn_w_out", [2, DFF, D], **inp)
    x1 = dt("x1", [NEXT, 128, D], **mid_out)
    pqd = dt("pqd", [2, HALF // 256, 128, (LW // 128) * 256], **mid_out)
    hall = dt("hall", [256, LW // 128], **({} if mode != "B" else inp))
    xmid = dt("xmid", [NEXT, 128, D])
    if mode != "B":
        xf = dt("xf", [SEQ, D], **inp)
        pos = dt("pos", [1, SEQ], **inp)
        invf = dt("invf", [128, 1], **inp)
        ret_w_in = dt("ret_w_in", [D, 2 * QKW + 2 * VW], **inp)
        ret_w_out = dt("ret_w_out", [VW, D], **inp)
        lru_w_in = dt("lru_w_in", [D, 2 * LW], **inp)
        lru_conv_w = dt("lru_conv_w", [4, LW], **inp)
        lru_conv_b = dt("lru_conv_b", [1, LW], **inp)
        lru_gate_w = dt("lru_gate_w", [2, 6, 256, 256], **inp)
        lru_gate_b = dt("lru_gate_b", [2, LW], **inp)
        lru_a = dt("lru_a_param", [1, LW], **inp)
        tabs = dt("tabs", [SEQ, 2, 128])
        ogd = dt("ogd", [NEXT, 128, VW], BF16)
        hend = dt("hend", [128, LW // 128])
        with contextlib.ExitStack() as o0:
            rw = alloc_weight(nc, o0, 8, 2 * QKW + 2 * VW, "retwin")
            phase_tables(nc, pos, invf, tabs,
                         prefetch=lambda P: issue_weight_load(P, rw, ret_w_in, 8, 2 * QKW + 2 * VW, "retwin"))
            phase_ret(nc, xf, tabs, ret_w_in, ng[0:1, :], ogd, pre_w=rw)
        with contextlib.ExitStack() as o1:
            fw0 = alloc_weight(nc, o1, 8, 2 * DFF, "ffnwin")
            phase_ret_out(nc, xf, ogd, ret_w_out, ng[1:2, :], xmid,
                          prefetch=lambda P: issue_weight_load(P, fw0, fwi[0], 8, 2 * DFF, "ffnwin"))
            phase_ffn(nc, xmid, fwi[0], fwo[0], ng[2:3, :], ng[3:4, :], lambda c: x1[c, :, :], 0, NEXT, pre_wi=fw0)
        phase_lru(nc, x1, lru_w_in, lru_conv_w, lru_conv_b, lru_gate_w, lru_gate_b, lru_a, ng[4:5, :], pqd, hend)
        phase_exchange(nc, hend, hall)
        if mode == "A":
            hallo = dt("hallo", [256, LW // 128], **outk)
            Prog.uid += 1
            with nc.semaphore("cps%d" % Prog.uid) as s:
                with nc.Block() as b0:
                    @b0.gpsimd
                    def _(g):
                        g.sem_clear(s)
                with nc.Block() as blk:
                    @blk.sync
                    def _(e):
                        e.dma_start(out=hallo[:, :], in_=hall[:, :]).then_inc(s, 16)
                        e.wait_ge(s, 16)
            return nc
    out = dt("out", [HALF // C, C, D], **outk)
    phase_lru_out(nc, x1, pqd, hall, isodd, lru_w_out, ng[5:6, :], xmid)
    phase_ffn(nc, xmid, fwi[1], fwo[1], ng[6:7, :], ng[7:8, :], lambda c: out[c - 1, :, :], 1, NEXT)
    return nc


_INV = None


def make_in_maps(x, ret_w_in, ret_w_out, lru_w_in, lru_conv_w, lru_conv_b, lru_gate_w, lru_gate_b,
                 lru_a_param, lru_w_out, norm_g, ffn_w_in, ffn_w_out):
    f = lambda a: np.ascontiguousarray(np.asarray(a, dtype=np.float32))
    inv = (1.0 / (np.float32(10000.0) ** (np.arange(128, dtype=np.float32) / np.float32(128)))).astype(np.float32)
    x = f(x)
    shared = {
        "invf": inv[:, None].copy(),
        "ret_w_in": f(ret_w_in[0]), "ret_w_out": f(ret_w_out[0]), "lru_w_in": f(lru_w_in[0]),
        "lru_conv_w": f(lru_conv_w[0]), "lru_conv_b": f(lru_conv_b), "lru_gate_w": f(lru_gate_w[0]),
        "lru_gate_b": f(np.asarray(lru_gate_b)[0].reshape(2, LW)), "lru_a_param": f(lru_a_param),
        "lru_w_out": f(lru_w_out[0]), "norm_g": f(np.asarray(norm_g).reshape(8, D)),
        "ffn_w_in": f(ffn_w_in), "ffn_w_out": f(ffn_w_out),
    }
    maps = []
    for c in range(NCORES):
        b, half = c // 2, c % 2
        if half:
            xfull = x[b]
            p = np.arange(SEQ)
        else:
            xfull = np.concatenate([np.zeros((HALF, D), np.float32), x[b, :HALF]], 0)
            p = (np.arange(SEQ) + HALF) % SEQ
        m = dict(shared)
        m["xf"] = np.ascontiguousarray(xfull)
        m["pos"] = p[None, :].astype(np.float32)
        m["isodd"] = np.full((128, 1), float(half), np.float32)
        maps.append(m)
    return maps


FUSED = True


def kernel(**inputs):
    maps = make_in_maps(**inputs)
    cores = list(range(NCORES))
    if FUSED:
        res = run_bass_kernel_spmd(build_program("fused"), maps, core_ids=cores)
        outs = [r["out"] for r in res.results]
    else:
        keysB = ("isodd", "lru_w_out", "norm_g", "ffn_w_in", "ffn_w_out")
        resA = run_bass_kernel_spmd(build_program("A"), maps, core_ids=cores)
        mapsB = []
        for c in range(NCORES):
            m = {k: maps[c][k] for k in keysB}
            m["x1"] = np.asarray(resA.results[c]["x1"])
            m["pqd"] = np.asarray(resA.results[c]["pqd"])
            m["hall"] = np.asarray(resA.results[c]["hallo"])
            mapsB.append(m)
        del resA
        resB = run_bass_kernel_spmd(build_program("B"), mapsB, core_ids=cores)
        outs = [r["out"] for r in resB.results]
    B = NCORES // 2
    out = np.empty((B, SEQ, D), np.float32)
    for c in range(NCORES):
        b, half = c // 2, c % 2
        out[b, half * HALF:(half + 1) * HALF] = np.asarray(outs[c]).reshape(HALF, D)
    return out
```
